# Optimizing a Trainium2 kernel written in Bass

```python
import math
import jax, jax.numpy as jnp
from jax import lax
import numpy as np

D_MODEL = 1024
BATCH = 16
SEQ = 2048
DEPTH = 2

CHUNK = 64
PAST_CHUNKS = 8
BAND = (PAST_CHUNKS + 1) * CHUNK
PAD = PAST_CHUNKS * CHUNK
N_LAYERS_A = DEPTH // 2
N_LAYERS_B = DEPTH - N_LAYERS_A
HEAD_DIM = 64
A_HEADS = D_MODEL // HEAD_DIM
A_WIDTH = A_HEADS * HEAD_DIM
MAX_REL = 128
B_HEADS = D_MODEL // (2 * HEAD_DIM)
B_WIDTH = B_HEADS * 2 * HEAD_DIM
ROT_DIM = HEAD_DIM // 4
ROPE_THETA = 500000.0
Q_BLOCK = 128
RMS_EPS = 1e-6
NEG_INF = -1e30

kernel_name = "yoco_chunked_relpos_diff_attention_trunk"


def rmsnorm(t, g):
    tf = t.astype(jnp.float32)
    y = tf * lax.rsqrt(jnp.mean(tf * tf, axis=-1, keepdims=True) + RMS_EPS)
    return (y * g.astype(jnp.float32)).astype(t.dtype)


def partial_rope(t, positions):
    half = ROT_DIM // 2
    inv_freq = jnp.power(jnp.float32(ROPE_THETA), -jnp.arange(half, dtype=jnp.float32) * 2.0 / ROT_DIM)
    ang = positions.astype(jnp.float32)[..., None] * inv_freq
    ang = ang.reshape(ang.shape[:2] + (1,) * (t.ndim - 3) + (half,))
    cos = jnp.cos(ang).astype(t.dtype)
    sin = jnp.sin(ang).astype(t.dtype)
    t1 = t[..., :half]
    t2 = t[..., half:ROT_DIM]
    return jnp.concatenate([t1 * cos - t2 * sin, t2 * cos + t1 * sin, t[..., ROT_DIM:]], axis=-1)


def chunk_band_attention(q, k, v, positions, rel_bias):
    B, S, H, dh = q.shape
    n_chunks = S // CHUNK
    scale = 1.0 / math.sqrt(dh)
    qt = q.transpose(0, 2, 1, 3).reshape(B, H, n_chunks, CHUNK, dh).transpose(2, 0, 1, 3, 4)
    kp = jnp.pad(k.transpose(0, 2, 1, 3), ((0, 0), (0, 0), (PAD, 0), (0, 0)))
    vp = jnp.pad(v.transpose(0, 2, 1, 3), ((0, 0), (0, 0), (PAD, 0), (0, 0)))
    posp = jnp.pad(positions, ((0, 0), (PAD, 0)))

    def one_chunk(args):
        qc, c = args
        start = c * CHUNK
        kb = lax.dynamic_slice_in_dim(kp, start, BAND, axis=2)
        vb = lax.dynamic_slice_in_dim(vp, start, BAND, axis=2)
        kpos = lax.dynamic_slice_in_dim(posp, start, BAND, axis=1)
        qpos = lax.dynamic_slice_in_dim(positions, start, CHUNK, axis=1)
        valid = (start + jnp.arange(BAND)) >= PAD
        rel = jnp.clip(qpos[:, :, None] - kpos[:, None, :], -MAX_REL, MAX_REL) + MAX_REL
        bias = jnp.take(rel_bias, rel, axis=1).transpose(1, 0, 2, 3)
        s = jnp.einsum('bhqd,bhkd->bhqk', qc, kb).astype(jnp.float32) * scale
        s = s + bias.astype(jnp.float32)
        s = jnp.where(valid[None, None, None, :], s, NEG_INF)
        p = jax.nn.softmax(s, axis=-1)
        return jnp.einsum('bhqk,bhkd->bhqd', p.astype(vb.dtype), vb)

    out = lax.map(one_chunk, (qt, jnp.arange(n_chunks)))
    return out.transpose(1, 0, 3, 2, 4).reshape(B, S, H * dh)


def diff_attention(q, k, v, lam):
    B, S, H, _, dh = q.shape
    n_blocks = S // Q_BLOCK
    scale = 1.0 / math.sqrt(dh)
    qb = q.transpose(0, 2, 3, 1, 4).reshape(B, H, 2, n_blocks, Q_BLOCK, dh).transpose(3, 0, 1, 2, 4, 5)
    kchunk = jnp.arange(S) // CHUNK

    def one_block(args):
        qblk, bi = args
        s = jnp.einsum('bhmqd,bhmkd->bhmqk', qblk, k).astype(jnp.float32) * scale
        qchunk = (bi * Q_BLOCK + jnp.arange(Q_BLOCK)) // CHUNK
        mask = kchunk[None, :] <= qchunk[:, None]
        s = jnp.where(mask[None, None, None], s, NEG_INF)
        p = jax.nn.softmax(s, axis=-1)
        a = p[:, :, 0] - lam * p[:, :, 1]
        return jnp.einsum('bhqk,bhkd->bhqd', a.astype(v.dtype), v)

    out = lax.map(one_block, (qb, jnp.arange(n_blocks)))
    return out.transpose(1, 0, 3, 2, 4).reshape(B, S, H, 2 * dh)


def setup_inputs(seed: int = 0) -> dict:
    key = jax.random.key(seed)
    ks = jax.random.split(key, 20)
    f32 = jnp.float32
    nrm = lambda k, shp, s: jax.random.normal(k, shp, f32) * s
    x = jax.random.normal(ks[0], (BATCH, SEQ, D_MODEL), f32)
    offset = jax.random.randint(ks[1], (BATCH, 1), 0, 64, dtype=jnp.int32) * CHUNK
    positions = (offset + jnp.arange(SEQ, dtype=jnp.int32)[None, :]).astype(jnp.int32)
    return {
        "x": x,
        "positions": positions,
        "a_norm_pre": 1.0 + nrm(ks[2], (N_LAYERS_A, D_MODEL), 0.01),
        "a_w_in": nrm(ks[3], (N_LAYERS_A, D_MODEL, 4 * A_WIDTH), D_MODEL ** -0.5),
        "a_rel_bias": nrm(ks[4], (N_LAYERS_A, A_HEADS, 2 * MAX_REL + 1), 0.5),
        "a_w_out": nrm(ks[5], (N_LAYERS_A, A_WIDTH, D_MODEL), A_WIDTH ** -0.5),
        "a_norm_post": 1.0 + nrm(ks[6], (N_LAYERS_A, D_MODEL), 0.01),
        "kv_norm": 1.0 + nrm(ks[7], (D_MODEL,), 0.01),
        "kv_w": nrm(ks[8], (D_MODEL, 2 * B_WIDTH), D_MODEL ** -0.5),
        "b_norm_pre": 1.0 + nrm(ks[9], (N_LAYERS_B, D_MODEL), 0.01),
        "b_w_in": nrm(ks[10], (N_LAYERS_B, D_MODEL, 2 * B_WIDTH), D_MODEL ** -0.5),
        "b_lambda_q1": nrm(ks[11], (N_LAYERS_B, HEAD_DIM), 0.1),
        "b_lambda_k1": nrm(ks[12], (N_LAYERS_B, HEAD_DIM), 0.1),
        "b_lambda_q2": nrm(ks[13], (N_LAYERS_B, HEAD_DIM), 0.1),
        "b_lambda_k2": nrm(ks[14], (N_LAYERS_B, HEAD_DIM), 0.1),
        "b_subln": 1.0 + nrm(ks[15], (N_LAYERS_B, 2 * HEAD_DIM), 0.01),
        "b_w_out": nrm(ks[16], (N_LAYERS_B, B_WIDTH, D_MODEL), B_WIDTH ** -0.5),
        "b_norm_post": 1.0 + nrm(ks[17], (N_LAYERS_B, D_MODEL), 0.01),
    }


def reference(x, positions, a_norm_pre, a_w_in, a_rel_bias, a_w_out, a_norm_post,
              kv_norm, kv_w, b_norm_pre, b_w_in, b_lambda_q1, b_lambda_k1,
              b_lambda_q2, b_lambda_k2, b_subln, b_w_out, b_norm_post):
    B, S, _ = x.shape
    h = x
    k_sh = None
    v_sh = None
    for layer in range(DEPTH):
        if layer < N_LAYERS_A:
            i = layer
            u = rmsnorm(h, a_norm_pre[i])
            proj = u @ a_w_in[i]
            q, k, v, g = jnp.split(proj, 4, axis=-1)
            q = q.reshape(B, S, A_HEADS, HEAD_DIM)
            k = k.reshape(B, S, A_HEADS, HEAD_DIM)
            v = v.reshape(B, S, A_HEADS, HEAD_DIM)
            o = chunk_band_attention(q, k, v, positions, a_rel_bias[i])
            y = (o * jax.nn.silu(g)) @ a_w_out[i]
            h = h + rmsnorm(y, a_norm_post[i])
            if layer == N_LAYERS_A - 1:
                kv = rmsnorm(h, kv_norm) @ kv_w
                ks_, vs_ = jnp.split(kv, 2, axis=-1)
                ks_ = partial_rope(ks_.reshape(B, S, B_HEADS, 2, HEAD_DIM), positions)
                k_sh = ks_.transpose(0, 2, 3, 1, 4)
                v_sh = vs_.reshape(B, S, B_HEADS, 2 * HEAD_DIM).transpose(0, 2, 1, 3)
        else:
            j = layer - N_LAYERS_A
            lam_init = 0.8 - 0.6 * math.exp(-0.3 * layer)
            lam = (jnp.exp(jnp.sum(b_lambda_q1[j].astype(jnp.float32) * b_lambda_k1[j].astype(jnp.float32)))
                   - jnp.exp(jnp.sum(b_lambda_q2[j].astype(jnp.float32) * b_lambda_k2[j].astype(jnp.float32)))
                   + lam_init)
            u = rmsnorm(h, b_norm_pre[j])
            proj = u @ b_w_in[j]
            q, g = jnp.split(proj, 2, axis=-1)
            q = partial_rope(q.reshape(B, S, B_HEADS, 2, HEAD_DIM), positions)
            o = diff_attention(q, k_sh, v_sh, lam)
            o = rmsnorm(o, b_subln[j]) * (1.0 - lam_init)
            y = (o.reshape(B, S, B_WIDTH) * jax.nn.silu(g)) @ b_w_out[j]
            h = h + rmsnorm(y, b_norm_post[j])
    return h
```

```python
import math
from contextlib import ExitStack
import numpy as np
import concourse.bass as bass
import concourse.mybir as mybir
from concourse.bass_utils import run_bass_kernel_spmd

F32 = mybir.dt.float32
BF16 = mybir.dt.bfloat16
I32 = mybir.dt.int32
AF = mybir.ActivationFunctionType
ALU = mybir.AluOpType
AX = mybir.AxisListType

NCORES = 8
SEQ = 2048
D = 1024
NSEQ = 2
GT = 512
NG = SEQ // GT
EPS = 1e-6
LAM_INIT = 0.8 - 0.6 * math.exp(-0.3 * 1)
ROPE_THETA = 500000.0

ENGS = ("pe", "act", "dve", "pool", "sp")
DBG_TILES = 4


class Buf:
    __slots__ = ("name", "w", "r", "dsem", "excl")

    def __init__(self, name, excl=False):
        self.name = name
        self.w = None
        self.r = []
        self.dsem = None
        self.excl = excl


class Sched:
    def __init__(self, nc):
        self.nc = nc
        self.ops = {e: [] for e in ENGS}
        self.cnt = {e: 0 for e in ENGS}
        self.seen = {e: {} for e in ENGS}
        self.ndsem = 0
        self.dcnt = {}

    def _waits(self, eng, reads, writes):
        waits = {}

        def need(t):
            if t is None:
                return
            k, n = t
            if eng == "pe" and k == "pe":
                return
            if n > self.seen[eng].get(k, 0) and n > waits.get(k, 0):
                waits[k] = n

        for b in reads:
            need(b.w)
            if b.excl:
                for t in b.r:
                    if t[0] != eng:
                        need(t)
        for b in writes:
            need(b.w)
            for t in b.r:
                need(t)
        for k, n in waits.items():
            self.seen[eng][k] = n
        return list(waits.items())

    def op(self, eng, fn, reads=(), writes=()):
        waits = self._waits(eng, reads, writes)
        self.cnt[eng] += 1
        tick = (eng, self.cnt[eng])
        for b in reads:
            if len(b.r) > 64:
                _prune(b)
            b.r.append(tick)
        for b in writes:
            b.w = tick
            b.r = []
        self.ops[eng].append((fn, waits, tick))
        return tick

    def dma(self, fn, dst, src, queue="sp"):
        waits = self._waits(queue, [src], [dst])
        if dst.dsem is None:
            dst.dsem = "q%d" % self.ndsem
            self.ndsem += 1
            self.dcnt[dst.dsem] = 0
        self.dcnt[dst.dsem] += 16
        tick = (dst.dsem, self.dcnt[dst.dsem])
        if len(src.r) > 64:
            _prune(src)
        src.r.append(tick)
        dst.w = tick
        dst.r = []
        self.ops[queue].append((fn, waits, tick))
        return tick

    def final_wait(self, eng, bufs):
        waits = self._waits(eng, [], bufs)
        self.ops[eng].append((None, waits, None))

    def emit(self, stack):
        nc = self.nc
        sems = {}
        for e in ENGS:
            sems[e] = stack.enter_context(nc.semaphore("s_" + e))
        for k in self.dcnt:
            sems[k] = stack.enter_context(nc.semaphore("s_" + k))
        block = stack.enter_context(nc.Block())

        def run(ename):
            def body(eng):
                for fn, waits, tick in self.ops[ename]:
                    for k, n in waits:
                        eng.wait_ge(sems[k], n)
                    if fn is None:
                        continue
                    ins = fn(eng)
                    k, n = tick
                    ins.then_inc(sems[k], 16 if k.startswith("q") else 1)
            return body

        block.tensor(run("pe"))
        block.scalar(run("act"))
        block.vector(run("dve"))
        block.gpsimd(run("pool"))
        block.sync(run("sp"))


def _prune(b):
    best = {}
    for k, n in b.r:
        if n > best.get(k, 0):
            best[k] = n
    b.r = list(best.items())


class _Stop(Exception):
    pass


def build_program(dbg_stage=None, dbg_groups=None, dbg_dump=()):
    def stage(k):
        if dbg_stage is not None and dbg_stage == k:
            raise _Stop()

    nc = bass.Bass("TRN2", target_bir_lowering=False)

    def din(name, shape, dt=F32):
        return nc.dram_tensor(name, list(shape), dt, kind="ExternalInput")

    x_d = din("x", [NSEQ, SEQ, D])
    pos_d = din("pos", [NSEQ, SEQ], I32)
    a_npre_d = din("a_norm_pre", [D])
    a_win_d = din("a_w_in", [D, 4096])
    a_rb_d = din("a_rel_bias", [16, 257])
    a_wout_d = din("a_w_out", [D, D])
    a_npost_d = din("a_norm_post", [D])
    kvn_d = din("kv_norm", [D])
    kvw_d = din("kv_w", [D, 2048])
    b_npre_d = din("b_norm_pre", [D])
    b_win_d = din("b_w_in", [D, 2048])
    lam_d = din("b_lam", [4, 64])
    subln_d = din("b_subln", [128])
    b_wout_d = din("b_w_out", [D, D])
    b_npost_d = din("b_norm_post", [D])
    cmat_d = din("cmat", [4, 128, 128])
    rtab_d = din("rtab", [128, 4])
    out_d = nc.dram_tensor("out", [NSEQ, SEQ, D], F32, kind="ExternalOutput")

    w1s = nc.dram_tensor("w1s", [D, 4096], BF16)
    w2s = nc.dram_tensor("w2s", [D, D], BF16)
    w3s = nc.dram_tensor("w3s", [D, 2048], BF16)
    w4s = nc.dram_tensor("w4s", [D, 2048], BF16)
    w5s = nc.dram_tensor("w5s", [D, D], BF16)
    bx_s = nc.dram_tensor("bx_s", [16, 512], F32)
    chs = nc.dram_tensor("chs", [16], F32)
    ksh_s = nc.dram_tensor("ksh_s", [NSEQ, 8, 128, SEQ], BF16)
    vsh_s = nc.dram_tensor("vsh_s", [NSEQ, 8, SEQ, 128], BF16)

    x_ap = x_d.ap(); out_ap = out_d.ap(); cmat_ap = cmat_d.ap(); rtab_ap = rtab_d.ap()

    def dap(t, offset, ap):
        return bass.AP(tensor=t, offset=offset, ap=[list(a) for a in ap])

    S = Sched(nc)
    with ExitStack() as st:
        def sb(name, shape, dt):
            return st.enter_context(nc.sbuf_tensor(name, list(shape), dt))

        def ps(name, shape, dt=F32):
            return st.enter_context(nc.psum_tensor(name, list(shape), dt))

        IDb16 = sb("IDb16", [128, 128], BF16); ONESb = sb("ONESb", [128, 128], BF16)
        PERMb = sb("PERMb", [128, 128], BF16); JJ = sb("JJ", [128, 128], F32)
        CST = sb("CST", [128, 128], F32)
        RTAB = sb("RTAB", [128, 4], F32)
        EB = sb("EB", [128, 16, 256], BF16)
        CH = sb("CH", [128, 16], F32)
        RB16 = sb("RB16", [16, 257], F32)
        GPA = sb("GPA", [128, D], F32); GPB = sb("GPB", [128, D], F32)
        GN = sb("GN", [128, 3, 8], F32)
        SUBS = sb("SUBS", [128, 1], F32)
        LAMT = sb("LAMT", [128, 4, 64], F32)
        LAMS = sb("LAMS", [128, 8], F32)
        EPSC = sb("EPSC", [128, 2], F32)
        ZER = sb("ZER", [128, 256], F32)
        ST = sb("ST", [128, 16], F32)
        STN = [sb("STN%d" % i, [128, 4], F32) for i in range(2)]
        STO = [sb("STO%d" % i, [128, 4], F32) for i in range(2)]
        NCH = sb("NCH", [128, 16], F32)
        NW = 3
        WB = [sb("WB%d" % i, [128, 8, 512], BF16) for i in range(NW)]
        NXT = 2
        NCX = 3
        XTL = [sb("XTL%d" % i, [128, D], F32) for i in range(NXT)]
        CXL = [sb("CXL%d" % i, [128, D], F32) for i in range(NCX)]
        YN = [sb("YN%d" % i, [128, D], F32) for i in range(2)]
        UU = [sb("UU%d" % i, [128, D], BF16) for i in range(3)]
        XT = sb("XT", [128, 8, GT], BF16)
        X2T = sb("X2T", [128, 8, GT], BF16)
        QT = sb("QT", [128, 8, GT], BF16)
        SGT = sb("SGT", [128, 8, GT], BF16)
        KT = sb("KT", [128, 8, 1024], BF16)
        VA = sb("VA", [128, 8, 1536], BF16)
        NPT = 4
        PT = [sb("PT%d" % i, [128, GT], BF16) for i in range(NPT)]
        RR = [sb("RR%d" % i, [128, GT], F32) for i in range(2)]
        T0 = sb("T0", [128, GT], F32); T1 = sb("T1", [128, GT], F32)
        TA = [sb("TA%d" % i, [128, GT], F32) for i in range(2)]
        TBB = [sb("TBB%d" % i, [128, GT], F32) for i in range(2)]
        SQ = sb("SQ", [128, GT], BF16); RS = sb("RS", [128, GT], F32)
        KRAW = [sb("KRAW%d" % i, [128, GT], BF16) for i in range(2)]
        COS = sb("COS", [128, GT], F32); SIN = sb("SIN", [128, GT], F32)
        POSI = sb("POSI", [128, GT], I32)
        KB = [sb("KB%d" % i, [128, SEQ], BF16) for i in range(2)]
        VB = [sb("VB%d" % i, [128, 16, 128], BF16) for i in range(2)]
        KST = [sb("KST%d" % i, [128, GT], BF16) for i in range(2)]
        VST = [sb("VST%d" % i, [128, D], BF16) for i in range(2)]

        PB = [ps("PB%d" % i, [128, 512], F32) for i in range(7)]
        TRB = ps("TRB", [128, 1024], BF16)

        b = {}
        def B(name, excl=False):
            b[name] = Buf(name, excl)
            return b[name]
        for nm in ["ID", "ONES", "PERM", "JJ", "CST", "RTAB", "EB", "CH", "GPA", "GPB", "GN", "SUBS", "LAMT",
                   "LAMS", "EPSC", "ZER", "ST", "XT", "X2T", "RB16", "QT", "SGT", "KT", "VA", "T0", "T1", "SQ", "RS",
                   "COS", "SIN", "POSI"]:
            B(nm)
        WBb = [Buf("WB%d" % i) for i in range(NW)]
        STNb = [Buf("STN%d" % i) for i in range(2)]
        STOb = [Buf("STO%d" % i) for i in range(2)]
        TAb = [Buf("TA%d" % i) for i in range(2)]
        TBBb = [Buf("TBB%d" % i) for i in range(2)]
        KRAWb = [Buf("KRAW%d" % i) for i in range(2)]
        B("NCH")
        XTLb = [Buf("XTL%d" % i) for i in range(NXT)]
        CXLb = [Buf("CXL%d" % i) for i in range(NCX)]
        YNb = [Buf("YN%d" % i) for i in range(2)]
        UUb = [Buf("UU%d" % i) for i in range(3)]
        PTb = [Buf("PT%d" % i) for i in range(NPT)]
        RRb = [Buf("RR%d" % i) for i in range(2)]
        KBb = [Buf("KB%d" % i) for i in range(2)]
        VBb = [Buf("VB%d" % i) for i in range(2)]
        KSTb = [Buf("KST%d" % i) for i in range(2)]
        VSTb = [Buf("VST%d" % i) for i in range(2)]
        PBb = [Buf("PB%d" % i, True) for i in range(7)]
        TRBb = Buf("TRB", True)
        d_in = Buf("d_in")
        d_w = {k: Buf("d_" + k) for k in ["w1", "w2", "w3", "w4", "w5", "bx", "chs"]}
        d_ksh = [[Buf("d_ksh%d_%d" % (s, g)) for g in range(NG)] for s in range(NSEQ)]
        d_vsh = [[Buf("d_vsh%d_%d" % (s, g)) for g in range(NG)] for s in range(NSEQ)]
        d_out = [[[Buf("d_out%d_%d_%d" % (s, g, t)) for t in range(4)] for g in range(NG)] for s in range(NSEQ)]

        def act(out, in_, func, reads, writes, **kw):
            S.op("act", lambda e: e.activation(out=out, in_=in_, func=func, **kw), reads, writes)

        def mm(out, lhsT, rhs, start, reads, writes):
            S.op("pe", lambda e: e.matmul(out, lhsT=lhsT, rhs=rhs, start=start, stop=False,
                                          skip_group_check=True), reads, writes)

        def dve_copy(out, in_, reads, writes):
            S.op("dve", lambda e: e.tensor_copy(out=out, in_=in_), reads, writes)

        def dve_tt(out, in0, in1, op, reads, writes, eng="dve"):
            S.op(eng, lambda e: e.tensor_tensor(out=out, in0=in0, in1=in1, op=op), reads, writes)

        def dve_ts(out, in0, s1, s2, op0, op1, reads, writes, eng="dve"):
            if op1 is None:
                S.op(eng, lambda e: e.tensor_scalar(out=out, in0=in0, scalar1=s1, scalar2=None, op0=op0), reads, writes)
            else:
                S.op(eng, lambda e: e.tensor_scalar(out=out, in0=in0, scalar1=s1, scalar2=s2, op0=op0, op1=op1),
                     reads, writes)

        def dma(out, in_, dst, src, queue="sp", slow=False):
            if slow:
                S.dma(lambda e: e.dma_start(out=out, in_=in_, allow_slow_non_contiguous=True), dst, src, queue)
            else:
                S.dma(lambda e: e.dma_start(out=out, in_=in_), dst, src, queue)

        evac_flip = [0]

        def evac(out, in_, reads, writes):
            evac_flip[0] ^= 1
            if evac_flip[0]:
                act(out, in_, AF.Copy, reads, writes)
            else:
                dve_copy(out, in_, reads, writes)

        for i, (t, nm) in enumerate([(IDb16, "ID"), (ONESb, "ONES"), (PERMb, "PERM")]):
            dma(CST[:], cmat_ap[i], b["CST"], d_in)
            dve_copy(t[:], CST[:], [b["CST"]], [b[nm]])
        dma(JJ[:], cmat_ap[3], b["JJ"], d_in)
        dma(RTAB[:], rtab_ap, b["RTAB"], d_in)
        dma(GPA[:], dap(a_npost_d, 0, [[0, 128], [1, D]]), b["GPA"], d_in)
        dma(GPB[:], dap(b_npost_d, 0, [[0, 128], [1, D]]), b["GPB"], d_in)
        for i, t in enumerate([a_npre_d, kvn_d, b_npre_d]):
            for kc in range(8):
                dma(GN[:, i, kc:kc + 1], dap(t, kc * 128, [[1, 128], [1, 1]]), b["GN"], d_in)
        dma(SUBS[:], dap(subln_d, 0, [[1, 128], [1, 1]]), b["SUBS"], d_in)
        dma(LAMT[:], dap(lam_d, 0, [[0, 128], [64, 4], [1, 64]]), b["LAMT"], d_in)
        dma(RB16[:], a_rb_d.ap(), b["RB16"], d_in)
        dma(dap(chs, 0, [[1, 16], [1, 1]]), RB16[:, 256:257], d_w["chs"], b["RB16"])
        dma(CH[:], dap(chs, 0, [[0, 128], [1, 16]]), b["CH"], d_w["chs"])
        dve_ts(NCH[:], CH[:], -1.0, None, ALU.mult, None, [b["CH"]], [b["NCH"]])
        S.op("dve", lambda e: e.memset(EPSC[:, 0:1], EPS), [], [b["EPSC"]])
        S.op("dve", lambda e: e.memset(EPSC[:, 1:2], 0.0), [], [b["EPSC"]])
        S.op("dve", lambda e: e.memset(ZER[:], 0.0), [], [b["ZER"]])
        S.op("pool", lambda e: e.memset(VA[:], 1.0), [], [b["VA"]])
        dve_tt(LAMT[:, 0, :], LAMT[:, 0, :], LAMT[:, 1, :], ALU.mult, [b["LAMT"]], [b["LAMT"]])
        dve_tt(LAMT[:, 2, :], LAMT[:, 2, :], LAMT[:, 3, :], ALU.mult, [b["LAMT"]], [b["LAMT"]])
        S.op("dve", lambda e: e.tensor_reduce(out=LAMS[:, 0:1], in_=LAMT[:, 0, :], axis=AX.X, op=ALU.add),
             [b["LAMT"]], [b["LAMS"]])
        S.op("dve", lambda e: e.tensor_reduce(out=LAMS[:, 1:2], in_=LAMT[:, 2, :], axis=AX.X, op=ALU.add),
             [b["LAMT"]], [b["LAMS"]])
        act(LAMS[:, 4:6], LAMS[:, 0:2], AF.Exp, [b["LAMS"]], [b["LAMS"]])
        dve_tt(LAMS[:, 2:3], LAMS[:, 4:5], LAMS[:, 5:6], ALU.subtract, [b["LAMS"]], [b["LAMS"]])
        dve_ts(LAMS[:, 3:4], LAMS[:, 2:3], LAM_INIT, -1.0, ALU.add, ALU.mult, [b["LAMS"]], [b["LAMS"]])
        NEGLAM = LAMS[:, 3:4]
        dve_ts(SUBS[:], SUBS[:], 1.0 - LAM_INIT, None, ALU.mult, None, [b["SUBS"]], [b["SUBS"]])

        dbg_t = {}
        wlist = [(a_win_d, w1s, 4096, 0, "w1"), (a_wout_d, w2s, D, None, "w2"), (kvw_d, w3s, 2048, 1, "w3"),
                 (b_win_d, w4s, 2048, 2, "w4"), (b_wout_d, w5s, D, None, "w5")]
        conv_jobs = []
        conv_state = {"loaded": 0, "done": 0}
        CONV_LA = NCX - 1

        def conv_load(i):
            src, dst, ncol, gi, key, kc, c0 = conv_jobs[i]
            xi = i % NCX
            dma(CXL[xi][:], dap(src, kc * 128 * ncol + c0, [[ncol, 128], [1, D]]), CXLb[xi], d_in,
                queue="sp" if i % 2 == 0 else "pool")

        def conv_compute(i):
            src, dst, ncol, gi, key, kc, c0 = conv_jobs[i]
            xi = i % NCX
            ui = i % 3
            if gi is None:
                evac(UU[ui][:], CXL[xi][:], [CXLb[xi]], [UUb[ui]])
            elif i % 2 == 0:
                act(UU[ui][:], CXL[xi][:], AF.Copy, [CXLb[xi], b["GN"]], [UUb[ui]], scale=GN[:, gi, kc:kc + 1])
            else:
                dve_ts(UU[ui][:], CXL[xi][:], GN[:, gi, kc:kc + 1], None, ALU.mult, None,
                       [CXLb[xi], b["GN"]], [UUb[ui]])
            dma(dap(dst, kc * 128 * ncol + c0, [[ncol, 128], [1, D]]), UU[ui][:], d_w[key], UUb[ui],
                queue="pool" if i % 2 == 0 else "sp")
            _prune(d_w[key])

        def conv_step():
            i = conv_state["done"]
            while conv_state["loaded"] < min(len(conv_jobs), i + 1 + CONV_LA):
                conv_load(conv_state["loaded"])
                conv_state["loaded"] += 1
            conv_compute(i)
            conv_state["done"] += 1

        bg_work = []
        conv_done = {}

        def phase0b():
            for (src, dst, ncol, gi, key) in wlist:
                for c0 in range(0, ncol, D):
                    for kc in range(8):
                        conv_jobs.append((src, dst, ncol, gi, key, kc, c0))
                        bg_work.append((key, c0 // D))

        def bg_pop(n=1):
            for _ in range(n):
                if bg_work:
                    kk = bg_work.pop(0)
                    conv_step()
                    conv_done[kk] = conv_done.get(kk, 0) + 1

        def ensure_converted(key, blk):
            kk = (key, blk // 2)
            while conv_done.get(kk, 0) < 8:
                assert bg_work, kk
                bg_pop()

        if dbg_stage is None or dbg_stage >= 1:
            phase0b()
            bg_pop(16)
        dve_copy(T1[0:16, 0:257], RB16[:, :], [b["RB16"]], [b["T1"]])
        dve_ts(T1[0:16, 257:512], ZER[0:16, 0:255], RB16[:, 256:257], None, ALU.add, None,
               [b["ZER"], b["RB16"]], [b["T1"]])
        dma(bx_s.ap(), T1[0:16, :], d_w["bx"], b["T1"])
        for h in range(16):
            hk, hkb = RR[h % 2], RRb[h % 2]
            pb, pbb = PB[h % 2], PBb[h % 2]
            dma(hk[:, 0:256], dap(bx_s, h * 512 + 1, [[1, 128], [1, 256]]), hkb, d_w["bx"])
            S.op("pe", lambda e, pb=pb, hk=hk: e.matmul(pb[:, 0:256], lhsT=JJ[:], rhs=hk[:, 0:256], start=True, stop=True),
                 [b["JJ"], hkb], [pbb])
            act(EB[:, h, 0:256], pb[:, 0:256], AF.Exp, [pbb, b["NCH"]], [b["EB"]], bias=NCH[:, h:h + 1])
        S.op("dve", lambda e: e.memset(EB[64:128, :, 0:64], 0.0), [], [b["EB"]])

        wsrc = {"w1": (w1s, 4096), "w2": (w2s, D), "w3": (w3s, 2048), "w4": (w4s, 2048), "w5": (w5s, D)}
        group_blocks = ([("w1", c) for c in range(8)] + [("w2", c) for c in range(2)] + [("w3", c) for c in range(4)]
                        + [("w4", c) for c in range(4)] + [("w5", c) for c in range(2)])
        all_blocks = group_blocks * (NSEQ * NG)
        wstate = {"issued": 0, "used": 0}

        def wissue(upto):
            while wstate["issued"] <= upto and wstate["issued"] < len(all_blocks):
                n = wstate["issued"]
                key, c = all_blocks[n]
                ensure_converted(key, c)
                t, ncol = wsrc[key]
                slot = n % NW
                dma(WB[slot][:], dap(t, c * 512, [[ncol, 128], [128 * ncol, 8], [1, 512]]), WBb[slot], d_w[key])
                _prune(d_w[key])
                wstate["issued"] += 1

        def wnext(expect, live_prev=0):
            n = wstate["used"]
            assert all_blocks[n] == expect, (all_blocks[n], expect)
            wissue(n + NW - 1 - live_prev)
            wstate["used"] += 1
            return WB[n % NW], WBb[n % NW]

        pbr = [0]

        def next_pb(lo, hi):
            i = lo + pbr[0] % (hi - lo)
            pbr[0] += 1
            return PB[i], PBb[i]

        xtr = [0]

        def next_xtl():
            i = xtr[0] % NXT
            xtr[0] += 1
            return XTL[i], XTLb[i]

        ptr = [0]

        def next_pt():
            i = ptr[0] % NPT
            ptr[0] += 1
            return PT[i], PTb[i]

        def rstd_from_ss(ss_ap, n, out_ap, stb):
            act(out_ap, ss_ap, AF.Ln, [stb, b["EPSC"]], [stb], scale=1.0 / n, bias=EPSC[:, 0:1])
            act(out_ap, out_ap, AF.Exp, [stb], [stb], scale=-0.5)

        def norm_part(src_tile, src_buf, gi):
            ui = gi % 2
            st_, stb = STN[ui], STNb[ui]
            act(YN[ui][:], src_tile[:], AF.Square, [src_buf], [YNb[ui], stb], accum_out=st_[:, 0:1])
            rstd_from_ss(st_[:, 0:1], D, st_[:, 1:2], stb)
            act(UU[ui][:], src_tile[:], AF.Copy, [src_buf, stb], [UUb[ui]], scale=st_[:, 1:2])

        def tr_part(tcol, gi, XD, XDb):
            ui = gi % 2
            for kc in range(8):
                S.op("pe", lambda e, kc=kc: e.transpose(TRB[:, kc * 128:(kc + 1) * 128],
                                                        UU[ui][:, kc * 128:(kc + 1) * 128], IDb16[:]),
                     [UUb[ui], b["ID"]], [TRBb])
            evac(XD[:, :, tcol:tcol + 128], TRB[:].rearrange("p (k t) -> p k t", t=128), [TRBb], [XDb])

        def norm_transpose(src_tile, src_buf, tcol, gi, XD, XDb):
            norm_part(src_tile, src_buf, gi)
            tr_part(tcol, gi, XD, XDb)

        def proj_fm(wb, wbb, c, post, XS=None, XSb=None):
            if XS is None:
                XS, XSb = XT, b["XT"]
            pb, pbb = next_pb(0, 3)
            for kc in range(8):
                mm(pb[:, :], wb[:, kc, c * 128:(c + 1) * 128], XS[:, kc, :], kc == 0, [wbb, XSb], [pbb])
            post(pb, pbb)
            bg_pop()

        def outproj_norm_residual(w_key, gp_tile, gp_buf, s, g, layer, XS, XSb):
            w0, w0b = wnext((w_key, 0))
            w1_, w1b = wnext((w_key, 1), live_prev=1)
            xts = {}
            bankss = {}
            ob7 = [0]

            def mm_tile(t):
                row0 = g * GT + t * 128
                xt, xtb = next_xtl()
                if layer == 0:
                    dma(xt[:], x_ap[s, row0:row0 + 128, :], xtb, d_in)
                else:
                    dma(xt[:], out_ap[s, row0:row0 + 128, :], xtb, d_out[s][g][t])
                xts[t] = (xt, xtb)
                banks = []
                for hb, (w, wbuf) in enumerate([(w0, w0b), (w1_, w1b)]):
                    bi_ = ob7[0] % 6
                    ob7[0] += 1
                    pb, pbb = PB[bi_], PBb[bi_]
                    for fc in range(8):
                        mm(pb[:, :], XS[:, fc, t * 128:(t + 1) * 128], w[:, fc, :], fc == 0, [XSb, wbuf], [pbb])
                    banks.append((pb, pbb))
                bankss[t] = banks

            def post_tile(t):
                row0 = g * GT + t * 128
                xt, xtb = xts.pop(t)
                banks = bankss.pop(t)
                yi = t % 2
                so_, sob = STO[yi], STOb[yi]
                for hb, (pb, pbb) in enumerate(banks):
                    act(YN[yi][:, hb * 512:(hb + 1) * 512], pb[:, :], AF.Square, [pbb], [YNb[yi], sob],
                        accum_out=so_[:, 2 + hb:3 + hb])
                dve_tt(so_[:, 0:1], so_[:, 2:3], so_[:, 3:4], ALU.add, [sob], [sob])
                rstd_from_ss(so_[:, 0:1], D, so_[:, 1:2], sob)
                for hb, (pb, pbb) in enumerate(banks):
                    S.op("dve", lambda e, pb=pb, hb=hb, yi=yi, so_=so_: e.scalar_tensor_tensor(
                        out=YN[yi][:, hb * 512:(hb + 1) * 512], in0=pb[:, :], scalar=so_[:, 1:2],
                        in1=gp_tile[:, hb * 512:(hb + 1) * 512], op0=ALU.mult, op1=ALU.mult),
                        [pbb, sob, gp_buf], [YNb[yi]])
                dve_tt(xt[:], xt[:], YN[yi][:], ALU.add, [xtb, YNb[yi]], [xtb])
                dma(out_ap[s, row0:row0 + 128, :], xt[:], d_out[s][g][t], xtb, queue="pool")
                if layer == 0:
                    norm_part(xt, xtb, t)

            NT = DBG_TILES
            INFL = 2
            for t in range(min(INFL, NT)):
                mm_tile(t)
            for t in range(NT):
                post_tile(t)
                if t + INFL < NT:
                    mm_tile(t + INFL)
                if layer == 0:
                    tr_part(t * 128, t, X2T, b["X2T"])

        glist = [(s, g) for s in range(NSEQ) for g in range(NG)]
        if dbg_groups is not None:
            glist = glist[:dbg_groups]

        def run_all():
            stage(0)
            stage(1)
            pre(*glist[0])
            for gi_, (s, g) in enumerate(glist):
                do_group(s, g, glist[gi_ + 1] if gi_ + 1 < len(glist) else None)

        def pre(s, g):
            if True:
                tok0 = g * GT
                dma(POSI[:], dap(pos_d, s * SEQ + tok0, [[0, 128], [1, GT]]), b["POSI"], d_in)
                dve_copy(T0[:], POSI[:], [b["POSI"]], [b["T0"]])
                dve_ts(T0[:], T0[:], RTAB[:, 0:1], None, ALU.mult, None, [b["T0"], b["RTAB"]], [b["T0"]])
                TWO_PI = 2.0 * math.pi
                for (dst, dstb, shift) in ((SIN, b["SIN"], 0.0), (COS, b["COS"], math.pi / 2)):
                    dve_ts(T1[:], T0[:], shift, 1.0 / TWO_PI, ALU.add, ALU.mult, [b["T0"]], [b["T1"]])
                    dve_copy(POSI[:], T1[:], [b["T1"]], [b["POSI"]])
                    dve_copy(T1[:], POSI[:], [b["POSI"]], [b["T1"]])
                    dve_ts(T1[:], T1[:], -TWO_PI, shift, ALU.mult, ALU.add, [b["T1"]], [b["T1"]])
                    dve_tt(T1[:], T1[:], T0[:], ALU.add, [b["T1"], b["T0"]], [b["T1"]])
                    dve_ts(RS[:], T1[:], math.pi, TWO_PI, ALU.is_gt, ALU.mult, [b["T1"]], [b["RS"]])
                    dve_tt(T1[:], T1[:], RS[:], ALU.subtract, [b["T1"], b["RS"]], [b["T1"]])
                    dve_ts(RS[:], T1[:], -math.pi, TWO_PI, ALU.is_lt, ALU.mult, [b["T1"]], [b["RS"]])
                    dve_tt(T1[:], T1[:], RS[:], ALU.add, [b["T1"], b["RS"]], [b["T1"]])
                    act(dst[:], T1[:], AF.Sin, [b["T1"]], [dstb])
                dve_ts(SIN[:], SIN[:], RTAB[:, 1:2], None, ALU.mult, None, [b["SIN"], b["RTAB"]], [b["SIN"]])
                dve_ts(COS[:], COS[:], RTAB[:, 1:2], RTAB[:, 2:3], ALU.mult, ALU.add, [b["COS"], b["RTAB"]], [b["COS"]])

                for t in range(4):
                    xt, xtb = next_xtl()
                    dma(xt[:], x_ap[s, tok0 + t * 128: tok0 + (t + 1) * 128, :], xtb, d_in)
                    norm_transpose(xt, xtb, t * 128, t, XT, b["XT"])

        def do_group(s, g, nxt):
            if True:
                tok0 = g * GT
                kcol0 = (g % 2) * 512
                for blk in range(2):
                    wb, wbb = wnext(("w1", blk))
                    for c in range(4):
                        fc = blk * 4 + c
                        proj_fm(wb, wbb, c, lambda pb, pbb, fc=fc: evac(QT[:, fc, :], pb[:, :], [pbb], [b["QT"]]))
                for blk in range(2):
                    wb, wbb = wnext(("w1", 2 + blk))
                    for c in range(4):
                        fc = blk * 4 + c
                        proj_fm(wb, wbb, c, lambda pb, pbb, fc=fc: evac(KT[:, fc, kcol0:kcol0 + 512], pb[:, :],
                                                                        [pbb], [b["KT"]]))
                for blk in range(2):
                    wb, wbb = wnext(("w1", 4 + blk))
                    for t in range(4):
                        slot = (4 * g + t) % 8
                        pb, pbb = next_pb(0, 3)
                        for kc in range(8):
                            mm(pb[:, :], XT[:, kc, t * 128:(t + 1) * 128], wb[:, kc, :], kc == 0,
                               [b["XT"], wbb], [pbb])
                        src = pb[:, :].rearrange("p (i c) -> p i c", c=128)
                        dst = VA[:, slot, blk * 768:(blk + 1) * 768].rearrange("p (i c) -> p i c", c=192)
                        act(dst[:, :, 0:64], src[:, :, 0:64], AF.Copy, [pbb], [b["VA"]])
                        dve_copy(dst[:, :, 128:192], src[:, :, 64:128], [pbb], [b["VA"]])
                for blk in range(2):
                    wb, wbb = wnext(("w1", 6 + blk))
                    for c in range(4):
                        fc = blk * 4 + c
                        proj_fm(wb, wbb, c, lambda pb, pbb, fc=fc: act(SGT[:, fc, :], pb[:, :], AF.Silu,
                                                                       [pbb], [b["SGT"]]))

                stage(2)
                items = []
                tl_ = []
                for Tk in range(max(0, 4 * g - 4), 4 * g + 4):
                    qlo = max(Tk, 4 * g)
                    qhi = min(Tk + 4, 4 * g + 3)
                    tl_.append((Tk, (qlo - 4 * g) * 128, (qhi + 1 - 4 * g) * 128, qlo - Tk))
                for i_ in range(8):
                    for n, (Tk, c0, c1, jlo) in enumerate(tl_):
                        for h in (2 * i_, 2 * i_ + 1):
                            items.append((h, n, Tk, c0, c1, jlo, n == len(tl_) - 1))
                DEPTH_A = 4
                sbank = {}
                sA = [0]

                def qkA(idx):
                    h, n, Tk, c0, c1, jlo, last = items[idx]
                    i, r0 = h // 2, 64 * (h % 2)
                    bi_ = (0, 1, 2, 3, 6)[sA[0] % 5]
                    sA[0] += 1
                    pb, pbb = PB[bi_], PBb[bi_]
                    kc0 = (Tk % 8) * 128
                    mm(pb[:, c0:c1], KT[r0:r0 + 64, i, kc0:kc0 + 128], QT[r0:r0 + 64, i, c0:c1], True,
                       [b["KT"], b["QT"]], [pbb])
                    sbank[idx] = (pb, pbb)

                def finA(idx):
                    h, n, Tk, c0, c1, jlo, last = items[idx]
                    i, half = h // 2, h % 2
                    r0 = 64 * half
                    ob, obb = PB[4 + h % 2], PBb[4 + h % 2]
                    pb, pbb = sbank.pop(idx)
                    pt, ptb = next_pt()
                    act(pt[:, c0:c1], pb[:, c0:c1], AF.Exp, [pbb], [ptb], scale=0.125)
                    jhi = jlo + (c1 - c0) // 128 - 1
                    if jlo <= 1:
                        nb_ = (min(jhi, 1) - jlo + 1) * 128
                        dve_tt(pt[:, c0:c0 + nb_], pt[:, c0:c0 + nb_], EB[:, h, jlo * 128: jlo * 128 + nb_],
                               ALU.mult, [ptb, b["EB"]], [ptb])
                    if jhi == 4:
                        S.op("dve", lambda e, pt=pt, c1=c1: e.memset(pt[0:64, c1 - 64:c1], 0.0), [], [ptb])
                    vcol = i * 192 + 64 * half
                    mm(ob[:, c0:c1], VA[:, Tk % 8, vcol:vcol + 128], pt[:, c0:c1], n == 0, [b["VA"], ptb], [obb])
                    if last:
                        so = 64 - r0
                        rr, rrb = RR[h % 2], RRb[h % 2]
                        act(rr[so:so + 64, :], ob[so:so + 64, :], AF.Ln, [obb], [rrb])
                        act(rr[so:so + 64, :], rr[so:so + 64, :], AF.Exp, [rrb], [rrb], scale=-1.0)
                        ta, tab_ = TA[h % 2], TAb[h % 2]
                        dve_tt(ta[r0:r0 + 64, :], ob[r0:r0 + 64, :], rr[so:so + 64, :], ALU.mult, [obb, rrb], [tab_])
                        dve_tt(XT[r0:r0 + 64, i, :], ta[r0:r0 + 64, :], SGT[r0:r0 + 64, i, :], ALU.mult,
                               [tab_, b["SGT"]], [b["XT"]])
                        bg_pop(3)

                for idx in range(0, len(items) + DEPTH_A, 2):
                    for k_ in (idx - DEPTH_A, idx + 1 - DEPTH_A):
                        if 0 <= k_ < len(items):
                            finA(k_)
                    for k_ in (idx, idx + 1):
                        if k_ < len(items):
                            qkA(k_)

                stage(3)
                outproj_norm_residual("w2", GPA, b["GPA"], s, g, 0, XT, b["XT"])

                stage(4)
                rp = [0]

                def rope_post(pb, pbb, dst_ap, dst_buf):
                    k = rp[0] % 2
                    rp[0] += 1
                    kr, krb, ta, tab_, tb, tbb = KRAW[k], KRAWb[k], TA[k], TAb[k], TBB[k], TBBb[k]
                    act(kr[:], pb[:, :], AF.Copy, [pbb], [krb])
                    p2, p2b = PB[3 + k], PBb[3 + k]
                    mm(p2[:, :], PERMb[:], kr[:], True, [b["PERM"], krb], [p2b])
                    dve_tt(ta[:], pb[:, :], COS[:], ALU.mult, [pbb, b["COS"]], [tab_])
                    dve_tt(tb[:], p2[:, :], SIN[:], ALU.mult, [p2b, b["SIN"]], [tbb])
                    dve_tt(dst_ap, ta[:], tb[:], ALU.add, [tab_, tbb], [dst_buf])

                for blk in range(2):
                    wb, wbb = wnext(("w3", blk))
                    for c in range(4):
                        hh = blk * 4 + c
                        ks, ksb = KST[hh % 2], KSTb[hh % 2]
                        proj_fm(wb, wbb, c, lambda pb, pbb, ks=ks, ksb=ksb: rope_post(pb, pbb, ks[:], ksb),
                                X2T, b["X2T"])
                        dma(dap(ksh_s, ((s * 8 + hh) * 128) * SEQ + tok0, [[SEQ, 128], [1, GT]]), ks[:],
                            d_ksh[s][g], ksb, queue="pool")
                        _prune(d_ksh[s][g])
                for t in range(4):
                    pass
                vblocks = [wnext(("w3", 2)), wnext(("w3", 3), live_prev=1)]
                for t in range(4):
                    vs, vsb = VST[t % 2], VSTb[t % 2]
                    for blk in range(2):
                        wb, wbb = vblocks[blk]
                        pb, pbb = next_pb(0, 3)
                        for kc in range(8):
                            mm(pb[:, :], X2T[:, kc, t * 128:(t + 1) * 128], wb[:, kc, :], kc == 0,
                               [b["X2T"], wbb], [pbb])
                        evac(vs[:, blk * 512:(blk + 1) * 512], pb[:, :], [pbb], [vsb])
                    dma(dap(vsh_s, (s * 8 * SEQ + tok0 + t * 128) * 128, [[128, 128], [SEQ * 128, 8], [1, 128]]),
                        vs[:].rearrange("p (h v) -> p h v", v=128), d_vsh[s][g], vsb, queue="pool")
                    _prune(d_vsh[s][g])

                for blk in range(2):
                    wb, wbb = wnext(("w4", blk))
                    for c in range(4):
                        hh = blk * 4 + c
                        proj_fm(wb, wbb, c, lambda pb, pbb, hh=hh: rope_post(pb, pbb, QT[:, hh, :], b["QT"]),
                                X2T, b["X2T"])
                for blk in range(2):
                    wb, wbb = wnext(("w4", 2 + blk))
                    for c in range(4):
                        hh = blk * 4 + c

                        def gpost(pb, pbb, hh=hh):
                            act(T0[:], pb[:, :], AF.Silu, [pbb], [b["T0"]])
                            dve_ts(SGT[:, hh, :], T0[:], SUBS[:, 0:1], None, ALU.mult, None,
                                   [b["T0"], b["SUBS"]], [b["SGT"]])
                        proj_fm(wb, wbb, c, gpost, X2T, b["X2T"])

                stage(5)
                if nxt is not None:
                    pre(*nxt)
                ntile = 4 * g + 4
                O0, O0b, O1, O1b = PB[3], PBb[3], PB[4], PBb[4]
                Z0, Z0b, Z1, Z1b = PB[5], PBb[5], PB[6], PBb[6]
                itemsB = [(hh, n) for hh in range(8) for n in range(ntile)]
                kvslot = {}
                sbankB = {}
                pend_post = [None]

                def loadB(hh):
                    kb, kbb = KB[hh % 2], KBb[hh % 2]
                    vb, vbb = VB[hh % 2], VBb[hh % 2]
                    for gg in range(g + 1):
                        dma(kb[:, gg * GT:(gg + 1) * GT],
                            dap(ksh_s, ((s * 8 + hh) * 128) * SEQ + gg * GT, [[SEQ, 128], [1, GT]]),
                            kbb, d_ksh[s][gg])
                        dma(vb[:, gg * 4:(gg + 1) * 4, :],
                            dap(vsh_s, ((s * 8 + hh) * SEQ + gg * GT) * 128, [[128, 128], [128 * 128, 4], [1, 128]]),
                            vbb, d_vsh[s][gg])
                    kvslot[hh] = (kb, kbb, vb, vbb)

                def qkB(idx):
                    hh, n = itemsB[idx]
                    if n == 0:
                        loadB(hh)
                    kb, kbb, vb, vbb = kvslot[hh]
                    c0 = max(0, n - 4 * g) * 128
                    res = []
                    for m in range(2):
                        pb, pbb = next_pb(0, 3)
                        mm(pb[:, c0:GT], kb[64 * m:64 * m + 64, n * 128:(n + 1) * 128],
                           QT[64 * m:64 * m + 64, hh, c0:GT], True, [kbb, b["QT"]], [pbb])
                        res.append((pb, pbb))
                    sbankB[idx] = res

                def expB(idx):
                    hh, n = itemsB[idx]
                    c0 = max(0, n - 4 * g) * 128
                    res = sbankB.pop(idx)
                    pts = []
                    for m in range(2):
                        pb, pbb = res[m]
                        pt, ptb = next_pt()
                        act(pt[:, c0:GT], pb[:, c0:GT], AF.Exp, [pbb], [ptb], scale=0.125)
                        if n >= 4 * g:
                            S.op("dve", lambda e, pt=pt, c0=c0: e.memset(pt[64:128, c0:c0 + 64], 0.0), [], [ptb])
                        pts.append((pt, ptb))
                    return pts

                def pvB(idx, pts):
                    hh, n = itemsB[idx]
                    kb, kbb, vb, vbb = kvslot[hh]
                    c0 = max(0, n - 4 * g) * 128
                    for m, (O, Ob) in enumerate([(O0, O0b), (O1, O1b)]):
                        pt, ptb = pts[m]
                        mm(O[:, c0:GT], vb[:, n, :], pt[:, c0:GT], n == 0, [vbb, ptb], [Ob])
                    for m in range(2):
                        pt, ptb = pts[m]
                        S.op("pe", lambda e, m=m, pt=pt, c0=c0, n=n: e.matmul(
                            Z0[64 * m:64 * m + 64, c0:GT], lhsT=ONESb[:, 0:64], rhs=pt[:, c0:GT], start=(n == 0),
                            stop=False, skip_group_check=True, tile_position=(0, 64 * m)),
                            [b["ONES"], ptb], [Z0b])
                    if n == 0 and pend_post[0] is not None:
                        pend_post[0]()
                        pend_post[0] = None
                    if n == ntile - 1:
                        act(RR[0][:], Z0[:, :], AF.Ln, [Z0b], [RRb[0]])
                        act(RR[0][:], RR[0][:], AF.Exp, [RRb[0]], [RRb[0]], scale=-1.0)
                        for r_ in (0, 64):
                            dve_tt(T0[r_:r_ + 64, :], O0[r_:r_ + 64, :], RR[0][0:64, :], ALU.mult, [O0b, RRb[0]], [b["T0"]])
                            dve_tt(T1[r_:r_ + 64, :], O1[r_:r_ + 64, :], RR[0][64:128, :], ALU.mult, [O1b, RRb[0]], [b["T1"]])
                        S.op("dve", lambda e: e.scalar_tensor_tensor(out=T0[:], in0=T1[:], scalar=NEGLAM, in1=T0[:],
                                                                     op0=ALU.mult, op1=ALU.add),
                             [b["T0"], b["T1"], b["LAMS"]], [b["T0"]])
                        act(SQ[:], T0[:], AF.Square, [b["T0"]], [b["SQ"]])

                        def post2(hh=hh):
                            pb, pbb = next_pb(0, 3)
                            mm(pb[:, :], ONESb[:], SQ[:], True, [b["ONES"], b["SQ"]], [pbb])
                            act(RS[:], pb[:, :], AF.Ln, [pbb, b["EPSC"]], [b["RS"]], scale=1.0 / 128, bias=EPSC[:, 0:1])
                            act(RS[:], RS[:], AF.Exp, [b["RS"]], [b["RS"]], scale=-0.5)
                            dve_tt(T0[:], T0[:], RS[:], ALU.mult, [b["T0"], b["RS"]], [b["T0"]])
                            dve_tt(X2T[:, hh, :], T0[:], SGT[:, hh, :], ALU.mult, [b["T0"], b["SGT"]], [b["X2T"]])
                        pend_post[0] = post2

                qkB(0)
                for idx in range(len(itemsB)):
                    pts = expB(idx)
                    if idx + 1 < len(itemsB):
                        qkB(idx + 1)
                    pvB(idx, pts)
                if pend_post[0] is not None:
                    pend_post[0]()
                    pend_post[0] = None

                stage(6)
                outproj_norm_residual("w5", GPB, b["GPB"], s, g, 1, X2T, b["X2T"])

        try:
            run_all()
        except _Stop:
            pass
        outs = [d_out[s][g][t] for s in range(NSEQ) for g in range(NG) for t in range(4)]
        names = {"EB": (EB, [128, 16 * 256], BF16), "CH": (CH, [128, 16], F32), "LAMS": (LAMS, [128, 8], F32),
                 "XT": (XT, [128, 8 * GT], BF16), "X2T": (X2T, [128, 8 * GT], BF16), "QT": (QT, [128, 8 * GT], BF16),
                 "SGT": (SGT, [128, 8 * GT], BF16), "KT": (KT, [128, 8 * 1024], BF16), "VA": (VA, [128, 8 * 1536], BF16),
                 "COS": (COS, [128, GT], F32), "SIN": (SIN, [128, GT], F32), "GN": (GN, [128, 24], F32),
                 "PERM": (PERMb, [128, 128], BF16), "ST": (ST, [128, 16], F32), "YN0": (YN[0], [128, D], F32),
                 "YN1": (YN[1], [128, D], F32)}
        b["YN0"] = YNb[0]; b["YN1"] = YNb[1]
        for nm in dbg_dump:
            tl, shp, dt = names[nm]
            dd = nc.dram_tensor("dbg_" + nm, shp, dt, kind="ExternalOutput")
            db_ = Buf("dbg_" + nm)
            src = tl[:] if len(tl.shape) == 2 else tl[:].rearrange("p a b -> p (a b)")
            dma(dd.ap(), src, db_, b.get(nm, b.get("PERM")), queue="pool")
            outs.append(db_)
        S.final_wait("pool", outs)
        print("sbuf bytes remaining/partition:", nc.sbuf_bytes_remaining, flush=True)
        S.emit(st)
        print("instr counts:", {e: len(S.ops[e]) for e in ENGS}, "dma sems:", S.ndsem, flush=True)
    return nc


_CACHE = {}


def _consts():
    ident = np.eye(128, dtype=np.float32)
    ones = np.ones((128, 128), np.float32)
    perm = np.zeros((128, 128), np.float32)
    for base in (0, 64):
        for d in range(8):
            perm[base + d + 8, base + d] = -1.0
            perm[base + d, base + d + 8] = 1.0
    anti = np.ascontiguousarray(np.eye(128, dtype=np.float32)[::-1])
    cmat = np.stack([ident, ones, perm, anti]).astype(np.float32)
    rtab = np.zeros((128, 4), np.float32)
    inv = np.power(np.float32(ROPE_THETA), -np.arange(8, dtype=np.float32) * np.float32(2.0) / np.float32(16))
    for p in range(128):
        d = p % 64
        if d < 16:
            rtab[p, 0] = inv[d % 8]
            rtab[p, 1] = 1.0
        else:
            rtab[p, 2] = 1.0
        rtab[p, 3] = math.pi
    return cmat, rtab.astype(np.float32)


def kernel(x, positions, a_norm_pre, a_w_in, a_rel_bias, a_w_out, a_norm_post, kv_norm, kv_w,
           b_norm_pre, b_w_in, b_lambda_q1, b_lambda_k1, b_lambda_q2, b_lambda_k2, b_subln,
           b_w_out, b_norm_post):
    f = lambda a: np.ascontiguousarray(np.asarray(a, dtype=np.float32))
    x = f(x)
    positions = np.ascontiguousarray(np.asarray(positions, dtype=np.int32))
    cmat, rtab = _consts()
    shared = {
        "a_norm_pre": f(a_norm_pre).reshape(D), "a_w_in": f(a_w_in).reshape(D, 4096),
        "a_rel_bias": f(a_rel_bias).reshape(16, 257), "a_w_out": f(a_w_out).reshape(D, D),
        "a_norm_post": f(a_norm_post).reshape(D), "kv_norm": f(kv_norm).reshape(D),
        "kv_w": f(kv_w).reshape(D, 2048), "b_norm_pre": f(b_norm_pre).reshape(D),
        "b_w_in": f(b_w_in).reshape(D, 2048),
        "b_lam": np.stack([f(b_lambda_q1).reshape(64), f(b_lambda_k1).reshape(64),
                           f(b_lambda_q2).reshape(64), f(b_lambda_k2).reshape(64)]),
        "b_subln": f(b_subln).reshape(128), "b_w_out": f(b_w_out).reshape(D, D),
        "b_norm_post": f(b_norm_post).reshape(D), "cmat": cmat, "rtab": rtab,
    }
    if "nc" not in _CACHE:
        _CACHE["nc"] = build_program()
    nc = _CACHE["nc"]
    in_maps = []
    for c in range(NCORES):
        m = dict(shared)
        m["x"] = x[c * NSEQ:(c + 1) * NSEQ]
        m["pos"] = positions[c * NSEQ:(c + 1) * NSEQ]
        in_maps.append(m)
    res = run_bass_kernel_spmd(nc, in_maps, core_ids=list(range(NCORES)))
    return np.concatenate([np.asarray(r["out"]).reshape(NSEQ, SEQ, D) for r in res.results], axis=0).astype(np.float32)
```

```python
import math
from contextlib import ExitStack
import numpy as np
import concourse.bass as bass
import concourse.mybir as mybir
from concourse.bass_utils import run_bass_kernel_spmd

F32 = mybir.dt.float32
BF16 = mybir.dt.bfloat16
I32 = mybir.dt.int32
AF = mybir.ActivationFunctionType
ALU = mybir.AluOpType
AX = mybir.AxisListType

NCORES = 8
SEQ = 2048
D = 1024
NSEQ = 2
GT = 512
NG = SEQ // GT
EPS = 1e-6
LAM_INIT = 0.8 - 0.6 * math.exp(-0.3 * 1)
ROPE_THETA = 500000.0

ENGS = ("pe", "act", "dve", "pool", "sp")
DBG_TILES = 4


class Buf:
    __slots__ = ("name", "w", "r", "dsem", "excl")

    def __init__(self, name, excl=False):
        self.name = name
        self.w = None
        self.r = []
        self.dsem = None
        self.excl = excl


class Sched:
    def __init__(self, nc):
        self.nc = nc
        self.ops = {e: [] for e in ENGS}
        self.cnt = {e: 0 for e in ENGS}
        self.seen = {e: {} for e in ENGS}
        self.ndsem = 0
        self.dcnt = {}

    def _waits(self, eng, reads, writes):
        waits = {}

        def need(t):
            if t is None:
                return
            k, n = t
            if eng == "pe" and k == "pe":
                return
            if n > self.seen[eng].get(k, 0) and n > waits.get(k, 0):
                waits[k] = n

        for b in reads:
            need(b.w)
            if b.excl:
                for t in b.r:
                    if t[0] != eng:
                        need(t)
        for b in writes:
            need(b.w)
            for t in b.r:
                need(t)
        for k, n in waits.items():
            self.seen[eng][k] = n
        return list(waits.items())

    def op(self, eng, fn, reads=(), writes=()):
        waits = self._waits(eng, reads, writes)
        self.cnt[eng] += 1
        tick = (eng, self.cnt[eng])
        for b in reads:
            if len(b.r) > 64:
                _prune(b)
            b.r.append(tick)
        for b in writes:
            b.w = tick
            b.r = []
        self.ops[eng].append((fn, waits, tick))
        return tick

    def dma(self, fn, dst, src, queue="sp"):
        waits = self._waits(queue, [src], [dst])
        if dst.dsem is None:
            dst.dsem = "q%d" % self.ndsem
            self.ndsem += 1
            self.dcnt[dst.dsem] = 0
        self.dcnt[dst.dsem] += 16
        tick = (dst.dsem, self.dcnt[dst.dsem])
        if len(src.r) > 64:
            _prune(src)
        src.r.append(tick)
        dst.w = tick
        dst.r = []
        self.ops[queue].append((fn, waits, tick))
        return tick

    def final_wait(self, eng, bufs):
        waits = self._waits(eng, [], bufs)
        self.ops[eng].append((None, waits, None))

    def emit(self, stack):
        nc = self.nc
        sems = {}
        for e in ENGS:
            sems[e] = stack.enter_context(nc.semaphore("s_" + e))
        for k in self.dcnt:
            sems[k] = stack.enter_context(nc.semaphore("s_" + k))
        block = stack.enter_context(nc.Block())

        def run(ename):
            def body(eng):
                for fn, waits, tick in self.ops[ename]:
                    for k, n in waits:
                        eng.wait_ge(sems[k], n)
                    if fn is None:
                        continue
                    ins = fn(eng)
                    k, n = tick
                    ins.then_inc(sems[k], 16 if k.startswith("q") else 1)
            return body

        block.tensor(run("pe"))
        block.scalar(run("act"))
        block.vector(run("dve"))
        block.gpsimd(run("pool"))
        block.sync(run("sp"))


def _prune(b):
    best = {}
    for k, n in b.r:
        if n > best.get(k, 0):
            best[k] = n
    b.r = list(best.items())


class _Stop(Exception):
    pass


def build_program(dbg_stage=None, dbg_groups=None, dbg_dump=()):
    def stage(k):
        if dbg_stage is not None and dbg_stage == k:
            raise _Stop()

    nc = bass.Bass("TRN2", target_bir_lowering=False)

    def din(name, shape, dt=F32):
        return nc.dram_tensor(name, list(shape), dt, kind="ExternalInput")

    x_d = din("x", [NSEQ, SEQ, D])
    pos_d = din("pos", [NSEQ, SEQ], I32)
    a_npre_d = din("a_norm_pre", [D])
    a_win_d = din("a_w_in", [D, 4096])
    a_rb_d = din("a_rel_bias", [16, 257])
    a_wout_d = din("a_w_out", [D, D])
    a_npost_d = din("a_norm_post", [D])
    kvn_d = din("kv_norm", [D])
    kvw_d = din("kv_w", [D, 2048])
    b_npre_d = din("b_norm_pre", [D])
    b_win_d = din("b_w_in", [D, 2048])
    lam_d = din("b_lam", [4, 64])
    subln_d = din("b_subln", [128])
    b_wout_d = din("b_w_out", [D, D])
    b_npost_d = din("b_norm_post", [D])
    cmat_d = din("cmat", [4, 128, 128])
    rtab_d = din("rtab", [128, 4])
    out_d = nc.dram_tensor("out", [NSEQ, SEQ, D], F32, kind="ExternalOutput")

    w1s = nc.dram_tensor("w1s", [D, 4096], BF16)
    w2s = nc.dram_tensor("w2s", [D, D], BF16)
    w3s = nc.dram_tensor("w3s", [D, 2048], BF16)
    w4s = nc.dram_tensor("w4s", [D, 2048], BF16)
    w5s = nc.dram_tensor("w5s", [D, D], BF16)
    bx_s = nc.dram_tensor("bx_s", [16, 512], F32)
    chs = nc.dram_tensor("chs", [16], F32)
    ksh_s = nc.dram_tensor("ksh_s", [NSEQ, 8, 128, SEQ], BF16)
    vsh_s = nc.dram_tensor("vsh_s", [NSEQ, 8, SEQ, 128], BF16)

    x_ap = x_d.ap(); out_ap = out_d.ap(); cmat_ap = cmat_d.ap(); rtab_ap = rtab_d.ap()

    def dap(t, offset, ap):
        return bass.AP(tensor=t, offset=offset, ap=[list(a) for a in ap])

    S = Sched(nc)
    with ExitStack() as st:
        def sb(name, shape, dt):
            return st.enter_context(nc.sbuf_tensor(name, list(shape), dt))

        def ps(name, shape, dt=F32):
            return st.enter_context(nc.psum_tensor(name, list(shape), dt))

        IDb16 = sb("IDb16", [128, 128], BF16); ONESb = sb("ONESb", [128, 128], BF16)
        PERMb = sb("PERMb", [128, 128], BF16); JJ = sb("JJ", [128, 128], F32)
        CST = sb("CST", [128, 128], F32)
        RTAB = sb("RTAB", [128, 4], F32)
        EB = sb("EB", [128, 16, 256], BF16)
        CH = sb("CH", [128, 16], F32)
        RB16 = sb("RB16", [16, 257], F32)
        GPA = sb("GPA", [128, D], F32); GPB = sb("GPB", [128, D], F32)
        GN = sb("GN", [128, 3, 8], F32)
        SUBS = sb("SUBS", [128, 1], F32)
        LAMT = sb("LAMT", [128, 4, 64], F32)
        LAMS = sb("LAMS", [128, 8], F32)
        EPSC = sb("EPSC", [128, 2], F32)
        ZER = sb("ZER", [128, 256], F32)
        ST = sb("ST", [128, 16], F32)
        STN = [sb("STN%d" % i, [128, 4], F32) for i in range(2)]
        STO = [sb("STO%d" % i, [128, 4], F32) for i in range(2)]
        NCH = sb("NCH", [128, 16], F32)
        NW = 3
        WB = [sb("WB%d" % i, [128, 8, 512], BF16) for i in range(NW)]
        NXT = 2
        NCX = 3
        XTL = [sb("XTL%d" % i, [128, D], F32) for i in range(NXT)]
        CXL = [sb("CXL%d" % i, [128, D], F32) for i in range(NCX)]
        YN = [sb("YN%d" % i, [128, D], F32) for i in range(2)]
        UU = [sb("UU%d" % i, [128, D], BF16) for i in range(3)]
        XT = sb("XT", [128, 8, GT], BF16)
        X2T = sb("X2T", [128, 8, GT], BF16)
        QT = sb("QT", [128, 8, GT], BF16)
        SGT = sb("SGT", [128, 8, GT], BF16)
        KT = sb("KT", [128, 8, 1024], BF16)
        VA = sb("VA", [128, 8, 1536], BF16)
        NPT = 4
        PT = [sb("PT%d" % i, [128, GT], BF16) for i in range(NPT)]
        RR = [sb("RR%d" % i, [128, GT], F32) for i in range(2)]
        T0 = sb("T0", [128, GT], F32); T1 = sb("T1", [128, GT], F32)
        TA = [sb("TA%d" % i, [128, GT], F32) for i in range(2)]
        TBB = [sb("TBB%d" % i, [128, GT], F32) for i in range(2)]
        SQ = sb("SQ", [128, GT], BF16); RS = sb("RS", [128, GT], F32)
        KRAW = [sb("KRAW%d" % i, [128, GT], BF16) for i in range(2)]
        COS = sb("COS", [128, GT], F32); SIN = sb("SIN", [128, GT], F32)
        POSI = sb("POSI", [128, GT], I32)
        KB = [sb("KB%d" % i, [128, SEQ], BF16) for i in range(2)]
        VB = [sb("VB%d" % i, [128, 16, 128], BF16) for i in range(2)]
        KST = [sb("KST%d" % i, [128, GT], BF16) for i in range(2)]
        VST = [sb("VST%d" % i, [128, D], BF16) for i in range(2)]

        PB = [ps("PB%d" % i, [128, 512], F32) for i in range(7)]
        TRB = ps("TRB", [128, 1024], BF16)

        b = {}
        def B(name, excl=False):
            b[name] = Buf(name, excl)
            return b[name]
        for nm in ["ID", "ONES", "PERM", "JJ", "CST", "RTAB", "EB", "CH", "GPA", "GPB", "GN", "SUBS", "LAMT",
                   "LAMS", "EPSC", "ZER", "ST", "XT", "X2T", "RB16", "QT", "SGT", "KT", "VA", "T0", "T1", "SQ", "RS",
                   "COS", "SIN", "POSI"]:
            B(nm)
        WBb = [Buf("WB%d" % i) for i in range(NW)]
        STNb = [Buf("STN%d" % i) for i in range(2)]
        STOb = [Buf("STO%d" % i) for i in range(2)]
        TAb = [Buf("TA%d" % i) for i in range(2)]
        TBBb = [Buf("TBB%d" % i) for i in range(2)]
        KRAWb = [Buf("KRAW%d" % i) for i in range(2)]
        B("NCH")
        XTLb = [Buf("XTL%d" % i) for i in range(NXT)]
        CXLb = [Buf("CXL%d" % i) for i in range(NCX)]
        YNb = [Buf("YN%d" % i) for i in range(2)]
        UUb = [Buf("UU%d" % i) for i in range(3)]
        PTb = [Buf("PT%d" % i) for i in range(NPT)]
        RRb = [Buf("RR%d" % i) for i in range(2)]
        KBb = [Buf("KB%d" % i) for i in range(2)]
        VBb = [Buf("VB%d" % i) for i in range(2)]
        KSTb = [Buf("KST%d" % i) for i in range(2)]
        VSTb = [Buf("VST%d" % i) for i in range(2)]
        PBb = [Buf("PB%d" % i, True) for i in range(7)]
        TRBb = Buf("TRB", True)
        d_in = Buf("d_in")
        d_w = {k: Buf("d_" + k) for k in ["w1", "w2", "w3", "w4", "w5", "bx", "chs"]}
        d_ksh = [[Buf("d_ksh%d_%d" % (s, g)) for g in range(NG)] for s in range(NSEQ)]
        d_vsh = [[Buf("d_vsh%d_%d" % (s, g)) for g in range(NG)] for s in range(NSEQ)]
        d_out = [[[Buf("d_out%d_%d_%d" % (s, g, t)) for t in range(4)] for g in range(NG)] for s in range(NSEQ)]

        def act(out, in_, func, reads, writes, **kw):
            S.op("act", lambda e: e.activation(out=out, in_=in_, func=func, **kw), reads, writes)

        def mm(out, lhsT, rhs, start, reads, writes):
            S.op("pe", lambda e: e.matmul(out, lhsT=lhsT, rhs=rhs, start=start, stop=False,
                                          skip_group_check=True), reads, writes)

        def dve_copy(out, in_, reads, writes):
            S.op("dve", lambda e: e.tensor_copy(out=out, in_=in_), reads, writes)

        def dve_tt(out, in0, in1, op, reads, writes, eng="dve"):
            S.op(eng, lambda e: e.tensor_tensor(out=out, in0=in0, in1=in1, op=op), reads, writes)

        def dve_ts(out, in0, s1, s2, op0, op1, reads, writes, eng="dve"):
            if op1 is None:
                S.op(eng, lambda e: e.tensor_scalar(out=out, in0=in0, scalar1=s1, scalar2=None, op0=op0), reads, writes)
            else:
                S.op(eng, lambda e: e.tensor_scalar(out=out, in0=in0, scalar1=s1, scalar2=s2, op0=op0, op1=op1),
                     reads, writes)

        def dma(out, in_, dst, src, queue="sp", slow=False):
            if slow:
                S.dma(lambda e: e.dma_start(out=out, in_=in_, allow_slow_non_contiguous=True), dst, src, queue)
            else:
                S.dma(lambda e: e.dma_start(out=out, in_=in_), dst, src, queue)

        evac_flip = [0]

        def evac(out, in_, reads, writes):
            evac_flip[0] ^= 1
            if evac_flip[0]:
                act(out, in_, AF.Copy, reads, writes)
            else:
                dve_copy(out, in_, reads, writes)

        for i, (t, nm) in enumerate([(IDb16, "ID"), (ONESb, "ONES"), (PERMb, "PERM")]):
            dma(CST[:], cmat_ap[i], b["CST"], d_in)
            dve_copy(t[:], CST[:], [b["CST"]], [b[nm]])
        dma(JJ[:], cmat_ap[3], b["JJ"], d_in)
        dma(RTAB[:], rtab_ap, b["RTAB"], d_in)
        dma(GPA[:], dap(a_npost_d, 0, [[0, 128], [1, D]]), b["GPA"], d_in)
        dma(GPB[:], dap(b_npost_d, 0, [[0, 128], [1, D]]), b["GPB"], d_in)
        for i, t in enumerate([a_npre_d, kvn_d, b_npre_d]):
            for kc in range(8):
                dma(GN[:, i, kc:kc + 1], dap(t, kc * 128, [[1, 128], [1, 1]]), b["GN"], d_in)
        dma(SUBS[:], dap(subln_d, 0, [[1, 128], [1, 1]]), b["SUBS"], d_in)
        dma(LAMT[:], dap(lam_d, 0, [[0, 128], [64, 4], [1, 64]]), b["LAMT"], d_in)
        dma(RB16[:], a_rb_d.ap(), b["RB16"], d_in)
        dma(dap(chs, 0, [[1, 16], [1, 1]]), RB16[:, 256:257], d_w["chs"], b["RB16"])
        dma(CH[:], dap(chs, 0, [[0, 128], [1, 16]]), b["CH"], d_w["chs"])
        dve_ts(NCH[:], CH[:], -1.0, None, ALU.mult, None, [b["CH"]], [b["NCH"]])
        S.op("dve", lambda e: e.memset(EPSC[:, 0:1], EPS), [], [b["EPSC"]])
        S.op("dve", lambda e: e.memset(EPSC[:, 1:2], 0.0), [], [b["EPSC"]])
        S.op("dve", lambda e: e.memset(ZER[:], 0.0), [], [b["ZER"]])
        S.op("pool", lambda e: e.memset(VA[:], 1.0), [], [b["VA"]])
        dve_tt(LAMT[:, 0, :], LAMT[:, 0, :], LAMT[:, 1, :], ALU.mult, [b["LAMT"]], [b["LAMT"]])
        dve_tt(LAMT[:, 2, :], LAMT[:, 2, :], LAMT[:, 3, :], ALU.mult, [b["LAMT"]], [b["LAMT"]])
        S.op("dve", lambda e: e.tensor_reduce(out=LAMS[:, 0:1], in_=LAMT[:, 0, :], axis=AX.X, op=ALU.add),
             [b["LAMT"]], [b["LAMS"]])
        S.op("dve", lambda e: e.tensor_reduce(out=LAMS[:, 1:2], in_=LAMT[:, 2, :], axis=AX.X, op=ALU.add),
             [b["LAMT"]], [b["LAMS"]])
        act(LAMS[:, 4:6], LAMS[:, 0:2], AF.Exp, [b["LAMS"]], [b["LAMS"]])
        dve_tt(LAMS[:, 2:3], LAMS[:, 4:5], LAMS[:, 5:6], ALU.subtract, [b["LAMS"]], [b["LAMS"]])
        dve_ts(LAMS[:, 3:4], LAMS[:, 2:3], LAM_INIT, -1.0, ALU.add, ALU.mult, [b["LAMS"]], [b["LAMS"]])
        NEGLAM = LAMS[:, 3:4]
        dve_ts(SUBS[:], SUBS[:], 1.0 - LAM_INIT, None, ALU.mult, None, [b["SUBS"]], [b["SUBS"]])

        dbg_t = {}
        wlist = [(a_win_d, w1s, 4096, 0, "w1"), (a_wout_d, w2s, D, None, "w2"), (kvw_d, w3s, 2048, 1, "w3"),
                 (b_win_d, w4s, 2048, 2, "w4"), (b_wout_d, w5s, D, None, "w5")]
        conv_jobs = []
        conv_state = {"loaded": 0, "done": 0}
        CONV_LA = NCX - 1

        def conv_load(i):
            src, dst, ncol, gi, key, kc, c0 = conv_jobs[i]
            xi = i % NCX
            dma(CXL[xi][:], dap(src, kc * 128 * ncol + c0, [[ncol, 128], [1, D]]), CXLb[xi], d_in,
                queue="sp" if i % 2 == 0 else "pool")

        def conv_compute(i):
            src, dst, ncol, gi, key, kc, c0 = conv_jobs[i]
            xi = i % NCX
            ui = i % 3
            if gi is None:
                evac(UU[ui][:], CXL[xi][:], [CXLb[xi]], [UUb[ui]])
            elif i % 2 == 0:
                act(UU[ui][:], CXL[xi][:], AF.Copy, [CXLb[xi], b["GN"]], [UUb[ui]], scale=GN[:, gi, kc:kc + 1])
            else:
                dve_ts(UU[ui][:], CXL[xi][:], GN[:, gi, kc:kc + 1], None, ALU.mult, None,
                       [CXLb[xi], b["GN"]], [UUb[ui]])
            dma(dap(dst, kc * 128 * ncol + c0, [[ncol, 128], [1, D]]), UU[ui][:], d_w[key], UUb[ui],
                queue="pool" if i % 2 == 0 else "sp")
            _prune(d_w[key])

        def conv_step():
            i = conv_state["done"]
            while conv_state["loaded"] < min(len(conv_jobs), i + 1 + CONV_LA):
                conv_load(conv_state["loaded"])
                conv_state["loaded"] += 1
            conv_compute(i)
            conv_state["done"] += 1

        bg_work = []
        conv_done = {}

        def phase0b():
            for (src, dst, ncol, gi, key) in wlist:
                for c0 in range(0, ncol, D):
                    for kc in range(8):
                        conv_jobs.append((src, dst, ncol, gi, key, kc, c0))
                        bg_work.append((key, c0 // D))

        def bg_pop(n=1):
            for _ in range(n):
                if bg_work:
                    kk = bg_work.pop(0)
                    conv_step()
                    conv_done[kk] = conv_done.get(kk, 0) + 1

        def ensure_converted(key, blk):
            kk = (key, blk // 2)
            while conv_done.get(kk, 0) < 8:
                assert bg_work, kk
                bg_pop()

        if dbg_stage is None or dbg_stage >= 1:
            phase0b()
            bg_pop(16)
        dve_copy(T1[0:16, 0:257], RB16[:, :], [b["RB16"]], [b["T1"]])
        dve_ts(T1[0:16, 257:512], ZER[0:16, 0:255], RB16[:, 256:257], None, ALU.add, None,
               [b["ZER"], b["RB16"]], [b["T1"]])
        dma(bx_s.ap(), T1[0:16, :], d_w["bx"], b["T1"])
        for h in range(16):
            hk, hkb = RR[h % 2], RRb[h % 2]
            pb, pbb = PB[h % 2], PBb[h % 2]
            dma(hk[:, 0:256], dap(bx_s, h * 512 + 1, [[1, 128], [1, 256]]), hkb, d_w["bx"])
            S.op("pe", lambda e, pb=pb, hk=hk: e.matmul(pb[:, 0:256], lhsT=JJ[:], rhs=hk[:, 0:256], start=True, stop=True),
                 [b["JJ"], hkb], [pbb])
            act(EB[:, h, 0:256], pb[:, 0:256], AF.Exp, [pbb, b["NCH"]], [b["EB"]], bias=NCH[:, h:h + 1])
        S.op("dve", lambda e: e.memset(EB[64:128, :, 0:64], 0.0), [], [b["EB"]])

        wsrc = {"w1": (w1s, 4096), "w2": (w2s, D), "w3": (w3s, 2048), "w4": (w4s, 2048), "w5": (w5s, D)}
        group_blocks = ([("w1", c) for c in range(8)] + [("w2", c) for c in range(2)] + [("w3", c) for c in range(4)]
                        + [("w4", c) for c in range(4)] + [("w5", c) for c in range(2)])
        all_blocks = group_blocks * (NSEQ * NG)
        wstate = {"issued": 0, "used": 0}

        def wissue(upto):
            while wstate["issued"] <= upto and wstate["issued"] < len(all_blocks):
                n = wstate["issued"]
                key, c = all_blocks[n]
                ensure_converted(key, c)
                t, ncol = wsrc[key]
                slot = n % NW
                dma(WB[slot][:], dap(t, c * 512, [[ncol, 128], [128 * ncol, 8], [1, 512]]), WBb[slot], d_w[key])
                _prune(d_w[key])
                wstate["issued"] += 1

        def wnext(expect, live_prev=0):
            n = wstate["used"]
            assert all_blocks[n] == expect, (all_blocks[n], expect)
            wissue(n + NW - 1 - live_prev)
            wstate["used"] += 1
            return WB[n % NW], WBb[n % NW]

        pbr = [0]

        def next_pb(lo, hi):
            i = lo + pbr[0] % (hi - lo)
            pbr[0] += 1
            return PB[i], PBb[i]

        xtr = [0]

        def next_xtl():
            i = xtr[0] % NXT
            xtr[0] += 1
            return XTL[i], XTLb[i]

        ptr = [0]

        def next_pt():
            i = ptr[0] % NPT
            ptr[0] += 1
            return PT[i], PTb[i]

        def rstd_from_ss(ss_ap, n, out_ap, stb):
            act(out_ap, ss_ap, AF.Ln, [stb, b["EPSC"]], [stb], scale=1.0 / n, bias=EPSC[:, 0:1])
            act(out_ap, out_ap, AF.Exp, [stb], [stb], scale=-0.5)

        def norm_part(src_tile, src_buf, gi):
            ui = gi % 2
            st_, stb = STN[ui], STNb[ui]
            act(YN[ui][:], src_tile[:], AF.Square, [src_buf], [YNb[ui], stb], accum_out=st_[:, 0:1])
            rstd_from_ss(st_[:, 0:1], D, st_[:, 1:2], stb)
            act(UU[ui][:], src_tile[:], AF.Copy, [src_buf, stb], [UUb[ui]], scale=st_[:, 1:2])

        def tr_part(tcol, gi, XD, XDb):
            ui = gi % 2
            for kc in range(8):
                S.op("pe", lambda e, kc=kc: e.transpose(TRB[:, kc * 128:(kc + 1) * 128],
                                                        UU[ui][:, kc * 128:(kc + 1) * 128], IDb16[:]),
                     [UUb[ui], b["ID"]], [TRBb])
            evac(XD[:, :, tcol:tcol + 128], TRB[:].rearrange("p (k t) -> p k t", t=128), [TRBb], [XDb])

        def norm_transpose(src_tile, src_buf, tcol, gi, XD, XDb):
            norm_part(src_tile, src_buf, gi)
            tr_part(tcol, gi, XD, XDb)

        def proj_fm(wb, wbb, c, post, XS=None, XSb=None):
            if XS is None:
                XS, XSb = XT, b["XT"]
            pb, pbb = next_pb(0, 3)
            for kc in range(8):
                mm(pb[:, :], wb[:, kc, c * 128:(c + 1) * 128], XS[:, kc, :], kc == 0, [wbb, XSb], [pbb])
            post(pb, pbb)
            bg_pop()

        def outproj_norm_residual(w_key, gp_tile, gp_buf, s, g, layer, XS, XSb):
            w0, w0b = wnext((w_key, 0))
            w1_, w1b = wnext((w_key, 1), live_prev=1)
            xts = {}
            bankss = {}
            ob7 = [0]

            def mm_tile(t):
                row0 = g * GT + t * 128
                xt, xtb = next_xtl()
                if layer == 0:
                    dma(xt[:], x_ap[s, row0:row0 + 128, :], xtb, d_in)
                else:
                    dma(xt[:], out_ap[s, row0:row0 + 128, :], xtb, d_out[s][g][t])
                xts[t] = (xt, xtb)
                banks = []
                for hb, (w, wbuf) in enumerate([(w0, w0b), (w1_, w1b)]):
                    bi_ = ob7[0] % 6
                    ob7[0] += 1
                    pb, pbb = PB[bi_], PBb[bi_]
                    for fc in range(8):
                        mm(pb[:, :], XS[:, fc, t * 128:(t + 1) * 128], w[:, fc, :], fc == 0, [XSb, wbuf], [pbb])
                    banks.append((pb, pbb))
                bankss[t] = banks

            def post_tile(t):
                row0 = g * GT + t * 128
                xt, xtb = xts.pop(t)
                banks = bankss.pop(t)
                yi = t % 2
                so_, sob = STO[yi], STOb[yi]
                for hb, (pb, pbb) in enumerate(banks):
                    act(YN[yi][:, hb * 512:(hb + 1) * 512], pb[:, :], AF.Square, [pbb], [YNb[yi], sob],
                        accum_out=so_[:, 2 + hb:3 + hb])
                dve_tt(so_[:, 0:1], so_[:, 2:3], so_[:, 3:4], ALU.add, [sob], [sob])
                rstd_from_ss(so_[:, 0:1], D, so_[:, 1:2], sob)
                for hb, (pb, pbb) in enumerate(banks):
                    S.op("dve", lambda e, pb=pb, hb=hb, yi=yi, so_=so_: e.scalar_tensor_tensor(
                        out=YN[yi][:, hb * 512:(hb + 1) * 512], in0=pb[:, :], scalar=so_[:, 1:2],
                        in1=gp_tile[:, hb * 512:(hb + 1) * 512], op0=ALU.mult, op1=ALU.mult),
                        [pbb, sob, gp_buf], [YNb[yi]])
                dve_tt(xt[:], xt[:], YN[yi][:], ALU.add, [xtb, YNb[yi]], [xtb])
                dma(out_ap[s, row0:row0 + 128, :], xt[:], d_out[s][g][t], xtb, queue="pool")
                if layer == 0:
                    norm_part(xt, xtb, t)

            NT = DBG_TILES
            INFL = 2
            for t in range(min(INFL, NT)):
                mm_tile(t)
            for t in range(NT):
                post_tile(t)
                if t + INFL < NT:
                    mm_tile(t + INFL)
                if layer == 0:
                    tr_part(t * 128, t, X2T, b["X2T"])

        glist = [(s, g) for s in range(NSEQ) for g in range(NG)]
        if dbg_groups is not None:
            glist = glist[:dbg_groups]

        def run_all():
            stage(0)
            stage(1)
            pre(*glist[0])
            for gi_, (s, g) in enumerate(glist):
                do_group(s, g, glist[gi_ + 1] if gi_ + 1 < len(glist) else None)

        def pre(s, g):
            if True:
                tok0 = g * GT
                dma(POSI[:], dap(pos_d, s * SEQ + tok0, [[0, 128], [1, GT]]), b["POSI"], d_in)
                dve_copy(T0[:], POSI[:], [b["POSI"]], [b["T0"]])
                dve_ts(T0[:], T0[:], RTAB[:, 0:1], None, ALU.mult, None, [b["T0"], b["RTAB"]], [b["T0"]])
                TWO_PI = 2.0 * math.pi
                for (dst, dstb, shift) in ((SIN, b["SIN"], 0.0), (COS, b["COS"], math.pi / 2)):
                    dve_ts(T1[:], T0[:], shift, 1.0 / TWO_PI, ALU.add, ALU.mult, [b["T0"]], [b["T1"]])
                    dve_copy(POSI[:], T1[:], [b["T1"]], [b["POSI"]])
                    dve_copy(T1[:], POSI[:], [b["POSI"]], [b["T1"]])
                    dve_ts(T1[:], T1[:], -TWO_PI, shift, ALU.mult, ALU.add, [b["T1"]], [b["T1"]])
                    dve_tt(T1[:], T1[:], T0[:], ALU.add, [b["T1"], b["T0"]], [b["T1"]])
                    dve_ts(RS[:], T1[:], math.pi, TWO_PI, ALU.is_gt, ALU.mult, [b["T1"]], [b["RS"]])
                    dve_tt(T1[:], T1[:], RS[:], ALU.subtract, [b["T1"], b["RS"]], [b["T1"]])
                    dve_ts(RS[:], T1[:], -math.pi, TWO_PI, ALU.is_lt, ALU.mult, [b["T1"]], [b["RS"]])
                    dve_tt(T1[:], T1[:], RS[:], ALU.add, [b["T1"], b["RS"]], [b["T1"]])
                    act(dst[:], T1[:], AF.Sin, [b["T1"]], [dstb])
                dve_ts(SIN[:], SIN[:], RTAB[:, 1:2], None, ALU.mult, None, [b["SIN"], b["RTAB"]], [b["SIN"]])
                dve_ts(COS[:], COS[:], RTAB[:, 1:2], RTAB[:, 2:3], ALU.mult, ALU.add, [b["COS"], b["RTAB"]], [b["COS"]])

                for t in range(4):
                    xt, xtb = next_xtl()
                    dma(xt[:], x_ap[s, tok0 + t * 128: tok0 + (t + 1) * 128, :], xtb, d_in)
                    norm_transpose(xt, xtb, t * 128, t, XT, b["XT"])

        def do_group(s, g, nxt):
            if True:
                tok0 = g * GT
                kcol0 = (g % 2) * 512
                for blk in range(2):
                    wb, wbb = wnext(("w1", blk))
                    for c in range(4):
                        fc = blk * 4 + c
                        proj_fm(wb, wbb, c, lambda pb, pbb, fc=fc: evac(QT[:, fc, :], pb[:, :], [pbb], [b["QT"]]))
                for blk in range(2):
                    wb, wbb = wnext(("w1", 2 + blk))
                    for c in range(4):
                        fc = blk * 4 + c
                        proj_fm(wb, wbb, c, lambda pb, pbb, fc=fc: evac(KT[:, fc, kcol0:kcol0 + 512], pb[:, :],
                                                                        [pbb], [b["KT"]]))
                for blk in range(2):
                    wb, wbb = wnext(("w1", 4 + blk))
                    for t in range(4):
                        slot = (4 * g + t) % 8
                        pb, pbb = next_pb(0, 3)
                        for kc in range(8):
                            mm(pb[:, :], XT[:, kc, t * 128:(t + 1) * 128], wb[:, kc, :], kc == 0,
                               [b["XT"], wbb], [pbb])
                        src = pb[:, :].rearrange("p (i c) -> p i c", c=128)
                        dst = VA[:, slot, blk * 768:(blk + 1) * 768].rearrange("p (i c) -> p i c", c=192)
                        act(dst[:, :, 0:64], src[:, :, 0:64], AF.Copy, [pbb], [b["VA"]])
                        dve_copy(dst[:, :, 128:192], src[:, :, 64:128], [pbb], [b["VA"]])
                for blk in range(2):
                    wb, wbb = wnext(("w1", 6 + blk))
                    for c in range(4):
                        fc = blk * 4 + c
                        proj_fm(wb, wbb, c, lambda pb, pbb, fc=fc: act(SGT[:, fc, :], pb[:, :], AF.Silu,
                                                                       [pbb], [b["SGT"]]))

                stage(2)
                items = []
                tl_ = []
                for Tk in range(max(0, 4 * g - 4), 4 * g + 4):
                    qlo = max(Tk, 4 * g)
                    qhi = min(Tk + 4, 4 * g + 3)
                    tl_.append((Tk, (qlo - 4 * g) * 128, (qhi + 1 - 4 * g) * 128, qlo - Tk))
                for i_ in range(8):
                    for n, (Tk, c0, c1, jlo) in enumerate(tl_):
                        for h in (2 * i_, 2 * i_ + 1):
                            items.append((h, n, Tk, c0, c1, jlo, n == len(tl_) - 1))
                DEPTH_A = 4
                sbank = {}
                sA = [0]

                def qkA(idx):
                    h, n, Tk, c0, c1, jlo, last = items[idx]
                    i, r0 = h // 2, 64 * (h % 2)
                    bi_ = (0, 1, 2, 3, 6)[sA[0] % 5]
                    sA[0] += 1
                    pb, pbb = PB[bi_], PBb[bi_]
                    kc0 = (Tk % 8) * 128
                    mm(pb[:, c0:c1], KT[r0:r0 + 64, i, kc0:kc0 + 128], QT[r0:r0 + 64, i, c0:c1], True,
                       [b["KT"], b["QT"]], [pbb])
                    sbank[idx] = (pb, pbb)

                def finA(idx):
                    h, n, Tk, c0, c1, jlo, last = items[idx]
                    i, half = h // 2, h % 2
                    r0 = 64 * half
                    ob, obb = PB[4 + h % 2], PBb[4 + h % 2]
                    pb, pbb = sbank.pop(idx)
                    pt, ptb = next_pt()
                    act(pt[:, c0:c1], pb[:, c0:c1], AF.Exp, [pbb], [ptb], scale=0.125)
                    jhi = jlo + (c1 - c0) // 128 - 1
                    if jlo <= 1:
                        nb_ = (min(jhi, 1) - jlo + 1) * 128
                        dve_tt(pt[:, c0:c0 + nb_], pt[:, c0:c0 + nb_], EB[:, h, jlo * 128: jlo * 128 + nb_],
                               ALU.mult, [ptb, b["EB"]], [ptb])
                    if jhi == 4:
                        S.op("dve", lambda e, pt=pt, c1=c1: e.memset(pt[0:64, c1 - 64:c1], 0.0), [], [ptb])
                    vcol = i * 192 + 64 * half
                    mm(ob[:, c0:c1], VA[:, Tk % 8, vcol:vcol + 128], pt[:, c0:c1], n == 0, [b["VA"], ptb], [obb])
                    if last:
                        so = 64 - r0
                        rr, rrb = RR[h % 2], RRb[h % 2]
                        act(rr[so:so + 64, :], ob[so:so + 64, :], AF.Ln, [obb], [rrb])
                        act(rr[so:so + 64, :], rr[so:so + 64, :], AF.Exp, [rrb], [rrb], scale=-1.0)
                        ta, tab_ = TA[h % 2], TAb[h % 2]
                        dve_tt(ta[r0:r0 + 64, :], ob[r0:r0 + 64, :], rr[so:so + 64, :], ALU.mult, [obb, rrb], [tab_])
                        dve_tt(XT[r0:r0 + 64, i, :], ta[r0:r0 + 64, :], SGT[r0:r0 + 64, i, :], ALU.mult,
                               [tab_, b["SGT"]], [b["XT"]])
                        bg_pop(3)

                for idx in range(0, len(items) + DEPTH_A, 2):
                    for k_ in (idx - DEPTH_A, idx + 1 - DEPTH_A):
                        if 0 <= k_ < len(items):
                            finA(k_)
                    for k_ in (idx, idx + 1):
                        if k_ < len(items):
                            qkA(k_)

                stage(3)
                outproj_norm_residual("w2", GPA, b["GPA"], s, g, 0, XT, b["XT"])

                stage(4)
                rp = [0]

                def rope_post(pb, pbb, dst_ap, dst_buf):
                    k = rp[0] % 2
                    rp[0] += 1
                    kr, krb, ta, tab_, tb, tbb = KRAW[k], KRAWb[k], TA[k], TAb[k], TBB[k], TBBb[k]
                    act(kr[:], pb[:, :], AF.Copy, [pbb], [krb])
                    p2, p2b = PB[3 + k], PBb[3 + k]
                    mm(p2[:, :], PERMb[:], kr[:], True, [b["PERM"], krb], [p2b])
                    dve_tt(ta[:], pb[:, :], COS[:], ALU.mult, [pbb, b["COS"]], [tab_])
                    dve_tt(tb[:], p2[:, :], SIN[:], ALU.mult, [p2b, b["SIN"]], [tbb])
                    dve_tt(dst_ap, ta[:], tb[:], ALU.add, [tab_, tbb], [dst_buf])

                for blk in range(2):
                    wb, wbb = wnext(("w3", blk))
                    for c in range(4):
                        hh = blk * 4 + c
                        ks, ksb = KST[hh % 2], KSTb[hh % 2]
                        proj_fm(wb, wbb, c, lambda pb, pbb, ks=ks, ksb=ksb: rope_post(pb, pbb, ks[:], ksb),
                                X2T, b["X2T"])
                        dma(dap(ksh_s, ((s * 8 + hh) * 128) * SEQ + tok0, [[SEQ, 128], [1, GT]]), ks[:],
                            d_ksh[s][g], ksb, queue="pool")
                        _prune(d_ksh[s][g])
                for t in range(4):
                    pass
                vblocks = [wnext(("w3", 2)), wnext(("w3", 3), live_prev=1)]
                for t in range(4):
                    vs, vsb = VST[t % 2], VSTb[t % 2]
                    for blk in range(2):
                        wb, wbb = vblocks[blk]
                        pb, pbb = next_pb(0, 3)
                        for kc in range(8):
                            mm(pb[:, :], X2T[:, kc, t * 128:(t + 1) * 128], wb[:, kc, :], kc == 0,
                               [b["X2T"], wbb], [pbb])
                        evac(vs[:, blk * 512:(blk + 1) * 512], pb[:, :], [pbb], [vsb])
                    dma(dap(vsh_s, (s * 8 * SEQ + tok0 + t * 128) * 128, [[128, 128], [SEQ * 128, 8], [1, 128]]),
                        vs[:].rearrange("p (h v) -> p h v", v=128), d_vsh[s][g], vsb, queue="pool")
                    _prune(d_vsh[s][g])

                for blk in range(2):
                    wb, wbb = wnext(("w4", blk))
                    for c in range(4):
                        hh = blk * 4 + c
                        proj_fm(wb, wbb, c, lambda pb, pbb, hh=hh: rope_post(pb, pbb, QT[:, hh, :], b["QT"]),
                                X2T, b["X2T"])
                for blk in range(2):
                    wb, wbb = wnext(("w4", 2 + blk))
                    for c in range(4):
                        hh = blk * 4 + c

                        def gpost(pb, pbb, hh=hh):
                            act(T0[:], pb[:, :], AF.Silu, [pbb], [b["T0"]])
                            dve_ts(SGT[:, hh, :], T0[:], SUBS[:, 0:1], None, ALU.mult, None,
                                   [b["T0"], b["SUBS"]], [b["SGT"]])
                        proj_fm(wb, wbb, c, gpost, X2T, b["X2T"])

                stage(5)
                if nxt is not None:
                    pre(*nxt)
                ntile = 4 * g + 4
                O0, O0b, O1, O1b = PB[3], PBb[3], PB[4], PBb[4]
                Z0, Z0b, Z1, Z1b = PB[5], PBb[5], PB[6], PBb[6]
                itemsB = [(hh, n) for hh in range(8) for n in range(ntile)]
                kvslot = {}
                sbankB = {}
                pend_d1 = [None]
                pend_d2 = [None]
                ssb = [None]

                def loadB(hh):
                    kb, kbb = KB[hh % 2], KBb[hh % 2]
                    vb, vbb = VB[hh % 2], VBb[hh % 2]
                    for gg in range(g + 1):
                        dma(kb[:, gg * GT:(gg + 1) * GT],
                            dap(ksh_s, ((s * 8 + hh) * 128) * SEQ + gg * GT, [[SEQ, 128], [1, GT]]),
                            kbb, d_ksh[s][gg])
                        dma(vb[:, gg * 4:(gg + 1) * 4, :],
                            dap(vsh_s, ((s * 8 + hh) * SEQ + gg * GT) * 128, [[128, 128], [128 * 128, 4], [1, 128]]),
                            vbb, d_vsh[s][gg])
                    kvslot[hh] = (kb, kbb, vb, vbb)

                def qkB(idx):
                    hh, n = itemsB[idx]
                    if n == 0:
                        loadB(hh)
                    kb, kbb, vb, vbb = kvslot[hh]
                    c0 = max(0, n - 4 * g) * 128
                    res = []
                    for m in range(2):
                        pb, pbb = next_pb(0, 3)
                        mm(pb[:, c0:GT], kb[64 * m:64 * m + 64, n * 128:(n + 1) * 128],
                           QT[64 * m:64 * m + 64, hh, c0:GT], True, [kbb, b["QT"]], [pbb])
                        res.append((pb, pbb))
                    sbankB[idx] = res

                def expB(idx):
                    hh, n = itemsB[idx]
                    c0 = max(0, n - 4 * g) * 128
                    res = sbankB.pop(idx)
                    pts = []
                    for m in range(2):
                        pb, pbb = res[m]
                        pt, ptb = next_pt()
                        act(pt[:, c0:GT], pb[:, c0:GT], AF.Exp, [pbb], [ptb], scale=0.125)
                        if n >= 4 * g:
                            S.op("dve", lambda e, pt=pt, c0=c0: e.memset(pt[64:128, c0:c0 + 64], 0.0), [], [ptb])
                        pts.append((pt, ptb))
                    return pts

                def pvB(idx, pts):
                    hh, n = itemsB[idx]
                    kb, kbb, vb, vbb = kvslot[hh]
                    c0 = max(0, n - 4 * g) * 128
                    for m, (O, Ob, Z, Zb) in enumerate([(O0, O0b, Z0, Z0b), (O1, O1b, Z1, Z1b)]):
                        pt, ptb = pts[m]
                        mm(O[:, c0:GT], vb[:, n, :], pt[:, c0:GT], n == 0, [vbb, ptb], [Ob])
                        mm(Z[:, c0:GT], ONESb[:], pt[:, c0:GT], n == 0, [b["ONES"], ptb], [Zb])
                    if n == 1 and pend_d1[0] is not None:
                        pend_d1[0]()
                        pend_d1[0] = None
                    if n == 3 and pend_d2[0] is not None:
                        pend_d2[0]()
                        pend_d2[0] = None
                    if n == ntile - 1:
                        act(RR[0][:], Z0[:, :], AF.Ln, [Z0b], [RRb[0]])
                        act(RR[1][:], Z1[:, :], AF.Ln, [Z1b], [RRb[1]])
                        act(RR[0][:], RR[0][:], AF.Exp, [RRb[0]], [RRb[0]], scale=-1.0)
                        act(RR[1][:], RR[1][:], AF.Exp, [RRb[1]], [RRb[1]], scale=-1.0)
                        dve_tt(T0[:], O0[:, :], RR[0][:], ALU.mult, [O0b, RRb[0]], [b["T0"]])
                        dve_tt(T1[:], O1[:, :], RR[1][:], ALU.mult, [O1b, RRb[1]], [b["T1"]])
                        S.op("dve", lambda e: e.scalar_tensor_tensor(out=T0[:], in0=T1[:], scalar=NEGLAM, in1=T0[:],
                                                                     op0=ALU.mult, op1=ALU.add),
                             [b["T0"], b["T1"], b["LAMS"]], [b["T0"]])

                        def d1():
                            act(SQ[:], T0[:], AF.Square, [b["T0"]], [b["SQ"]])
                            pb, pbb = next_pb(0, 3)
                            mm(pb[:, :], ONESb[:], SQ[:], True, [b["ONES"], b["SQ"]], [pbb])
                            dve_copy(RS[:], pb[:, :], [pbb], [b["RS"]])

                        def d2(hh=hh):
                            act(RS[:], RS[:], AF.Ln, [b["RS"], b["EPSC"]], [b["RS"]], scale=1.0 / 128, bias=EPSC[:, 0:1])
                            act(RS[:], RS[:], AF.Exp, [b["RS"]], [b["RS"]], scale=-0.5)
                            dve_tt(T0[:], T0[:], RS[:], ALU.mult, [b["T0"], b["RS"]], [b["T0"]])
                            dve_tt(X2T[:, hh, :], T0[:], SGT[:, hh, :], ALU.mult, [b["T0"], b["SGT"]], [b["X2T"]])
                        pend_d1[0] = d1
                        pend_d2[0] = d2

                qkB(0)
                for idx in range(len(itemsB)):
                    pts = expB(idx)
                    if idx + 1 < len(itemsB):
                        qkB(idx + 1)
                    pvB(idx, pts)
                for pd_ in (pend_d1, pend_d2):
                    if pd_[0] is not None:
                        pd_[0]()
                        pd_[0] = None

                stage(6)
                outproj_norm_residual("w5", GPB, b["GPB"], s, g, 1, X2T, b["X2T"])

        try:
            run_all()
        except _Stop:
            pass
        outs = [d_out[s][g][t] for s in range(NSEQ) for g in range(NG) for t in range(4)]
        names = {"EB": (EB, [128, 16 * 256], BF16), "CH": (CH, [128, 16], F32), "LAMS": (LAMS, [128, 8], F32),
                 "XT": (XT, [128, 8 * GT], BF16), "X2T": (X2T, [128, 8 * GT], BF16), "QT": (QT, [128, 8 * GT], BF16),
                 "SGT": (SGT, [128, 8 * GT], BF16), "KT": (KT, [128, 8 * 1024], BF16), "VA": (VA, [128, 8 * 1536], BF16),
                 "COS": (COS, [128, GT], F32), "SIN": (SIN, [128, GT], F32), "GN": (GN, [128, 24], F32),
                 "PERM": (PERMb, [128, 128], BF16), "ST": (ST, [128, 16], F32), "YN0": (YN[0], [128, D], F32),
                 "YN1": (YN[1], [128, D], F32)}
        b["YN0"] = YNb[0]; b["YN1"] = YNb[1]
        for nm in dbg_dump:
            tl, shp, dt = names[nm]
            dd = nc.dram_tensor("dbg_" + nm, shp, dt, kind="ExternalOutput")
            db_ = Buf("dbg_" + nm)
            src = tl[:] if len(tl.shape) == 2 else tl[:].rearrange("p a b -> p (a b)")
            dma(dd.ap(), src, db_, b.get(nm, b.get("PERM")), queue="pool")
            outs.append(db_)
        S.final_wait("pool", outs)
        print("sbuf bytes remaining/partition:", nc.sbuf_bytes_remaining, flush=True)
        S.emit(st)
        print("instr counts:", {e: len(S.ops[e]) for e in ENGS}, "dma sems:", S.ndsem, flush=True)
    return nc


_CACHE = {}


def _consts():
    ident = np.eye(128, dtype=np.float32)
    ones = np.ones((128, 128), np.float32)
    perm = np.zeros((128, 128), np.float32)
    for base in (0, 64):
        for d in range(8):
            perm[base + d + 8, base + d] = -1.0
            perm[base + d, base + d + 8] = 1.0
    anti = np.ascontiguousarray(np.eye(128, dtype=np.float32)[::-1])
    cmat = np.stack([ident, ones, perm, anti]).astype(np.float32)
    rtab = np.zeros((128, 4), np.float32)
    inv = np.power(np.float32(ROPE_THETA), -np.arange(8, dtype=np.float32) * np.float32(2.0) / np.float32(16))
    for p in range(128):
        d = p % 64
        if d < 16:
            rtab[p, 0] = inv[d % 8]
            rtab[p, 1] = 1.0
        else:
            rtab[p, 2] = 1.0
        rtab[p, 3] = math.pi
    return cmat, rtab.astype(np.float32)


def kernel(x, positions, a_norm_pre, a_w_in, a_rel_bias, a_w_out, a_norm_post, kv_norm, kv_w,
           b_norm_pre, b_w_in, b_lambda_q1, b_lambda_k1, b_lambda_q2, b_lambda_k2, b_subln,
           b_w_out, b_norm_post):
    f = lambda a: np.ascontiguousarray(np.asarray(a, dtype=np.float32))
    x = f(x)
    positions = np.ascontiguousarray(np.asarray(positions, dtype=np.int32))
    cmat, rtab = _consts()
    shared = {
        "a_norm_pre": f(a_norm_pre).reshape(D), "a_w_in": f(a_w_in).reshape(D, 4096),
        "a_rel_bias": f(a_rel_bias).reshape(16, 257), "a_w_out": f(a_w_out).reshape(D, D),
        "a_norm_post": f(a_norm_post).reshape(D), "kv_norm": f(kv_norm).reshape(D),
        "kv_w": f(kv_w).reshape(D, 2048), "b_norm_pre": f(b_norm_pre).reshape(D),
        "b_w_in": f(b_w_in).reshape(D, 2048),
        "b_lam": np.stack([f(b_lambda_q1).reshape(64), f(b_lambda_k1).reshape(64),
                           f(b_lambda_q2).reshape(64), f(b_lambda_k2).reshape(64)]),
        "b_subln": f(b_subln).reshape(128), "b_w_out": f(b_w_out).reshape(D, D),
        "b_norm_post": f(b_norm_post).reshape(D), "cmat": cmat, "rtab": rtab,
    }
    if "nc" not in _CACHE:
        _CACHE["nc"] = build_program()
    nc = _CACHE["nc"]
    in_maps = []
    for c in range(NCORES):
        m = dict(shared)
        m["x"] = x[c * NSEQ:(c + 1) * NSEQ]
        m["pos"] = positions[c * NSEQ:(c + 1) * NSEQ]
        in_maps.append(m)
    res = run_bass_kernel_spmd(nc, in_maps, core_ids=list(range(NCORES)))
    return np.concatenate([np.asarray(r["out"]).reshape(NSEQ, SEQ, D) for r in res.results], axis=0).astype(np.float32)
```

```python
import math
from contextlib import ExitStack
import numpy as np
import concourse.bass as bass
import concourse.mybir as mybir
from concourse.bass_utils import run_bass_kernel_spmd

F32 = mybir.dt.float32
BF16 = mybir.dt.bfloat16
I32 = mybir.dt.int32
AF = mybir.ActivationFunctionType
ALU = mybir.AluOpType
AX = mybir.AxisListType

NCORES = 8
SEQ = 2048
D = 1024
NSEQ = 2
GT = 512
NG = SEQ // GT
EPS = 1e-6
LAM_INIT = 0.8 - 0.6 * math.exp(-0.3 * 1)
ROPE_THETA = 500000.0

ENGS = ("pe", "act", "dve", "pool", "sp")
DBG_TILES = 4


class Buf:
    __slots__ = ("name", "w", "r", "dsem", "excl")

    def __init__(self, name, excl=False):
        self.name = name
        self.w = None
        self.r = []
        self.dsem = None
        self.excl = excl


class Sched:
    def __init__(self, nc):
        self.nc = nc
        self.ops = {e: [] for e in ENGS}
        self.cnt = {e: 0 for e in ENGS}
        self.seen = {e: {} for e in ENGS}
        self.ndsem = 0
        self.dcnt = {}

    def _waits(self, eng, reads, writes):
        waits = {}

        def need(t):
            if t is None:
                return
            k, n = t
            if eng == "pe" and k == "pe":
                return
            if n > self.seen[eng].get(k, 0) and n > waits.get(k, 0):
                waits[k] = n

        for b in reads:
            need(b.w)
            if b.excl:
                for t in b.r:
                    if t[0] != eng:
                        need(t)
        for b in writes:
            need(b.w)
            for t in b.r:
                need(t)
        for k, n in waits.items():
            self.seen[eng][k] = n
        return list(waits.items())

    def op(self, eng, fn, reads=(), writes=()):
        waits = self._waits(eng, reads, writes)
        self.cnt[eng] += 1
        tick = (eng, self.cnt[eng])
        for b in reads:
            if len(b.r) > 64:
                _prune(b)
            b.r.append(tick)
        for b in writes:
            b.w = tick
            b.r = []
        self.ops[eng].append((fn, waits, tick))
        return tick

    def dma(self, fn, dst, src, queue="sp"):
        waits = self._waits(queue, [src], [dst])
        if dst.dsem is None:
            dst.dsem = "q%d" % self.ndsem
            self.ndsem += 1
            self.dcnt[dst.dsem] = 0
        self.dcnt[dst.dsem] += 16
        tick = (dst.dsem, self.dcnt[dst.dsem])
        if len(src.r) > 64:
            _prune(src)
        src.r.append(tick)
        dst.w = tick
        dst.r = []
        self.ops[queue].append((fn, waits, tick))
        return tick

    def final_wait(self, eng, bufs):
        waits = dict(self._waits(eng, [], bufs))
        for k, n in self.dcnt.items():
            if n > self.seen[eng].get(k, 0):
                waits[k] = n
                self.seen[eng][k] = n
        self.ops[eng].append((None, list(waits.items()), None))

    def emit(self, stack):
        nc = self.nc
        sems = {}
        for e in ENGS:
            sems[e] = stack.enter_context(nc.semaphore("s_" + e))
        for k in self.dcnt:
            sems[k] = stack.enter_context(nc.semaphore("s_" + k))
        block = stack.enter_context(nc.Block())

        def run(ename):
            def body(eng):
                for fn, waits, tick in self.ops[ename]:
                    for k, n in waits:
                        eng.wait_ge(sems[k], n)
                    if fn is None:
                        continue
                    ins = fn(eng)
                    k, n = tick
                    ins.then_inc(sems[k], 16 if k.startswith("q") else 1)
            return body

        block.tensor(run("pe"))
        block.scalar(run("act"))
        block.vector(run("dve"))
        block.gpsimd(run("pool"))
        block.sync(run("sp"))


def _prune(b):
    best = {}
    for k, n in b.r:
        if n > best.get(k, 0):
            best[k] = n
    b.r = list(best.items())


class _Stop(Exception):
    pass


def build_program(dbg_stage=None, dbg_groups=None, dbg_dump=()):
    def stage(k):
        if dbg_stage is not None and dbg_stage == k:
            raise _Stop()

    nc = bass.Bass("TRN2", target_bir_lowering=False)

    def din(name, shape, dt=F32):
        return nc.dram_tensor(name, list(shape), dt, kind="ExternalInput")

    x_d = din("x", [NSEQ, SEQ, D])
    pos_d = din("pos", [NSEQ, SEQ], I32)
    a_npre_d = din("a_norm_pre", [D])
    a_win_d = din("a_w_in", [D, 4096])
    a_rb_d = din("a_rel_bias", [16, 257])
    a_wout_d = din("a_w_out", [D, D])
    a_npost_d = din("a_norm_post", [D])
    kvn_d = din("kv_norm", [D])
    kvw_d = din("kv_w", [D, 2048])
    b_npre_d = din("b_norm_pre", [D])
    b_win_d = din("b_w_in", [D, 2048])
    lam_d = din("b_lam", [4, 64])
    subln_d = din("b_subln", [128])
    b_wout_d = din("b_w_out", [D, D])
    b_npost_d = din("b_norm_post", [D])
    cmat_d = din("cmat", [4, 128, 128])
    rtab_d = din("rtab", [128, 4])
    out_d = nc.dram_tensor("out", [NSEQ, SEQ, D], F32, kind="ExternalOutput")

    w1s = nc.dram_tensor("w1s", [D, 4096], BF16)
    w2s = nc.dram_tensor("w2s", [D, D], BF16)
    w3s = nc.dram_tensor("w3s", [D, 2048], BF16)
    w4s = nc.dram_tensor("w4s", [D, 2048], BF16)
    w5s = nc.dram_tensor("w5s", [D, D], BF16)
    bx_s = nc.dram_tensor("bx_s", [16, 512], F32)
    chs = nc.dram_tensor("chs", [16], F32)
    ksh_s = nc.dram_tensor("ksh_s", [NSEQ, 8, 128, SEQ], BF16)
    vsh_s = nc.dram_tensor("vsh_s", [NSEQ, 8, SEQ, 128], BF16)

    x_ap = x_d.ap(); out_ap = out_d.ap(); cmat_ap = cmat_d.ap(); rtab_ap = rtab_d.ap()

    def dap(t, offset, ap):
        return bass.AP(tensor=t, offset=offset, ap=[list(a) for a in ap])

    S = Sched(nc)
    with ExitStack() as st:
        def sb(name, shape, dt):
            return st.enter_context(nc.sbuf_tensor(name, list(shape), dt))

        def ps(name, shape, dt=F32):
            return st.enter_context(nc.psum_tensor(name, list(shape), dt))

        IDb16 = sb("IDb16", [128, 128], BF16); ONESb = sb("ONESb", [128, 128], BF16)
        PERMb = sb("PERMb", [128, 128], BF16); JJ = sb("JJ", [128, 128], F32)
        CST = sb("CST", [128, 128], F32)
        RTAB = sb("RTAB", [128, 4], F32)
        EB = sb("EB", [128, 16, 256], BF16)
        CH = sb("CH", [128, 16], F32)
        RB16 = sb("RB16", [16, 257], F32)
        GPA = sb("GPA", [128, D], F32); GPB = sb("GPB", [128, D], F32)
        GN = sb("GN", [128, 3, 8], F32)
        SUBS = sb("SUBS", [128, 1], F32)
        LAMT = sb("LAMT", [128, 4, 64], F32)
        LAMS = sb("LAMS", [128, 8], F32)
        EPSC = sb("EPSC", [128, 2], F32)
        ZER = sb("ZER", [128, 256], F32)
        ST = sb("ST", [128, 16], F32)
        STN = [sb("STN%d" % i, [128, 4], F32) for i in range(3)]
        STO = [sb("STO%d" % i, [128, 4], F32) for i in range(2)]
        NCH = sb("NCH", [128, 16], F32)
        NW = 3
        WB = [sb("WB%d" % i, [128, 8, 512], BF16) for i in range(NW)]
        NXT = 2
        NCX = 3
        XTL = [sb("XTL%d" % i, [128, D], F32) for i in range(NXT)]
        CXL = [sb("CXL%d" % i, [128, D], F32) for i in range(NCX)]
        YN = [sb("YN%d" % i, [128, D], F32) for i in range(2)]
        UU = [sb("UU%d" % i, [128, D], BF16) for i in range(3)]
        XT = sb("XT", [128, 8, GT], BF16)
        X2T = sb("X2T", [128, 8, GT], BF16)
        QT = sb("QT", [128, 8, GT], BF16)
        SGT = sb("SGT", [128, 8, GT], BF16)
        KT = sb("KT", [128, 8, 1024], BF16)
        VA = sb("VA", [128, 8, 1536], BF16)
        NPT = 4
        PT = [sb("PT%d" % i, [128, GT], BF16) for i in range(NPT)]
        RR = [sb("RR%d" % i, [128, GT], F32) for i in range(2)]
        T0 = sb("T0", [128, GT], F32); T1 = sb("T1", [128, GT], F32)
        TA = [sb("TA%d" % i, [128, GT], F32) for i in range(2)]
        TBB = [sb("TBB%d" % i, [128, GT], F32) for i in range(2)]
        SQ = sb("SQ", [128, GT], BF16); RS = sb("RS", [128, GT], F32)
        KRAW = [sb("KRAW%d" % i, [128, GT], BF16) for i in range(2)]
        COS = sb("COS", [128, GT], F32); SIN = sb("SIN", [128, GT], F32)
        POSI = sb("POSI", [128, GT], I32)
        KB = [sb("KB%d" % i, [128, SEQ], BF16) for i in range(2)]
        VB = [sb("VB%d" % i, [128, 16, 128], BF16) for i in range(2)]
        KST = [sb("KST%d" % i, [128, GT], BF16) for i in range(2)]
        VST = [sb("VST%d" % i, [128, D], BF16) for i in range(2)]

        PB = [ps("PB%d" % i, [128, 512], F32) for i in range(7)]
        TRB = ps("TRB", [128, 1024], BF16)

        b = {}
        def B(name, excl=False):
            b[name] = Buf(name, excl)
            return b[name]
        for nm in ["ID", "ONES", "PERM", "JJ", "CST", "RTAB", "EB", "CH", "GPA", "GPB", "GN", "SUBS", "LAMT",
                   "LAMS", "EPSC", "ZER", "ST", "XT", "X2T", "RB16", "QT", "SGT", "KT", "VA", "T0", "T1", "SQ", "RS",
                   "COS", "SIN", "POSI"]:
            B(nm)
        WBb = [Buf("WB%d" % i) for i in range(NW)]
        STNb = [Buf("STN%d" % i) for i in range(3)]
        STOb = [Buf("STO%d" % i) for i in range(2)]
        TAb = [Buf("TA%d" % i) for i in range(2)]
        TBBb = [Buf("TBB%d" % i) for i in range(2)]
        KRAWb = [Buf("KRAW%d" % i) for i in range(2)]
        B("NCH")
        XTLb = [Buf("XTL%d" % i) for i in range(NXT)]
        CXLb = [Buf("CXL%d" % i) for i in range(NCX)]
        YNb = [Buf("YN%d" % i) for i in range(2)]
        UUb = [Buf("UU%d" % i) for i in range(3)]
        PTb = [Buf("PT%d" % i) for i in range(NPT)]
        RRb = [Buf("RR%d" % i) for i in range(2)]
        KBb = [Buf("KB%d" % i) for i in range(2)]
        VBb = [Buf("VB%d" % i) for i in range(2)]
        KSTb = [Buf("KST%d" % i) for i in range(2)]
        VSTb = [Buf("VST%d" % i) for i in range(2)]
        PBb = [Buf("PB%d" % i, True) for i in range(7)]
        TRBb = Buf("TRB", True)
        d_in = Buf("d_in")
        d_w = {k: Buf("d_" + k) for k in ["w1", "w2", "w3", "w4", "w5", "bx", "chs"]}
        d_ksh = [[Buf("d_ksh%d_%d" % (s, g)) for g in range(NG)] for s in range(NSEQ)]
        d_vsh = [[Buf("d_vsh%d_%d" % (s, g)) for g in range(NG)] for s in range(NSEQ)]
        d_out = [[[Buf("d_out%d_%d_%d" % (s, g, t)) for t in range(4)] for g in range(NG)] for s in range(NSEQ)]

        def act(out, in_, func, reads, writes, **kw):
            S.op("act", lambda e: e.activation(out=out, in_=in_, func=func, **kw), reads, writes)

        def mm(out, lhsT, rhs, start, reads, writes):
            S.op("pe", lambda e: e.matmul(out, lhsT=lhsT, rhs=rhs, start=start, stop=False,
                                          skip_group_check=True), reads, writes)

        def dve_copy(out, in_, reads, writes):
            S.op("dve", lambda e: e.tensor_copy(out=out, in_=in_), reads, writes)

        def dve_tt(out, in0, in1, op, reads, writes, eng="dve"):
            S.op(eng, lambda e: e.tensor_tensor(out=out, in0=in0, in1=in1, op=op), reads, writes)

        def dve_ts(out, in0, s1, s2, op0, op1, reads, writes, eng="dve"):
            if op1 is None:
                S.op(eng, lambda e: e.tensor_scalar(out=out, in0=in0, scalar1=s1, scalar2=None, op0=op0), reads, writes)
            else:
                S.op(eng, lambda e: e.tensor_scalar(out=out, in0=in0, scalar1=s1, scalar2=s2, op0=op0, op1=op1),
                     reads, writes)

        def dma(out, in_, dst, src, queue="sp", slow=False):
            if slow:
                S.dma(lambda e: e.dma_start(out=out, in_=in_, allow_slow_non_contiguous=True), dst, src, queue)
            else:
                S.dma(lambda e: e.dma_start(out=out, in_=in_), dst, src, queue)

        evac_flip = [0]

        def evac(out, in_, reads, writes):
            evac_flip[0] ^= 1
            if evac_flip[0]:
                act(out, in_, AF.Copy, reads, writes)
            else:
                dve_copy(out, in_, reads, writes)

        for i, (t, nm) in enumerate([(IDb16, "ID"), (ONESb, "ONES"), (PERMb, "PERM")]):
            dma(CST[:], cmat_ap[i], b["CST"], d_in)
            dve_copy(t[:], CST[:], [b["CST"]], [b[nm]])
        dma(JJ[:], cmat_ap[3], b["JJ"], d_in)
        dma(RTAB[:], rtab_ap, b["RTAB"], d_in)
        dma(GPA[:], dap(a_npost_d, 0, [[0, 128], [1, D]]), b["GPA"], d_in)
        dma(GPB[:], dap(b_npost_d, 0, [[0, 128], [1, D]]), b["GPB"], d_in)
        for i, t in enumerate([a_npre_d, kvn_d, b_npre_d]):
            for kc in range(8):
                dma(GN[:, i, kc:kc + 1], dap(t, kc * 128, [[1, 128], [1, 1]]), b["GN"], d_in)
        dma(SUBS[:], dap(subln_d, 0, [[1, 128], [1, 1]]), b["SUBS"], d_in)
        dma(LAMT[:], dap(lam_d, 0, [[0, 128], [64, 4], [1, 64]]), b["LAMT"], d_in)
        dma(RB16[:], a_rb_d.ap(), b["RB16"], d_in)
        dma(dap(chs, 0, [[1, 16], [1, 1]]), RB16[:, 256:257], d_w["chs"], b["RB16"])
        dma(CH[:], dap(chs, 0, [[0, 128], [1, 16]]), b["CH"], d_w["chs"])
        dve_ts(NCH[:], CH[:], -1.0, None, ALU.mult, None, [b["CH"]], [b["NCH"]])
        S.op("dve", lambda e: e.memset(EPSC[:, 0:1], EPS), [], [b["EPSC"]])
        S.op("dve", lambda e: e.memset(EPSC[:, 1:2], 0.0), [], [b["EPSC"]])
        S.op("dve", lambda e: e.memset(ZER[:], 0.0), [], [b["ZER"]])
        S.op("pool", lambda e: e.memset(VA[:], 1.0), [], [b["VA"]])
        dve_tt(LAMT[:, 0, :], LAMT[:, 0, :], LAMT[:, 1, :], ALU.mult, [b["LAMT"]], [b["LAMT"]])
        dve_tt(LAMT[:, 2, :], LAMT[:, 2, :], LAMT[:, 3, :], ALU.mult, [b["LAMT"]], [b["LAMT"]])
        S.op("dve", lambda e: e.tensor_reduce(out=LAMS[:, 0:1], in_=LAMT[:, 0, :], axis=AX.X, op=ALU.add),
             [b["LAMT"]], [b["LAMS"]])
        S.op("dve", lambda e: e.tensor_reduce(out=LAMS[:, 1:2], in_=LAMT[:, 2, :], axis=AX.X, op=ALU.add),
             [b["LAMT"]], [b["LAMS"]])
        act(LAMS[:, 4:6], LAMS[:, 0:2], AF.Exp, [b["LAMS"]], [b["LAMS"]])
        dve_tt(LAMS[:, 2:3], LAMS[:, 4:5], LAMS[:, 5:6], ALU.subtract, [b["LAMS"]], [b["LAMS"]])
        dve_ts(LAMS[:, 3:4], LAMS[:, 2:3], LAM_INIT, -1.0, ALU.add, ALU.mult, [b["LAMS"]], [b["LAMS"]])
        NEGLAM = LAMS[:, 3:4]
        dve_ts(SUBS[:], SUBS[:], 1.0 - LAM_INIT, None, ALU.mult, None, [b["SUBS"]], [b["SUBS"]])

        dbg_t = {}
        wlist = [(a_win_d, w1s, 4096, 0, "w1"), (a_wout_d, w2s, D, None, "w2"), (kvw_d, w3s, 2048, 1, "w3"),
                 (b_win_d, w4s, 2048, 2, "w4"), (b_wout_d, w5s, D, None, "w5")]
        conv_jobs = []
        conv_state = {"loaded": 0, "done": 0}
        CONV_LA = NCX - 1

        def conv_load(i):
            src, dst, ncol, gi, key, kc, c0 = conv_jobs[i]
            xi = i % NCX
            dma(CXL[xi][:], dap(src, kc * 128 * ncol + c0, [[ncol, 128], [1, D]]), CXLb[xi], d_in,
                queue="sp" if i % 2 == 0 else "pool")

        def conv_compute(i):
            src, dst, ncol, gi, key, kc, c0 = conv_jobs[i]
            xi = i % NCX
            ui = i % 3
            if gi is None:
                evac(UU[ui][:], CXL[xi][:], [CXLb[xi]], [UUb[ui]])
            elif i % 2 == 0:
                act(UU[ui][:], CXL[xi][:], AF.Copy, [CXLb[xi], b["GN"]], [UUb[ui]], scale=GN[:, gi, kc:kc + 1])
            else:
                dve_ts(UU[ui][:], CXL[xi][:], GN[:, gi, kc:kc + 1], None, ALU.mult, None,
                       [CXLb[xi], b["GN"]], [UUb[ui]])
            dma(dap(dst, kc * 128 * ncol + c0, [[ncol, 128], [1, D]]), UU[ui][:], d_w[key], UUb[ui],
                queue="pool" if i % 2 == 0 else "sp")
            _prune(d_w[key])

        def conv_step():
            i = conv_state["done"]
            while conv_state["loaded"] < min(len(conv_jobs), i + 1 + CONV_LA):
                conv_load(conv_state["loaded"])
                conv_state["loaded"] += 1
            conv_compute(i)
            conv_state["done"] += 1

        bg_work = []
        conv_done = {}

        def phase0b():
            for (src, dst, ncol, gi, key) in wlist:
                for c0 in range(0, ncol, D):
                    for kc in range(8):
                        conv_jobs.append((src, dst, ncol, gi, key, kc, c0))
                        bg_work.append((key, c0 // D))

        def bg_pop(n=1):
            for _ in range(n):
                if bg_work:
                    kk = bg_work.pop(0)
                    conv_step()
                    conv_done[kk] = conv_done.get(kk, 0) + 1

        def ensure_converted(key, blk):
            kk = (key, blk // 2)
            while conv_done.get(kk, 0) < 8:
                assert bg_work, kk
                bg_pop()

        if dbg_stage is None or dbg_stage >= 1:
            phase0b()
            bg_pop(16)
        dve_copy(T1[0:16, 0:257], RB16[:, :], [b["RB16"]], [b["T1"]])
        dve_ts(T1[0:16, 257:512], ZER[0:16, 0:255], RB16[:, 256:257], None, ALU.add, None,
               [b["ZER"], b["RB16"]], [b["T1"]])
        dma(bx_s.ap(), T1[0:16, :], d_w["bx"], b["T1"])
        for h in range(16):
            hk, hkb = RR[h % 2], RRb[h % 2]
            pb, pbb = PB[h % 2], PBb[h % 2]
            dma(hk[:, 0:256], dap(bx_s, h * 512 + 1, [[1, 128], [1, 256]]), hkb, d_w["bx"])
            S.op("pe", lambda e, pb=pb, hk=hk: e.matmul(pb[:, 0:256], lhsT=JJ[:], rhs=hk[:, 0:256], start=True, stop=True),
                 [b["JJ"], hkb], [pbb])
            act(EB[:, h, 0:256], pb[:, 0:256], AF.Exp, [pbb, b["NCH"]], [b["EB"]], bias=NCH[:, h:h + 1])
        S.op("dve", lambda e: e.memset(EB[64:128, :, 0:64], 0.0), [], [b["EB"]])

        wsrc = {"w1": (w1s, 4096), "w2": (w2s, D), "w3": (w3s, 2048), "w4": (w4s, 2048), "w5": (w5s, D)}
        group_blocks = ([("w1", c) for c in range(8)] + [("w2", c) for c in range(2)] + [("w3", c) for c in range(4)]
                        + [("w4", c) for c in range(4)] + [("w5", c) for c in range(2)])
        all_blocks = group_blocks * (NSEQ * NG)
        wstate = {"issued": 0, "used": 0}

        def wissue(upto):
            while wstate["issued"] <= upto and wstate["issued"] < len(all_blocks):
                n = wstate["issued"]
                key, c = all_blocks[n]
                ensure_converted(key, c)
                t, ncol = wsrc[key]
                slot = n % NW
                dma(WB[slot][:], dap(t, c * 512, [[ncol, 128], [128 * ncol, 8], [1, 512]]), WBb[slot], d_w[key])
                _prune(d_w[key])
                wstate["issued"] += 1

        def wnext(expect, live_prev=0):
            n = wstate["used"]
            assert all_blocks[n] == expect, (all_blocks[n], expect)
            wissue(n + NW - 1 - live_prev)
            wstate["used"] += 1
            return WB[n % NW], WBb[n % NW]

        pbr = [0]

        def next_pb(lo, hi):
            i = lo + pbr[0] % (hi - lo)
            pbr[0] += 1
            return PB[i], PBb[i]

        xtr = [0]

        def next_xtl():
            i = xtr[0] % NXT
            xtr[0] += 1
            return XTL[i], XTLb[i]

        ptr = [0]

        def next_pt():
            i = ptr[0] % NPT
            ptr[0] += 1
            return PT[i], PTb[i]

        def rstd_from_ss(ss_ap, n, out_ap, stb):
            act(out_ap, ss_ap, AF.Ln, [stb, b["EPSC"]], [stb], scale=1.0 / n, bias=EPSC[:, 0:1])
            act(out_ap, out_ap, AF.Exp, [stb], [stb], scale=-0.5)

        def norm_part(src_tile, src_buf, gi):
            ui = gi % 3
            yj = gi % 2
            st_, stb = STN[ui], STNb[ui]
            act(YN[yj][:], src_tile[:], AF.Square, [src_buf], [YNb[yj], stb], accum_out=st_[:, 0:1])
            rstd_from_ss(st_[:, 0:1], D, st_[:, 1:2], stb)
            act(UU[ui][:], src_tile[:], AF.Copy, [src_buf, stb], [UUb[ui]], scale=st_[:, 1:2])

        def tr_part(tcol, gi, XD, XDb):
            ui = gi % 3
            for kc in range(8):
                S.op("pe", lambda e, kc=kc: e.transpose(TRB[:, kc * 128:(kc + 1) * 128],
                                                        UU[ui][:, kc * 128:(kc + 1) * 128], IDb16[:]),
                     [UUb[ui], b["ID"]], [TRBb])
            evac(XD[:, :, tcol:tcol + 128], TRB[:].rearrange("p (k t) -> p k t", t=128), [TRBb], [XDb])

        def norm_transpose(src_tile, src_buf, tcol, gi, XD, XDb):
            norm_part(src_tile, src_buf, gi)
            tr_part(tcol, gi, XD, XDb)

        def proj_fm(wb, wbb, c, post, XS=None, XSb=None):
            if XS is None:
                XS, XSb = XT, b["XT"]
            pb, pbb = next_pb(0, 3)
            for kc in range(8):
                mm(pb[:, :], wb[:, kc, c * 128:(c + 1) * 128], XS[:, kc, :], kc == 0, [wbb, XSb], [pbb])
            post(pb, pbb)
            bg_pop()

        def outproj_norm_residual(w_key, gp_tile, gp_buf, s, g, layer, XS, XSb):
            w0, w0b = wnext((w_key, 0))
            w1_, w1b = wnext((w_key, 1), live_prev=1)
            xts = {}
            bankss = {}
            ob7 = [0]

            def mm_tile(t):
                row0 = g * GT + t * 128
                xt, xtb = next_xtl()
                if layer == 0:
                    dma(xt[:], x_ap[s, row0:row0 + 128, :], xtb, d_in)
                else:
                    dma(xt[:], out_ap[s, row0:row0 + 128, :], xtb, d_out[s][g][t])
                xts[t] = (xt, xtb)
                banks = []
                for hb, (w, wbuf) in enumerate([(w0, w0b), (w1_, w1b)]):
                    bi_ = ob7[0] % 6
                    ob7[0] += 1
                    pb, pbb = PB[bi_], PBb[bi_]
                    for fc in range(8):
                        mm(pb[:, :], XS[:, fc, t * 128:(t + 1) * 128], w[:, fc, :], fc == 0, [XSb, wbuf], [pbb])
                    banks.append((pb, pbb))
                bankss[t] = banks

            def post_tile(t):
                row0 = g * GT + t * 128
                xt, xtb = xts.pop(t)
                banks = bankss.pop(t)
                yi = t % 2
                so_, sob = STO[yi], STOb[yi]
                for hb, (pb, pbb) in enumerate(banks):
                    act(YN[yi][:, hb * 512:(hb + 1) * 512], pb[:, :], AF.Square, [pbb], [YNb[yi], sob],
                        accum_out=so_[:, 2 + hb:3 + hb])
                dve_tt(so_[:, 0:1], so_[:, 2:3], so_[:, 3:4], ALU.add, [sob], [sob])
                rstd_from_ss(so_[:, 0:1], D, so_[:, 1:2], sob)
                for hb, (pb, pbb) in enumerate(banks):
                    S.op("dve", lambda e, pb=pb, hb=hb, yi=yi, so_=so_: e.scalar_tensor_tensor(
                        out=YN[yi][:, hb * 512:(hb + 1) * 512], in0=pb[:, :], scalar=so_[:, 1:2],
                        in1=gp_tile[:, hb * 512:(hb + 1) * 512], op0=ALU.mult, op1=ALU.mult),
                        [pbb, sob, gp_buf], [YNb[yi]])
                dve_tt(xt[:], xt[:], YN[yi][:], ALU.add, [xtb, YNb[yi]], [xtb])
                dma(out_ap[s, row0:row0 + 128, :], xt[:], d_out[s][g][t], xtb, queue="pool")
                if layer == 0:
                    norm_part(xt, xtb, t)

            NT = DBG_TILES
            INFL = 2
            for t in range(min(INFL, NT)):
                mm_tile(t)
            for t in range(NT):
                post_tile(t)
                if t + INFL < NT:
                    mm_tile(t + INFL)
                if layer == 0:
                    tr_part(t * 128, t, X2T, b["X2T"])

        glist = [(s, g) for s in range(NSEQ) for g in range(NG)]
        if dbg_groups is not None:
            glist = glist[:dbg_groups]

        def run_all():
            stage(0)
            stage(1)
            pre(*glist[0])
            for gi_, (s, g) in enumerate(glist):
                do_group(s, g, glist[gi_ + 1] if gi_ + 1 < len(glist) else None)

        def ld_norm(s, g, t):
            xt, xtb = next_xtl()
            dma(xt[:], x_ap[s, g * GT + t * 128: g * GT + (t + 1) * 128, :], xtb, d_in)
            norm_part(xt, xtb, t)

        def pre(s, g, with_x=True):
            if True:
                tok0 = g * GT
                dma(POSI[:], dap(pos_d, s * SEQ + tok0, [[0, 128], [1, GT]]), b["POSI"], d_in)
                dve_copy(T0[:], POSI[:], [b["POSI"]], [b["T0"]])
                dve_ts(T0[:], T0[:], RTAB[:, 0:1], None, ALU.mult, None, [b["T0"], b["RTAB"]], [b["T0"]])
                TWO_PI = 2.0 * math.pi
                for (dst, dstb, shift) in ((SIN, b["SIN"], 0.0), (COS, b["COS"], math.pi / 2)):
                    dve_ts(T1[:], T0[:], shift, 1.0 / TWO_PI, ALU.add, ALU.mult, [b["T0"]], [b["T1"]])
                    dve_copy(POSI[:], T1[:], [b["T1"]], [b["POSI"]])
                    dve_copy(T1[:], POSI[:], [b["POSI"]], [b["T1"]])
                    dve_ts(T1[:], T1[:], -TWO_PI, shift, ALU.mult, ALU.add, [b["T1"]], [b["T1"]])
                    dve_tt(T1[:], T1[:], T0[:], ALU.add, [b["T1"], b["T0"]], [b["T1"]])
                    dve_ts(RS[:], T1[:], math.pi, TWO_PI, ALU.is_gt, ALU.mult, [b["T1"]], [b["RS"]])
                    dve_tt(T1[:], T1[:], RS[:], ALU.subtract, [b["T1"], b["RS"]], [b["T1"]])
                    dve_ts(RS[:], T1[:], -math.pi, TWO_PI, ALU.is_lt, ALU.mult, [b["T1"]], [b["RS"]])
                    dve_tt(T1[:], T1[:], RS[:], ALU.add, [b["T1"], b["RS"]], [b["T1"]])
                    act(dst[:], T1[:], AF.Sin, [b["T1"]], [dstb])
                dve_ts(SIN[:], SIN[:], RTAB[:, 1:2], None, ALU.mult, None, [b["SIN"], b["RTAB"]], [b["SIN"]])
                dve_ts(COS[:], COS[:], RTAB[:, 1:2], RTAB[:, 2:3], ALU.mult, ALU.add, [b["COS"], b["RTAB"]], [b["COS"]])

                if with_x:
                    for t in range(4):
                        ld_norm(s, g, t)
                        tr_part(t * 128, t, XT, b["XT"])

        def do_group(s, g, nxt):
            if True:
                tok0 = g * GT
                kcol0 = (g % 2) * 512
                for blk in range(2):
                    wb, wbb = wnext(("w1", blk))
                    for c in range(4):
                        fc = blk * 4 + c
                        proj_fm(wb, wbb, c, lambda pb, pbb, fc=fc: evac(QT[:, fc, :], pb[:, :], [pbb], [b["QT"]]))
                for blk in range(2):
                    wb, wbb = wnext(("w1", 2 + blk))
                    for c in range(4):
                        fc = blk * 4 + c
                        proj_fm(wb, wbb, c, lambda pb, pbb, fc=fc: evac(KT[:, fc, kcol0:kcol0 + 512], pb[:, :],
                                                                        [pbb], [b["KT"]]))
                for blk in range(2):
                    wb, wbb = wnext(("w1", 4 + blk))
                    for t in range(4):
                        slot = (4 * g + t) % 8
                        pb, pbb = next_pb(0, 3)
                        for kc in range(8):
                            mm(pb[:, :], XT[:, kc, t * 128:(t + 1) * 128], wb[:, kc, :], kc == 0,
                               [b["XT"], wbb], [pbb])
                        src = pb[:, :].rearrange("p (i c) -> p i c", c=128)
                        dst = VA[:, slot, blk * 768:(blk + 1) * 768].rearrange("p (i c) -> p i c", c=192)
                        act(dst[:, :, 0:64], src[:, :, 0:64], AF.Copy, [pbb], [b["VA"]])
                        dve_copy(dst[:, :, 128:192], src[:, :, 64:128], [pbb], [b["VA"]])
                for blk in range(2):
                    wb, wbb = wnext(("w1", 6 + blk))
                    for c in range(4):
                        fc = blk * 4 + c
                        proj_fm(wb, wbb, c, lambda pb, pbb, fc=fc: act(SGT[:, fc, :], pb[:, :], AF.Silu,
                                                                       [pbb], [b["SGT"]]))

                stage(2)
                items = []
                tl_ = []
                for Tk in range(max(0, 4 * g - 4), 4 * g + 4):
                    qlo = max(Tk, 4 * g)
                    qhi = min(Tk + 4, 4 * g + 3)
                    tl_.append((Tk, (qlo - 4 * g) * 128, (qhi + 1 - 4 * g) * 128, qlo - Tk))
                for i_ in range(8):
                    for n, (Tk, c0, c1, jlo) in enumerate(tl_):
                        for h in (2 * i_, 2 * i_ + 1):
                            items.append((h, n, Tk, c0, c1, jlo, n == len(tl_) - 1))
                DEPTH_A = 4
                sbank = {}
                sA = [0]
                postq = []

                def qkA(idx):
                    h, n, Tk, c0, c1, jlo, last = items[idx]
                    i, r0 = h // 2, 64 * (h % 2)
                    bi_ = sA[0] % 4
                    sA[0] += 1
                    pb, pbb = PB[bi_], PBb[bi_]
                    kc0 = (Tk % 8) * 128
                    mm(pb[:, c0:c1], KT[r0:r0 + 64, i, kc0:kc0 + 128], QT[r0:r0 + 64, i, c0:c1], True,
                       [b["KT"], b["QT"]], [pbb])
                    sbank[idx] = (pb, pbb)

                def finA(idx):
                    h, n, Tk, c0, c1, jlo, last = items[idx]
                    i, half = h // 2, h % 2
                    r0 = 64 * half
                    ob, obb = PB[4 + h % 3], PBb[4 + h % 3]
                    pb, pbb = sbank.pop(idx)
                    pt, ptb = next_pt()
                    act(pt[:, c0:c1], pb[:, c0:c1], AF.Exp, [pbb], [ptb], scale=0.125)
                    jhi = jlo + (c1 - c0) // 128 - 1
                    if jlo <= 1:
                        nb_ = (min(jhi, 1) - jlo + 1) * 128
                        dve_tt(pt[:, c0:c0 + nb_], pt[:, c0:c0 + nb_], EB[:, h, jlo * 128: jlo * 128 + nb_],
                               ALU.mult, [ptb, b["EB"]], [ptb])
                    if jhi == 4:
                        S.op("dve", lambda e, pt=pt, c1=c1: e.memset(pt[0:64, c1 - 64:c1], 0.0), [], [ptb])
                    vcol = i * 192 + 64 * half
                    mm(ob[:, c0:c1], VA[:, Tk % 8, vcol:vcol + 128], pt[:, c0:c1], n == 0, [b["VA"], ptb], [obb])
                    if last:
                        def postA(h=h, i=i, r0=r0, ob=ob, obb=obb):
                            so = 64 - r0
                            rr, rrb = RR[h % 2], RRb[h % 2]
                            act(rr[so:so + 64, :], ob[so:so + 64, :], AF.Ln, [obb], [rrb])
                            act(rr[so:so + 64, :], rr[so:so + 64, :], AF.Exp, [rrb], [rrb], scale=-1.0)
                            ta, tab_ = TA[h % 2], TAb[h % 2]
                            dve_tt(ta[r0:r0 + 64, :], ob[r0:r0 + 64, :], rr[so:so + 64, :], ALU.mult, [obb, rrb], [tab_])
                            dve_tt(XT[r0:r0 + 64, i, :], ta[r0:r0 + 64, :], SGT[r0:r0 + 64, i, :], ALU.mult,
                                   [tab_, b["SGT"]], [b["XT"]])
                            bg_pop(3)
                        postq.append([2, postA])

                def tickA():
                    for it in postq:
                        it[0] -= 1
                    while postq and postq[0][0] < 0:
                        postq.pop(0)[1]()

                for idx in range(0, len(items) + DEPTH_A, 2):
                    for k_ in (idx - DEPTH_A, idx + 1 - DEPTH_A):
                        if 0 <= k_ < len(items):
                            tickA()
                            finA(k_)
                    for k_ in (idx, idx + 1):
                        if k_ < len(items):
                            qkA(k_)
                while postq:
                    postq.pop(0)[1]()

                stage(3)
                outproj_norm_residual("w2", GPA, b["GPA"], s, g, 0, XT, b["XT"])

                stage(4)
                rp = [0]

                def rope_post(pb, pbb, dst_ap, dst_buf):
                    k = rp[0] % 2
                    rp[0] += 1
                    kr, krb, ta, tab_, tb, tbb = KRAW[k], KRAWb[k], TA[k], TAb[k], TBB[k], TBBb[k]
                    act(kr[:], pb[:, :], AF.Copy, [pbb], [krb])
                    p2, p2b = PB[3 + k], PBb[3 + k]
                    mm(p2[:, :], PERMb[:], kr[:], True, [b["PERM"], krb], [p2b])
                    dve_tt(ta[:], pb[:, :], COS[:], ALU.mult, [pbb, b["COS"]], [tab_])
                    dve_tt(tb[:], p2[:, :], SIN[:], ALU.mult, [p2b, b["SIN"]], [tbb])
                    dve_tt(dst_ap, ta[:], tb[:], ALU.add, [tab_, tbb], [dst_buf])

                if nxt is not None:
                    for t_ in range(3):
                        ld_norm(nxt[0], nxt[1], t_)
                for blk in range(2):
                    wb, wbb = wnext(("w3", blk))
                    for c in range(4):
                        hh = blk * 4 + c
                        ks, ksb = KST[hh % 2], KSTb[hh % 2]
                        proj_fm(wb, wbb, c, lambda pb, pbb, ks=ks, ksb=ksb: rope_post(pb, pbb, ks[:], ksb),
                                X2T, b["X2T"])
                        dma(dap(ksh_s, ((s * 8 + hh) * 128) * SEQ + tok0, [[SEQ, 128], [1, GT]]), ks[:],
                            d_ksh[s][g], ksb, queue="pool")
                        _prune(d_ksh[s][g])
                for t in range(4):
                    pass
                vblocks = [wnext(("w3", 2)), wnext(("w3", 3), live_prev=1)]
                for t in range(4):
                    vs, vsb = VST[t % 2], VSTb[t % 2]
                    for blk in range(2):
                        wb, wbb = vblocks[blk]
                        pb, pbb = next_pb(0, 3)
                        for kc in range(8):
                            mm(pb[:, :], X2T[:, kc, t * 128:(t + 1) * 128], wb[:, kc, :], kc == 0,
                               [b["X2T"], wbb], [pbb])
                        evac(vs[:, blk * 512:(blk + 1) * 512], pb[:, :], [pbb], [vsb])
                    dma(dap(vsh_s, (s * 8 * SEQ + tok0 + t * 128) * 128, [[128, 128], [SEQ * 128, 8], [1, 128]]),
                        vs[:].rearrange("p (h v) -> p h v", v=128), d_vsh[s][g], vsb, queue="pool")
                    _prune(d_vsh[s][g])

                if nxt is not None:
                    tr_part(0, 0, XT, b["XT"])
                    ld_norm(nxt[0], nxt[1], 3)
                    tr_part(128, 1, XT, b["XT"])
                    tr_part(256, 2, XT, b["XT"])
                for blk in range(2):
                    wb, wbb = wnext(("w4", blk))
                    for c in range(4):
                        hh = blk * 4 + c
                        proj_fm(wb, wbb, c, lambda pb, pbb, hh=hh: rope_post(pb, pbb, QT[:, hh, :], b["QT"]),
                                X2T, b["X2T"])
                for blk in range(2):
                    wb, wbb = wnext(("w4", 2 + blk))
                    for c in range(4):
                        hh = blk * 4 + c

                        def gpost(pb, pbb, hh=hh):
                            act(T0[:], pb[:, :], AF.Silu, [pbb], [b["T0"]])
                            dve_ts(SGT[:, hh, :], T0[:], SUBS[:, 0:1], None, ALU.mult, None,
                                   [b["T0"], b["SUBS"]], [b["SGT"]])
                        proj_fm(wb, wbb, c, gpost, X2T, b["X2T"])

                stage(5)
                if nxt is not None:
                    pre(nxt[0], nxt[1], with_x=False)
                    tr_part(384, 3, XT, b["XT"])
                ntile = 4 * g + 4
                O0, O0b, O1, O1b = PB[3], PBb[3], PB[4], PBb[4]
                Z0, Z0b, Z1, Z1b = PB[5], PBb[5], PB[6], PBb[6]
                itemsB = [(hh, n) for hh in range(8) for n in range(ntile)]
                kvslot = {}
                sbankB = {}
                pend_d1 = [None]
                pend_d2 = [None]
                ssb = [None]

                def loadB(hh):
                    kb, kbb = KB[hh % 2], KBb[hh % 2]
                    vb, vbb = VB[hh % 2], VBb[hh % 2]
                    for gg in range(g + 1):
                        dma(kb[:, gg * GT:(gg + 1) * GT],
                            dap(ksh_s, ((s * 8 + hh) * 128) * SEQ + gg * GT, [[SEQ, 128], [1, GT]]),
                            kbb, d_ksh[s][gg])
                        dma(vb[:, gg * 4:(gg + 1) * 4, :],
                            dap(vsh_s, ((s * 8 + hh) * SEQ + gg * GT) * 128, [[128, 128], [128 * 128, 4], [1, 128]]),
                            vbb, d_vsh[s][gg])
                    kvslot[hh] = (kb, kbb, vb, vbb)

                def qkB(idx):
                    hh, n = itemsB[idx]
                    if n == 0:
                        loadB(hh)
                    kb, kbb, vb, vbb = kvslot[hh]
                    c0 = max(0, n - 4 * g) * 128
                    res = []
                    for m in range(2):
                        pb, pbb = next_pb(0, 3)
                        mm(pb[:, c0:GT], kb[64 * m:64 * m + 64, n * 128:(n + 1) * 128],
                           QT[64 * m:64 * m + 64, hh, c0:GT], True, [kbb, b["QT"]], [pbb])
                        res.append((pb, pbb))
                    sbankB[idx] = res

                def expB(idx):
                    hh, n = itemsB[idx]
                    c0 = max(0, n - 4 * g) * 128
                    res = sbankB.pop(idx)
                    pts = []
                    for m in range(2):
                        pb, pbb = res[m]
                        pt, ptb = next_pt()
                        act(pt[:, c0:GT], pb[:, c0:GT], AF.Exp, [pbb], [ptb], scale=0.125)
                        if n >= 4 * g:
                            S.op("dve", lambda e, pt=pt, c0=c0: e.memset(pt[64:128, c0:c0 + 64], 0.0), [], [ptb])
                        pts.append((pt, ptb))
                    return pts

                def pvB(idx, pts):
                    hh, n = itemsB[idx]
                    kb, kbb, vb, vbb = kvslot[hh]
                    c0 = max(0, n - 4 * g) * 128
                    for m, (O, Ob, Z, Zb) in enumerate([(O0, O0b, Z0, Z0b), (O1, O1b, Z1, Z1b)]):
                        pt, ptb = pts[m]
                        mm(O[:, c0:GT], vb[:, n, :], pt[:, c0:GT], n == 0, [vbb, ptb], [Ob])
                        mm(Z[:, c0:GT], ONESb[:], pt[:, c0:GT], n == 0, [b["ONES"], ptb], [Zb])
                    if n == 1 and pend_d1[0] is not None:
                        pend_d1[0]()
                        pend_d1[0] = None
                    if n == 3 and pend_d2[0] is not None:
                        pend_d2[0]()
                        pend_d2[0] = None
                    if n == ntile - 1:
                        act(RR[0][:], Z0[:, :], AF.Ln, [Z0b], [RRb[0]])
                        act(RR[1][:], Z1[:, :], AF.Ln, [Z1b], [RRb[1]])
                        act(RR[0][:], RR[0][:], AF.Exp, [RRb[0]], [RRb[0]], scale=-1.0)
                        act(RR[1][:], RR[1][:], AF.Exp, [RRb[1]], [RRb[1]], scale=-1.0)
                        dve_tt(T0[:], O0[:, :], RR[0][:], ALU.mult, [O0b, RRb[0]], [b["T0"]])
                        dve_tt(T1[:], O1[:, :], RR[1][:], ALU.mult, [O1b, RRb[1]], [b["T1"]])
                        S.op("dve", lambda e: e.scalar_tensor_tensor(out=T0[:], in0=T1[:], scalar=NEGLAM, in1=T0[:],
                                                                     op0=ALU.mult, op1=ALU.add),
                             [b["T0"], b["T1"], b["LAMS"]], [b["T0"]])

                        def d1():
                            act(SQ[:], T0[:], AF.Square, [b["T0"]], [b["SQ"]])
                            pb, pbb = next_pb(0, 3)
                            mm(pb[:, :], ONESb[:], SQ[:], True, [b["ONES"], b["SQ"]], [pbb])
                            dve_copy(RS[:], pb[:, :], [pbb], [b["RS"]])

                        def d2(hh=hh):
                            act(RS[:], RS[:], AF.Ln, [b["RS"], b["EPSC"]], [b["RS"]], scale=1.0 / 128, bias=EPSC[:, 0:1])
                            act(RS[:], RS[:], AF.Exp, [b["RS"]], [b["RS"]], scale=-0.5)
                            dve_tt(T0[:], T0[:], RS[:], ALU.mult, [b["T0"], b["RS"]], [b["T0"]])
                            dve_tt(X2T[:, hh, :], T0[:], SGT[:, hh, :], ALU.mult, [b["T0"], b["SGT"]], [b["X2T"]])
                        pend_d1[0] = d1
                        pend_d2[0] = d2

                qkB(0)
                for idx in range(len(itemsB)):
                    pts = expB(idx)
                    if idx + 1 < len(itemsB):
                        qkB(idx + 1)
                    pvB(idx, pts)
                for pd_ in (pend_d1, pend_d2):
                    if pd_[0] is not None:
                        pd_[0]()
                        pd_[0] = None

                stage(6)
                outproj_norm_residual("w5", GPB, b["GPB"], s, g, 1, X2T, b["X2T"])

        try:
            run_all()
        except _Stop:
            pass
        outs = [d_out[s][g][t] for s in range(NSEQ) for g in range(NG) for t in range(4)]
        names = {"EB": (EB, [128, 16 * 256], BF16), "CH": (CH, [128, 16], F32), "LAMS": (LAMS, [128, 8], F32),
                 "XT": (XT, [128, 8 * GT], BF16), "X2T": (X2T, [128, 8 * GT], BF16), "QT": (QT, [128, 8 * GT], BF16),
                 "SGT": (SGT, [128, 8 * GT], BF16), "KT": (KT, [128, 8 * 1024], BF16), "VA": (VA, [128, 8 * 1536], BF16),
                 "COS": (COS, [128, GT], F32), "SIN": (SIN, [128, GT], F32), "GN": (GN, [128, 24], F32),
                 "PERM": (PERMb, [128, 128], BF16), "ST": (ST, [128, 16], F32), "YN0": (YN[0], [128, D], F32),
                 "YN1": (YN[1], [128, D], F32)}
        b["YN0"] = YNb[0]; b["YN1"] = YNb[1]
        for nm in dbg_dump:
            tl, shp, dt = names[nm]
            dd = nc.dram_tensor("dbg_" + nm, shp, dt, kind="ExternalOutput")
            db_ = Buf("dbg_" + nm)
            src = tl[:] if len(tl.shape) == 2 else tl[:].rearrange("p a b -> p (a b)")
            dma(dd.ap(), src, db_, b.get(nm, b.get("PERM")), queue="pool")
            outs.append(db_)
        S.final_wait("pool", outs)
        print("sbuf bytes remaining/partition:", nc.sbuf_bytes_remaining, flush=True)
        S.emit(st)
        print("instr counts:", {e: len(S.ops[e]) for e in ENGS}, "dma sems:", S.ndsem, flush=True)
    return nc


_CACHE = {}


def _consts():
    ident = np.eye(128, dtype=np.float32)
    ones = np.ones((128, 128), np.float32)
    perm = np.zeros((128, 128), np.float32)
    for base in (0, 64):
        for d in range(8):
            perm[base + d + 8, base + d] = -1.0
            perm[base + d, base + d + 8] = 1.0
    anti = np.ascontiguousarray(np.eye(128, dtype=np.float32)[::-1])
    cmat = np.stack([ident, ones, perm, anti]).astype(np.float32)
    rtab = np.zeros((128, 4), np.float32)
    inv = np.power(np.float32(ROPE_THETA), -np.arange(8, dtype=np.float32) * np.float32(2.0) / np.float32(16))
    for p in range(128):
        d = p % 64
        if d < 16:
            rtab[p, 0] = inv[d % 8]
            rtab[p, 1] = 1.0
        else:
            rtab[p, 2] = 1.0
        rtab[p, 3] = math.pi
    return cmat, rtab.astype(np.float32)


def kernel(x, positions, a_norm_pre, a_w_in, a_rel_bias, a_w_out, a_norm_post, kv_norm, kv_w,
           b_norm_pre, b_w_in, b_lambda_q1, b_lambda_k1, b_lambda_q2, b_lambda_k2, b_subln,
           b_w_out, b_norm_post):
    f = lambda a: np.ascontiguousarray(np.asarray(a, dtype=np.float32))
    x = f(x)
    positions = np.ascontiguousarray(np.asarray(positions, dtype=np.int32))
    cmat, rtab = _consts()
    shared = {
        "a_norm_pre": f(a_norm_pre).reshape(D), "a_w_in": f(a_w_in).reshape(D, 4096),
        "a_rel_bias": f(a_rel_bias).reshape(16, 257), "a_w_out": f(a_w_out).reshape(D, D),
        "a_norm_post": f(a_norm_post).reshape(D), "kv_norm": f(kv_norm).reshape(D),
        "kv_w": f(kv_w).reshape(D, 2048), "b_norm_pre": f(b_norm_pre).reshape(D),
        "b_w_in": f(b_w_in).reshape(D, 2048),
        "b_lam": np.stack([f(b_lambda_q1).reshape(64), f(b_lambda_k1).reshape(64),
                           f(b_lambda_q2).reshape(64), f(b_lambda_k2).reshape(64)]),
        "b_subln": f(b_subln).reshape(128), "b_w_out": f(b_w_out).reshape(D, D),
        "b_norm_post": f(b_norm_post).reshape(D), "cmat": cmat, "rtab": rtab,
    }
    if "nc" not in _CACHE:
        _CACHE["nc"] = build_program()
    nc = _CACHE["nc"]
    in_maps = []
    for c in range(NCORES):
        m = dict(shared)
        m["x"] = x[c * NSEQ:(c + 1) * NSEQ]
        m["pos"] = positions[c * NSEQ:(c + 1) * NSEQ]
        in_maps.append(m)
    res = run_bass_kernel_spmd(nc, in_maps, core_ids=list(range(NCORES)))
    return np.concatenate([np.asarray(r["out"]).reshape(NSEQ, SEQ, D) for r in res.results], axis=0).astype(np.float32)
```

```python
import math
from contextlib import ExitStack
import numpy as np
import concourse.bass as bass
import concourse.mybir as mybir
from concourse.bass_utils import run_bass_kernel_spmd

F32 = mybir.dt.float32
BF16 = mybir.dt.bfloat16
I32 = mybir.dt.int32
AF = mybir.ActivationFunctionType
ALU = mybir.AluOpType
AX = mybir.AxisListType

NCORES = 8
SEQ = 2048
D = 1024
NSEQ = 2
GT = 512
NG = SEQ // GT
EPS = 1e-6
LAM_INIT = 0.8 - 0.6 * math.exp(-0.3 * 1)
ROPE_THETA = 500000.0

ENGS = ("pe", "act", "dve", "pool", "sp")
DBG_TILES = 4


class Buf:
    __slots__ = ("name", "w", "r", "dsem", "excl")

    def __init__(self, name, excl=False):
        self.name = name
        self.w = None
        self.r = []
        self.dsem = None
        self.excl = excl


class Sched:
    def __init__(self, nc):
        self.nc = nc
        self.ops = {e: [] for e in ENGS}
        self.cnt = {e: 0 for e in ENGS}
        self.seen = {e: {} for e in ENGS}
        self.ndsem = 0
        self.dcnt = {}

    def _waits(self, eng, reads, writes):
        waits = {}

        def need(t):
            if t is None:
                return
            k, n = t
            if eng == "pe" and k == "pe":
                return
            if n > self.seen[eng].get(k, 0) and n > waits.get(k, 0):
                waits[k] = n

        for b in reads:
            need(b.w)
            if b.excl:
                for t in b.r:
                    if t[0] != eng:
                        need(t)
        for b in writes:
            need(b.w)
            for t in b.r:
                need(t)
        for k, n in waits.items():
            self.seen[eng][k] = n
        return list(waits.items())

    def op(self, eng, fn, reads=(), writes=()):
        waits = self._waits(eng, reads, writes)
        self.cnt[eng] += 1
        tick = (eng, self.cnt[eng])
        for b in reads:
            if len(b.r) > 64:
                _prune(b)
            b.r.append(tick)
        for b in writes:
            b.w = tick
            b.r = []
        self.ops[eng].append((fn, waits, tick))
        return tick

    def dma(self, fn, dst, src, queue="sp"):
        waits = self._waits(queue, [src], [dst])
        if dst.dsem is None:
            dst.dsem = "q%d" % self.ndsem
            self.ndsem += 1
            self.dcnt[dst.dsem] = 0
        self.dcnt[dst.dsem] += 16
        tick = (dst.dsem, self.dcnt[dst.dsem])
        if len(src.r) > 64:
            _prune(src)
        src.r.append(tick)
        dst.w = tick
        dst.r = []
        self.ops[queue].append((fn, waits, tick))
        return tick

    def final_wait(self, eng, bufs):
        waits = dict(self._waits(eng, [], bufs))
        for k, n in self.dcnt.items():
            if n > self.seen[eng].get(k, 0):
                waits[k] = n
                self.seen[eng][k] = n
        self.ops[eng].append((None, list(waits.items()), None))

    def emit(self, stack):
        nc = self.nc
        sems = {}
        for e in ENGS:
            sems[e] = stack.enter_context(nc.semaphore("s_" + e))
        for k in self.dcnt:
            sems[k] = stack.enter_context(nc.semaphore("s_" + k))
        block = stack.enter_context(nc.Block())

        def run(ename):
            def body(eng):
                for fn, waits, tick in self.ops[ename]:
                    for k, n in waits:
                        eng.wait_ge(sems[k], n)
                    if fn is None:
                        continue
                    ins = fn(eng)
                    k, n = tick
                    ins.then_inc(sems[k], 16 if k.startswith("q") else 1)
            return body

        block.tensor(run("pe"))
        block.scalar(run("act"))
        block.vector(run("dve"))
        block.gpsimd(run("pool"))
        block.sync(run("sp"))


def _prune(b):
    best = {}
    for k, n in b.r:
        if n > best.get(k, 0):
            best[k] = n
    b.r = list(best.items())


class _Stop(Exception):
    pass


def build_program(dbg_stage=None, dbg_groups=None, dbg_dump=()):
    def stage(k):
        if dbg_stage is not None and dbg_stage == k:
            raise _Stop()

    nc = bass.Bass("TRN2", target_bir_lowering=False)

    def din(name, shape, dt=F32):
        return nc.dram_tensor(name, list(shape), dt, kind="ExternalInput")

    x_d = din("x", [NSEQ, SEQ, D])
    pos_d = din("pos", [NSEQ, SEQ], I32)
    a_npre_d = din("a_norm_pre", [D])
    a_win_d = din("a_w_in", [D, 4096])
    a_rb_d = din("a_rel_bias", [16, 257])
    a_wout_d = din("a_w_out", [D, D])
    a_npost_d = din("a_norm_post", [D])
    kvn_d = din("kv_norm", [D])
    kvw_d = din("kv_w", [D, 2048])
    b_npre_d = din("b_norm_pre", [D])
    b_win_d = din("b_w_in", [D, 2048])
    lam_d = din("b_lam", [4, 64])
    subln_d = din("b_subln", [128])
    b_wout_d = din("b_w_out", [D, D])
    b_npost_d = din("b_norm_post", [D])
    cmat_d = din("cmat", [4, 128, 128])
    rtab_d = din("rtab", [128, 4])
    out_d = nc.dram_tensor("out", [NSEQ, SEQ, D], F32, kind="ExternalOutput")

    w1s = nc.dram_tensor("w1s", [D, 4096], BF16)
    w2s = nc.dram_tensor("w2s", [D, D], BF16)
    w3s = nc.dram_tensor("w3s", [D, 2048], BF16)
    w4s = nc.dram_tensor("w4s", [D, 2048], BF16)
    w5s = nc.dram_tensor("w5s", [D, D], BF16)
    bx_s = nc.dram_tensor("bx_s", [16, 512], F32)
    chs = nc.dram_tensor("chs", [16], F32)
    ksh_s = nc.dram_tensor("ksh_s", [NSEQ, 8, 128, SEQ], BF16)
    vsh_s = nc.dram_tensor("vsh_s", [NSEQ, 8, SEQ, 128], BF16)

    x_ap = x_d.ap(); out_ap = out_d.ap(); cmat_ap = cmat_d.ap(); rtab_ap = rtab_d.ap()

    def dap(t, offset, ap):
        return bass.AP(tensor=t, offset=offset, ap=[list(a) for a in ap])

    S = Sched(nc)
    with ExitStack() as st:
        def sb(name, shape, dt):
            return st.enter_context(nc.sbuf_tensor(name, list(shape), dt))

        def ps(name, shape, dt=F32):
            return st.enter_context(nc.psum_tensor(name, list(shape), dt))

        IDb16 = sb("IDb16", [128, 128], BF16); ONESb = sb("ONESb", [128, 128], BF16)
        PERMb = sb("PERMb", [128, 128], BF16); JJ = sb("JJ", [128, 128], F32)
        CST = sb("CST", [128, 128], F32)
        RTAB = sb("RTAB", [128, 4], F32)
        EB = sb("EB", [128, 16, 256], BF16)
        CH = sb("CH", [128, 16], F32)
        RB16 = sb("RB16", [16, 257], F32)
        GPA = sb("GPA", [128, D], F32); GPB = sb("GPB", [128, D], F32)
        GN = sb("GN", [128, 3, 8], F32)
        SUBS = sb("SUBS", [128, 1], F32)
        LAMT = sb("LAMT", [128, 4, 64], F32)
        LAMS = sb("LAMS", [128, 8], F32)
        EPSC = sb("EPSC", [128, 2], F32)
        ZER = sb("ZER", [128, 256], F32)
        ST = sb("ST", [128, 16], F32)
        STN = [sb("STN%d" % i, [128, 4], F32) for i in range(3)]
        STO = [sb("STO%d" % i, [128, 4], F32) for i in range(2)]
        NCH = sb("NCH", [128, 16], F32)
        NW = 3
        WB = [sb("WB%d" % i, [128, 8, 512], BF16) for i in range(NW)]
        NXT = 2
        NCX = 3
        XTL = [sb("XTL%d" % i, [128, D], F32) for i in range(NXT)]
        CXL = [sb("CXL%d" % i, [128, D], F32) for i in range(NCX)]
        YN = [sb("YN%d" % i, [128, D], F32) for i in range(2)]
        UU = [sb("UU%d" % i, [128, D], BF16) for i in range(3)]
        XT = sb("XT", [128, 8, GT], BF16)
        X2T = sb("X2T", [128, 8, GT], BF16)
        QT = sb("QT", [128, 8, GT], BF16)
        SGT = sb("SGT", [128, 8, GT], BF16)
        KT = sb("KT", [128, 8, 1024], BF16)
        VA = sb("VA", [128, 8, 1536], BF16)
        NPT = 4
        PT = [sb("PT%d" % i, [128, GT], BF16) for i in range(NPT)]
        RR = [sb("RR%d" % i, [128, GT], F32) for i in range(2)]
        T0 = sb("T0", [128, GT], F32); T1 = sb("T1", [128, GT], F32)
        TA = [sb("TA%d" % i, [128, GT], F32) for i in range(2)]
        TBB = [sb("TBB%d" % i, [128, GT], F32) for i in range(2)]
        SQ = sb("SQ", [128, GT], BF16); RS = sb("RS", [128, GT], F32)
        KRAW = [sb("KRAW%d" % i, [128, GT], BF16) for i in range(2)]
        COS = sb("COS", [128, GT], F32); SIN = sb("SIN", [128, GT], F32)
        POSI = sb("POSI", [128, GT], I32)
        KB = [sb("KB%d" % i, [128, SEQ], BF16) for i in range(2)]
        VB = [sb("VB%d" % i, [128, 16, 128], BF16) for i in range(2)]
        KST = [sb("KST%d" % i, [128, GT], BF16) for i in range(2)]
        VST = [sb("VST%d" % i, [128, D], BF16) for i in range(2)]

        PB = [ps("PB%d" % i, [128, 512], F32) for i in range(7)]
        TRB = ps("TRB", [128, 1024], BF16)

        b = {}
        def B(name, excl=False):
            b[name] = Buf(name, excl)
            return b[name]
        for nm in ["ID", "ONES", "PERM", "JJ", "CST", "RTAB", "EB", "CH", "GPA", "GPB", "GN", "SUBS", "LAMT",
                   "LAMS", "EPSC", "ZER", "ST", "XT", "X2T", "RB16", "QT", "SGT", "KT", "VA", "T0", "T1", "SQ", "RS",
                   "COS", "SIN", "POSI"]:
            B(nm)
        WBb = [Buf("WB%d" % i) for i in range(NW)]
        STNb = [Buf("STN%d" % i) for i in range(3)]
        STOb = [Buf("STO%d" % i) for i in range(2)]
        TAb = [Buf("TA%d" % i) for i in range(2)]
        TBBb = [Buf("TBB%d" % i) for i in range(2)]
        KRAWb = [Buf("KRAW%d" % i) for i in range(2)]
        B("NCH")
        XTLb = [Buf("XTL%d" % i) for i in range(NXT)]
        CXLb = [Buf("CXL%d" % i) for i in range(NCX)]
        YNb = [Buf("YN%d" % i) for i in range(2)]
        UUb = [Buf("UU%d" % i) for i in range(3)]
        PTb = [Buf("PT%d" % i) for i in range(NPT)]
        RRb = [Buf("RR%d" % i) for i in range(2)]
        KBb = [Buf("KB%d" % i) for i in range(2)]
        VBb = [Buf("VB%d" % i) for i in range(2)]
        KSTb = [Buf("KST%d" % i) for i in range(2)]
        VSTb = [Buf("VST%d" % i) for i in range(2)]
        PBb = [Buf("PB%d" % i, True) for i in range(7)]
        TRBb = Buf("TRB", True)
        d_in = Buf("d_in")
        d_w = {k: Buf("d_" + k) for k in ["w1", "w2", "w3", "w4", "w5", "bx", "chs"]}
        d_ksh = [[Buf("d_ksh%d_%d" % (s, g)) for g in range(NG)] for s in range(NSEQ)]
        d_vsh = [[Buf("d_vsh%d_%d" % (s, g)) for g in range(NG)] for s in range(NSEQ)]
        d_out = [[[Buf("d_out%d_%d_%d" % (s, g, t)) for t in range(4)] for g in range(NG)] for s in range(NSEQ)]

        def act(out, in_, func, reads, writes, **kw):
            S.op("act", lambda e: e.activation(out=out, in_=in_, func=func, **kw), reads, writes)

        def mm(out, lhsT, rhs, start, reads, writes):
            S.op("pe", lambda e: e.matmul(out, lhsT=lhsT, rhs=rhs, start=start, stop=False,
                                          skip_group_check=True), reads, writes)

        def dve_copy(out, in_, reads, writes):
            S.op("dve", lambda e: e.tensor_copy(out=out, in_=in_), reads, writes)

        def dve_tt(out, in0, in1, op, reads, writes, eng="dve"):
            S.op(eng, lambda e: e.tensor_tensor(out=out, in0=in0, in1=in1, op=op), reads, writes)

        def dve_ts(out, in0, s1, s2, op0, op1, reads, writes, eng="dve"):
            if op1 is None:
                S.op(eng, lambda e: e.tensor_scalar(out=out, in0=in0, scalar1=s1, scalar2=None, op0=op0), reads, writes)
            else:
                S.op(eng, lambda e: e.tensor_scalar(out=out, in0=in0, scalar1=s1, scalar2=s2, op0=op0, op1=op1),
                     reads, writes)

        def dma(out, in_, dst, src, queue="sp", slow=False):
            if slow:
                S.dma(lambda e: e.dma_start(out=out, in_=in_, allow_slow_non_contiguous=True), dst, src, queue)
            else:
                S.dma(lambda e: e.dma_start(out=out, in_=in_), dst, src, queue)

        evac_flip = [0]

        def evac(out, in_, reads, writes):
            evac_flip[0] ^= 1
            if evac_flip[0]:
                act(out, in_, AF.Copy, reads, writes)
            else:
                dve_copy(out, in_, reads, writes)

        for i, (t, nm) in enumerate([(IDb16, "ID"), (ONESb, "ONES"), (PERMb, "PERM")]):
            dma(CST[:], cmat_ap[i], b["CST"], d_in)
            dve_copy(t[:], CST[:], [b["CST"]], [b[nm]])
        dma(JJ[:], cmat_ap[3], b["JJ"], d_in)
        dma(RTAB[:], rtab_ap, b["RTAB"], d_in)
        dma(GPA[:], dap(a_npost_d, 0, [[0, 128], [1, D]]), b["GPA"], d_in)
        dma(GPB[:], dap(b_npost_d, 0, [[0, 128], [1, D]]), b["GPB"], d_in)
        for i, t in enumerate([a_npre_d, kvn_d, b_npre_d]):
            for kc in range(8):
                dma(GN[:, i, kc:kc + 1], dap(t, kc * 128, [[1, 128], [1, 1]]), b["GN"], d_in)
        dma(SUBS[:], dap(subln_d, 0, [[1, 128], [1, 1]]), b["SUBS"], d_in)
        dma(LAMT[:], dap(lam_d, 0, [[0, 128], [64, 4], [1, 64]]), b["LAMT"], d_in)
        dma(RB16[:], a_rb_d.ap(), b["RB16"], d_in)
        dma(dap(chs, 0, [[1, 16], [1, 1]]), RB16[:, 256:257], d_w["chs"], b["RB16"])
        dma(CH[:], dap(chs, 0, [[0, 128], [1, 16]]), b["CH"], d_w["chs"])
        dve_ts(NCH[:], CH[:], -1.0, None, ALU.mult, None, [b["CH"]], [b["NCH"]])
        S.op("dve", lambda e: e.memset(EPSC[:, 0:1], EPS), [], [b["EPSC"]])
        S.op("dve", lambda e: e.memset(EPSC[:, 1:2], 0.0), [], [b["EPSC"]])
        S.op("dve", lambda e: e.memset(ZER[:], 0.0), [], [b["ZER"]])
        S.op("pool", lambda e: e.memset(VA[:], 1.0), [], [b["VA"]])
        dve_tt(LAMT[:, 0, :], LAMT[:, 0, :], LAMT[:, 1, :], ALU.mult, [b["LAMT"]], [b["LAMT"]])
        dve_tt(LAMT[:, 2, :], LAMT[:, 2, :], LAMT[:, 3, :], ALU.mult, [b["LAMT"]], [b["LAMT"]])
        S.op("dve", lambda e: e.tensor_reduce(out=LAMS[:, 0:1], in_=LAMT[:, 0, :], axis=AX.X, op=ALU.add),
             [b["LAMT"]], [b["LAMS"]])
        S.op("dve", lambda e: e.tensor_reduce(out=LAMS[:, 1:2], in_=LAMT[:, 2, :], axis=AX.X, op=ALU.add),
             [b["LAMT"]], [b["LAMS"]])
        act(LAMS[:, 4:6], LAMS[:, 0:2], AF.Exp, [b["LAMS"]], [b["LAMS"]])
        dve_tt(LAMS[:, 2:3], LAMS[:, 4:5], LAMS[:, 5:6], ALU.subtract, [b["LAMS"]], [b["LAMS"]])
        dve_ts(LAMS[:, 3:4], LAMS[:, 2:3], LAM_INIT, -1.0, ALU.add, ALU.mult, [b["LAMS"]], [b["LAMS"]])
        NEGLAM = LAMS[:, 3:4]
        dve_ts(SUBS[:], SUBS[:], 1.0 - LAM_INIT, None, ALU.mult, None, [b["SUBS"]], [b["SUBS"]])

        dbg_t = {}
        wlist = [(a_win_d, w1s, 4096, 0, "w1"), (a_wout_d, w2s, D, None, "w2"), (kvw_d, w3s, 2048, 1, "w3"),
                 (b_win_d, w4s, 2048, 2, "w4"), (b_wout_d, w5s, D, None, "w5")]
        conv_jobs = []
        conv_state = {"loaded": 0, "done": 0}
        CONV_LA = NCX - 1

        def conv_load(i):
            src, dst, ncol, gi, key, kc, c0 = conv_jobs[i]
            xi = i % NCX
            dma(CXL[xi][:], dap(src, kc * 128 * ncol + c0, [[ncol, 128], [1, D]]), CXLb[xi], d_in,
                queue="sp" if i % 2 == 0 else "pool")

        def conv_compute(i):
            src, dst, ncol, gi, key, kc, c0 = conv_jobs[i]
            xi = i % NCX
            ui = i % 3
            if gi is None:
                evac(UU[ui][:], CXL[xi][:], [CXLb[xi]], [UUb[ui]])
            elif i % 2 == 0:
                act(UU[ui][:], CXL[xi][:], AF.Copy, [CXLb[xi], b["GN"]], [UUb[ui]], scale=GN[:, gi, kc:kc + 1])
            else:
                dve_ts(UU[ui][:], CXL[xi][:], GN[:, gi, kc:kc + 1], None, ALU.mult, None,
                       [CXLb[xi], b["GN"]], [UUb[ui]])
            dma(dap(dst, kc * 128 * ncol + c0, [[ncol, 128], [1, D]]), UU[ui][:], d_w[key], UUb[ui],
                queue="pool" if i % 2 == 0 else "sp")
            _prune(d_w[key])

        def conv_step():
            i = conv_state["done"]
            while conv_state["loaded"] < min(len(conv_jobs), i + 1 + CONV_LA):
                conv_load(conv_state["loaded"])
                conv_state["loaded"] += 1
            conv_compute(i)
            conv_state["done"] += 1

        bg_work = []
        conv_done = {}

        def phase0b():
            for (src, dst, ncol, gi, key) in wlist:
                for c0 in range(0, ncol, D):
                    for kc in range(8):
                        conv_jobs.append((src, dst, ncol, gi, key, kc, c0))
                        bg_work.append((key, c0 // D))

        def bg_pop(n=1):
            for _ in range(n):
                if bg_work:
                    kk = bg_work.pop(0)
                    conv_step()
                    conv_done[kk] = conv_done.get(kk, 0) + 1

        def ensure_converted(key, blk):
            kk = (key, blk // 2)
            while conv_done.get(kk, 0) < 8:
                assert bg_work, kk
                bg_pop()

        if dbg_stage is None or dbg_stage >= 1:
            phase0b()
            bg_pop(16)
        dve_copy(T1[0:16, 0:257], RB16[:, :], [b["RB16"]], [b["T1"]])
        dve_ts(T1[0:16, 257:512], ZER[0:16, 0:255], RB16[:, 256:257], None, ALU.add, None,
               [b["ZER"], b["RB16"]], [b["T1"]])
        dma(bx_s.ap(), T1[0:16, :], d_w["bx"], b["T1"])
        for h in range(16):
            hk, hkb = RR[h % 2], RRb[h % 2]
            pb, pbb = PB[h % 2], PBb[h % 2]
            dma(hk[:, 0:256], dap(bx_s, h * 512 + 1, [[1, 128], [1, 256]]), hkb, d_w["bx"])
            S.op("pe", lambda e, pb=pb, hk=hk: e.matmul(pb[:, 0:256], lhsT=JJ[:], rhs=hk[:, 0:256], start=True, stop=True),
                 [b["JJ"], hkb], [pbb])
            act(EB[:, h, 0:256], pb[:, 0:256], AF.Exp, [pbb, b["NCH"]], [b["EB"]], bias=NCH[:, h:h + 1])
        S.op("dve", lambda e: e.memset(EB[64:128, :, 0:64], 0.0), [], [b["EB"]])

        wsrc = {"w1": (w1s, 4096), "w2": (w2s, D), "w3": (w3s, 2048), "w4": (w4s, 2048), "w5": (w5s, D)}
        group_blocks = ([("w1", c) for c in range(8)] + [("w2", c) for c in range(2)] + [("w3", c) for c in (2, 3, 0, 1)]
                        + [("w4", c) for c in range(4)] + [("w5", c) for c in range(2)])
        all_blocks = group_blocks * (NSEQ * NG)
        wstate = {"issued": 0, "used": 0}

        def wissue(upto):
            while wstate["issued"] <= upto and wstate["issued"] < len(all_blocks):
                n = wstate["issued"]
                key, c = all_blocks[n]
                ensure_converted(key, c)
                t, ncol = wsrc[key]
                slot = n % NW
                dma(WB[slot][:], dap(t, c * 512, [[ncol, 128], [128 * ncol, 8], [1, 512]]), WBb[slot], d_w[key])
                _prune(d_w[key])
                wstate["issued"] += 1

        def wnext(expect, live_prev=0):
            n = wstate["used"]
            assert all_blocks[n] == expect, (all_blocks[n], expect)
            wissue(n + NW - 1 - live_prev)
            wstate["used"] += 1
            return WB[n % NW], WBb[n % NW]

        pbr = [0]

        def next_pb(lo, hi):
            i = lo + pbr[0] % (hi - lo)
            pbr[0] += 1
            return PB[i], PBb[i]

        xtr = [0]

        def next_xtl():
            i = xtr[0] % NXT
            xtr[0] += 1
            return XTL[i], XTLb[i]

        ptr = [0]

        def next_pt():
            i = ptr[0] % NPT
            ptr[0] += 1
            return PT[i], PTb[i]

        def rstd_from_ss(ss_ap, n, out_ap, stb):
            act(out_ap, ss_ap, AF.Ln, [stb, b["EPSC"]], [stb], scale=1.0 / n, bias=EPSC[:, 0:1])
            act(out_ap, out_ap, AF.Exp, [stb], [stb], scale=-0.5)

        def norm_part(src_tile, src_buf, gi):
            ui = gi % 3
            yj = gi % 2
            st_, stb = STN[ui], STNb[ui]
            act(YN[yj][:], src_tile[:], AF.Square, [src_buf], [YNb[yj], stb], accum_out=st_[:, 0:1])
            rstd_from_ss(st_[:, 0:1], D, st_[:, 1:2], stb)
            act(UU[ui][:], src_tile[:], AF.Copy, [src_buf, stb], [UUb[ui]], scale=st_[:, 1:2])

        def tr_part(tcol, gi, XD, XDb):
            ui = gi % 3
            for kc in range(8):
                S.op("pe", lambda e, kc=kc: e.transpose(TRB[:, kc * 128:(kc + 1) * 128],
                                                        UU[ui][:, kc * 128:(kc + 1) * 128], IDb16[:]),
                     [UUb[ui], b["ID"]], [TRBb])
            evac(XD[:, :, tcol:tcol + 128], TRB[:].rearrange("p (k t) -> p k t", t=128), [TRBb], [XDb])

        def norm_transpose(src_tile, src_buf, tcol, gi, XD, XDb):
            norm_part(src_tile, src_buf, gi)
            tr_part(tcol, gi, XD, XDb)

        def proj_fm(wb, wbb, c, post, XS=None, XSb=None):
            if XS is None:
                XS, XSb = XT, b["XT"]
            pb, pbb = next_pb(0, 3)
            for kc in range(8):
                mm(pb[:, :], wb[:, kc, c * 128:(c + 1) * 128], XS[:, kc, :], kc == 0, [wbb, XSb], [pbb])
            post(pb, pbb)
            bg_pop()

        def outproj_norm_residual(w_key, gp_tile, gp_buf, s, g, layer, XS, XSb):
            w0, w0b = wnext((w_key, 0))
            w1_, w1b = wnext((w_key, 1), live_prev=1)
            xts = {}
            bankss = {}
            ob7 = [0]

            def mm_tile(t):
                row0 = g * GT + t * 128
                xt, xtb = next_xtl()
                if layer == 0:
                    dma(xt[:], x_ap[s, row0:row0 + 128, :], xtb, d_in)
                else:
                    dma(xt[:], out_ap[s, row0:row0 + 128, :], xtb, d_out[s][g][t])
                xts[t] = (xt, xtb)
                banks = []
                for hb, (w, wbuf) in enumerate([(w0, w0b), (w1_, w1b)]):
                    bi_ = ob7[0] % 6
                    ob7[0] += 1
                    pb, pbb = PB[bi_], PBb[bi_]
                    for fc in range(8):
                        mm(pb[:, :], XS[:, fc, t * 128:(t + 1) * 128], w[:, fc, :], fc == 0, [XSb, wbuf], [pbb])
                    banks.append((pb, pbb))
                bankss[t] = banks

            def post_tile(t):
                row0 = g * GT + t * 128
                xt, xtb = xts.pop(t)
                banks = bankss.pop(t)
                yi = t % 2
                so_, sob = STO[yi], STOb[yi]
                for hb, (pb, pbb) in enumerate(banks):
                    act(YN[yi][:, hb * 512:(hb + 1) * 512], pb[:, :], AF.Square, [pbb], [YNb[yi], sob],
                        accum_out=so_[:, 2 + hb:3 + hb])
                dve_tt(so_[:, 0:1], so_[:, 2:3], so_[:, 3:4], ALU.add, [sob], [sob])
                rstd_from_ss(so_[:, 0:1], D, so_[:, 1:2], sob)
                for hb, (pb, pbb) in enumerate(banks):
                    S.op("dve", lambda e, pb=pb, hb=hb, yi=yi, so_=so_: e.scalar_tensor_tensor(
                        out=YN[yi][:, hb * 512:(hb + 1) * 512], in0=pb[:, :], scalar=so_[:, 1:2],
                        in1=gp_tile[:, hb * 512:(hb + 1) * 512], op0=ALU.mult, op1=ALU.mult),
                        [pbb, sob, gp_buf], [YNb[yi]])
                dve_tt(xt[:], xt[:], YN[yi][:], ALU.add, [xtb, YNb[yi]], [xtb])
                dma(out_ap[s, row0:row0 + 128, :], xt[:], d_out[s][g][t], xtb, queue="pool")
                if layer == 0:
                    norm_part(xt, xtb, t)

            NT = DBG_TILES
            INFL = 2
            for t in range(min(INFL, NT)):
                mm_tile(t)
            for t in range(NT):
                post_tile(t)
                if t + INFL < NT:
                    mm_tile(t + INFL)
                if layer == 0:
                    tr_part(t * 128, t, X2T, b["X2T"])

        glist = [(s, g) for s in range(NSEQ) for g in range(NG)]
        if dbg_groups is not None:
            glist = glist[:dbg_groups]

        def run_all():
            stage(0)
            stage(1)
            pre(*glist[0])
            for gi_, (s, g) in enumerate(glist):
                do_group(s, g, glist[gi_ + 1] if gi_ + 1 < len(glist) else None)

        rope_hook = [None]

        def ld_norm(s, g, t):
            xt, xtb = next_xtl()
            dma(xt[:], x_ap[s, g * GT + t * 128: g * GT + (t + 1) * 128, :], xtb, d_in)
            norm_part(xt, xtb, t)

        def pre(s, g, with_x=True):
            if True:
                tok0 = g * GT
                dma(POSI[:], dap(pos_d, s * SEQ + tok0, [[0, 128], [1, GT]]), b["POSI"], d_in)
                dve_copy(T0[:], POSI[:], [b["POSI"]], [b["T0"]])
                dve_ts(T0[:], T0[:], RTAB[:, 0:1], None, ALU.mult, None, [b["T0"], b["RTAB"]], [b["T0"]])
                TWO_PI = 2.0 * math.pi
                for (ang, angb, shift) in ((TA[0], TAb[0], 0.0), (TA[1], TAb[1], math.pi / 2)):
                    dve_ts(T1[:], T0[:], shift, 1.0 / TWO_PI, ALU.add, ALU.mult, [b["T0"]], [b["T1"]])
                    dve_copy(POSI[:], T1[:], [b["T1"]], [b["POSI"]])
                    dve_copy(T1[:], POSI[:], [b["POSI"]], [b["T1"]])
                    dve_ts(T1[:], T1[:], -TWO_PI, shift, ALU.mult, ALU.add, [b["T1"]], [b["T1"]])
                    dve_tt(T1[:], T1[:], T0[:], ALU.add, [b["T1"], b["T0"]], [b["T1"]])
                    dve_ts(RS[:], T1[:], math.pi, TWO_PI, ALU.is_gt, ALU.mult, [b["T1"]], [b["RS"]])
                    dve_tt(T1[:], T1[:], RS[:], ALU.subtract, [b["T1"], b["RS"]], [b["T1"]])
                    dve_ts(RS[:], T1[:], -math.pi, TWO_PI, ALU.is_lt, ALU.mult, [b["T1"]], [b["RS"]])
                    dve_tt(ang[:], T1[:], RS[:], ALU.add, [b["T1"], b["RS"]], [angb])

                def rope_fin():
                    act(SIN[:], TA[0][:], AF.Sin, [TAb[0]], [b["SIN"]])
                    act(COS[:], TA[1][:], AF.Sin, [TAb[1]], [b["COS"]])
                    dve_ts(SIN[:], SIN[:], RTAB[:, 1:2], None, ALU.mult, None, [b["SIN"], b["RTAB"]], [b["SIN"]])
                    dve_ts(COS[:], COS[:], RTAB[:, 1:2], RTAB[:, 2:3], ALU.mult, ALU.add, [b["COS"], b["RTAB"]], [b["COS"]])
                if with_x:
                    rope_fin()
                else:
                    rope_hook[0] = rope_fin

                if with_x:
                    for t in range(4):
                        ld_norm(s, g, t)
                        tr_part(t * 128, t, XT, b["XT"])

        def do_group(s, g, nxt):
            if True:
                tok0 = g * GT
                kcol0 = (g % 2) * 512
                for blk in range(2):
                    wb, wbb = wnext(("w1", blk))
                    for c in range(4):
                        fc = blk * 4 + c
                        proj_fm(wb, wbb, c, lambda pb, pbb, fc=fc: evac(QT[:, fc, :], pb[:, :], [pbb], [b["QT"]]))
                for blk in range(2):
                    wb, wbb = wnext(("w1", 2 + blk))
                    for c in range(4):
                        fc = blk * 4 + c
                        proj_fm(wb, wbb, c, lambda pb, pbb, fc=fc: evac(KT[:, fc, kcol0:kcol0 + 512], pb[:, :],
                                                                        [pbb], [b["KT"]]))
                for blk in range(2):
                    wb, wbb = wnext(("w1", 4 + blk))
                    for t in range(4):
                        slot = (4 * g + t) % 8
                        pb, pbb = next_pb(0, 3)
                        for kc in range(8):
                            mm(pb[:, :], XT[:, kc, t * 128:(t + 1) * 128], wb[:, kc, :], kc == 0,
                               [b["XT"], wbb], [pbb])
                        src = pb[:, :].rearrange("p (i c) -> p i c", c=128)
                        dst = VA[:, slot, blk * 768:(blk + 1) * 768].rearrange("p (i c) -> p i c", c=192)
                        act(dst[:, :, 0:64], src[:, :, 0:64], AF.Copy, [pbb], [b["VA"]])
                        dve_copy(dst[:, :, 128:192], src[:, :, 64:128], [pbb], [b["VA"]])
                for blk in range(2):
                    wb, wbb = wnext(("w1", 6 + blk))
                    for c in range(4):
                        fc = blk * 4 + c
                        proj_fm(wb, wbb, c, lambda pb, pbb, fc=fc: act(SGT[:, fc, :], pb[:, :], AF.Silu,
                                                                       [pbb], [b["SGT"]]))

                stage(2)
                items = []
                tl_ = []
                for Tk in range(max(0, 4 * g - 4), 4 * g + 4):
                    qlo = max(Tk, 4 * g)
                    qhi = min(Tk + 4, 4 * g + 3)
                    tl_.append((Tk, (qlo - 4 * g) * 128, (qhi + 1 - 4 * g) * 128, qlo - Tk))
                for i_ in range(8):
                    for n, (Tk, c0, c1, jlo) in enumerate(tl_):
                        for h in (2 * i_, 2 * i_ + 1):
                            items.append((h, n, Tk, c0, c1, jlo, n == len(tl_) - 1))
                DEPTH_A = 4
                sbank = {}
                sA = [0]
                postq = []

                def qkA(idx):
                    h, n, Tk, c0, c1, jlo, last = items[idx]
                    i, r0 = h // 2, 64 * (h % 2)
                    bi_ = sA[0] % 4
                    sA[0] += 1
                    pb, pbb = PB[bi_], PBb[bi_]
                    kc0 = (Tk % 8) * 128
                    mm(pb[:, c0:c1], KT[r0:r0 + 64, i, kc0:kc0 + 128], QT[r0:r0 + 64, i, c0:c1], True,
                       [b["KT"], b["QT"]], [pbb])
                    sbank[idx] = (pb, pbb)

                def finA(idx):
                    h, n, Tk, c0, c1, jlo, last = items[idx]
                    i, half = h // 2, h % 2
                    r0 = 64 * half
                    ob, obb = PB[4 + h % 3], PBb[4 + h % 3]
                    pb, pbb = sbank.pop(idx)
                    pt, ptb = next_pt()
                    act(pt[:, c0:c1], pb[:, c0:c1], AF.Exp, [pbb], [ptb], scale=0.125)
                    jhi = jlo + (c1 - c0) // 128 - 1
                    if jlo <= 1:
                        nb_ = (min(jhi, 1) - jlo + 1) * 128
                        dve_tt(pt[:, c0:c0 + nb_], pt[:, c0:c0 + nb_], EB[:, h, jlo * 128: jlo * 128 + nb_],
                               ALU.mult, [ptb, b["EB"]], [ptb])
                    if jhi == 4:
                        S.op("dve", lambda e, pt=pt, c1=c1: e.memset(pt[0:64, c1 - 64:c1], 0.0), [], [ptb])
                    vcol = i * 192 + 64 * half
                    mm(ob[:, c0:c1], VA[:, Tk % 8, vcol:vcol + 128], pt[:, c0:c1], n == 0, [b["VA"], ptb], [obb])
                    if last:
                        def postA(h=h, i=i, r0=r0, ob=ob, obb=obb):
                            so = 64 - r0
                            rr, rrb = RR[h % 2], RRb[h % 2]
                            act(rr[so:so + 64, :], ob[so:so + 64, :], AF.Ln, [obb], [rrb])
                            act(rr[so:so + 64, :], rr[so:so + 64, :], AF.Exp, [rrb], [rrb], scale=-1.0)
                            ta, tab_ = TA[h % 2], TAb[h % 2]
                            dve_tt(ta[r0:r0 + 64, :], ob[r0:r0 + 64, :], rr[so:so + 64, :], ALU.mult, [obb, rrb], [tab_])
                            dve_tt(XT[r0:r0 + 64, i, :], ta[r0:r0 + 64, :], SGT[r0:r0 + 64, i, :], ALU.mult,
                                   [tab_, b["SGT"]], [b["XT"]])
                            bg_pop(3)
                        postq.append([2, postA])

                def tickA():
                    for it in postq:
                        it[0] -= 1
                    while postq and postq[0][0] < 0:
                        postq.pop(0)[1]()

                for idx in range(0, len(items) + DEPTH_A, 2):
                    for k_ in (idx - DEPTH_A, idx + 1 - DEPTH_A):
                        if 0 <= k_ < len(items):
                            tickA()
                            finA(k_)
                    for k_ in (idx, idx + 1):
                        if k_ < len(items):
                            qkA(k_)
                while postq:
                    postq.pop(0)[1]()

                stage(3)
                outproj_norm_residual("w2", GPA, b["GPA"], s, g, 0, XT, b["XT"])

                stage(4)
                rp = [0]

                def rope_post(pb, pbb, dst_ap, dst_buf):
                    k = rp[0] % 2
                    rp[0] += 1
                    kr, krb, ta, tab_, tb, tbb = KRAW[k], KRAWb[k], TA[k], TAb[k], TBB[k], TBBb[k]
                    act(kr[:], pb[:, :], AF.Copy, [pbb], [krb])
                    p2, p2b = PB[3 + k], PBb[3 + k]
                    mm(p2[:, :], PERMb[:], kr[:], True, [b["PERM"], krb], [p2b])
                    dve_tt(ta[:], pb[:, :], COS[:], ALU.mult, [pbb, b["COS"]], [tab_])
                    dve_tt(tb[:], p2[:, :], SIN[:], ALU.mult, [p2b, b["SIN"]], [tbb])
                    dve_tt(dst_ap, ta[:], tb[:], ALU.add, [tab_, tbb], [dst_buf])

                if nxt is not None:
                    for t_ in range(3):
                        ld_norm(nxt[0], nxt[1], t_)
                vblocks = [wnext(("w3", 2)), wnext(("w3", 3), live_prev=1)]
                for t in range(4):
                    vs, vsb = VST[t % 2], VSTb[t % 2]
                    for blk in range(2):
                        wb, wbb = vblocks[blk]
                        pb, pbb = next_pb(0, 3)
                        for kc in range(8):
                            mm(pb[:, :], X2T[:, kc, t * 128:(t + 1) * 128], wb[:, kc, :], kc == 0,
                               [b["X2T"], wbb], [pbb])
                        evac(vs[:, blk * 512:(blk + 1) * 512], pb[:, :], [pbb], [vsb])
                    dma(dap(vsh_s, (s * 8 * SEQ + tok0 + t * 128) * 128, [[128, 128], [SEQ * 128, 8], [1, 128]]),
                        vs[:].rearrange("p (h v) -> p h v", v=128), d_vsh[s][g], vsb, queue="pool")
                    _prune(d_vsh[s][g])

                for blk in range(2):
                    wb, wbb = wnext(("w3", blk))
                    for c in range(4):
                        hh = blk * 4 + c
                        ks, ksb = KST[hh % 2], KSTb[hh % 2]
                        proj_fm(wb, wbb, c, lambda pb, pbb, ks=ks, ksb=ksb: rope_post(pb, pbb, ks[:], ksb),
                                X2T, b["X2T"])
                        dma(dap(ksh_s, ((s * 8 + hh) * 128) * SEQ + tok0, [[SEQ, 128], [1, GT]]), ks[:],
                            d_ksh[s][g], ksb, queue="pool")
                        _prune(d_ksh[s][g])
                if nxt is not None:
                    tr_part(0, 0, XT, b["XT"])
                    ld_norm(nxt[0], nxt[1], 3)
                    tr_part(128, 1, XT, b["XT"])
                    tr_part(256, 2, XT, b["XT"])
                for blk in range(2):
                    wb, wbb = wnext(("w4", blk))
                    for c in range(4):
                        hh = blk * 4 + c
                        proj_fm(wb, wbb, c, lambda pb, pbb, hh=hh: rope_post(pb, pbb, QT[:, hh, :], b["QT"]),
                                X2T, b["X2T"])
                for blk in range(2):
                    wb, wbb = wnext(("w4", 2 + blk))
                    for c in range(4):
                        hh = blk * 4 + c

                        def gpost(pb, pbb, hh=hh):
                            act(T0[:], pb[:, :], AF.Silu, [pbb], [b["T0"]])
                            dve_ts(SGT[:, hh, :], T0[:], SUBS[:, 0:1], None, ALU.mult, None,
                                   [b["T0"], b["SUBS"]], [b["SGT"]])
                        proj_fm(wb, wbb, c, gpost, X2T, b["X2T"])

                stage(5)
                if nxt is not None:
                    pre(nxt[0], nxt[1], with_x=False)
                    tr_part(384, 3, XT, b["XT"])
                ntile = 4 * g + 4
                O0, O0b, O1, O1b = PB[3], PBb[3], PB[4], PBb[4]
                Z0, Z0b, Z1, Z1b = PB[5], PBb[5], PB[6], PBb[6]
                itemsB = [(hh, n) for hh in range(8) for n in range(ntile)]
                kvslot = {}
                sbankB = {}
                pend_d1 = [None]
                pend_d2 = [None]
                ssb = [None]

                def loadB(hh):
                    kb, kbb = KB[hh % 2], KBb[hh % 2]
                    vb, vbb = VB[hh % 2], VBb[hh % 2]
                    for gg in range(g + 1):
                        dma(kb[:, gg * GT:(gg + 1) * GT],
                            dap(ksh_s, ((s * 8 + hh) * 128) * SEQ + gg * GT, [[SEQ, 128], [1, GT]]),
                            kbb, d_ksh[s][gg])
                        dma(vb[:, gg * 4:(gg + 1) * 4, :],
                            dap(vsh_s, ((s * 8 + hh) * SEQ + gg * GT) * 128, [[128, 128], [128 * 128, 4], [1, 128]]),
                            vbb, d_vsh[s][gg])
                    kvslot[hh] = (kb, kbb, vb, vbb)

                def qkB(idx):
                    hh, n = itemsB[idx]
                    if n == 0:
                        loadB(hh)
                    kb, kbb, vb, vbb = kvslot[hh]
                    c0 = max(0, n - 4 * g) * 128
                    res = []
                    for m in range(2):
                        pb, pbb = next_pb(0, 3)
                        mm(pb[:, c0:GT], kb[64 * m:64 * m + 64, n * 128:(n + 1) * 128],
                           QT[64 * m:64 * m + 64, hh, c0:GT], True, [kbb, b["QT"]], [pbb])
                        res.append((pb, pbb))
                    sbankB[idx] = res

                def expB(idx):
                    hh, n = itemsB[idx]
                    c0 = max(0, n - 4 * g) * 128
                    res = sbankB.pop(idx)
                    pts = []
                    for m in range(2):
                        pb, pbb = res[m]
                        pt, ptb = next_pt()
                        act(pt[:, c0:GT], pb[:, c0:GT], AF.Exp, [pbb], [ptb], scale=0.125)
                        if n >= 4 * g:
                            S.op("dve", lambda e, pt=pt, c0=c0: e.memset(pt[64:128, c0:c0 + 64], 0.0), [], [ptb])
                        pts.append((pt, ptb))
                    return pts

                def pvB(idx, pts):
                    hh, n = itemsB[idx]
                    kb, kbb, vb, vbb = kvslot[hh]
                    c0 = max(0, n - 4 * g) * 128
                    for m, (O, Ob, Z, Zb) in enumerate([(O0, O0b, Z0, Z0b), (O1, O1b, Z1, Z1b)]):
                        pt, ptb = pts[m]
                        mm(O[:, c0:GT], vb[:, n, :], pt[:, c0:GT], n == 0, [vbb, ptb], [Ob])
                        mm(Z[:, c0:GT], ONESb[:], pt[:, c0:GT], n == 0, [b["ONES"], ptb], [Zb])
                    if n == 2 and rope_hook[0] is not None:
                        rope_hook[0]()
                        rope_hook[0] = None
                    if n == 1 and pend_d1[0] is not None:
                        pend_d1[0]()
                        pend_d1[0] = None
                    if n == 3 and pend_d2[0] is not None:
                        pend_d2[0]()
                        pend_d2[0] = None
                    if n == ntile - 1:
                        act(RR[0][:], Z0[:, :], AF.Ln, [Z0b], [RRb[0]])
                        act(RR[1][:], Z1[:, :], AF.Ln, [Z1b], [RRb[1]])
                        act(RR[0][:], RR[0][:], AF.Exp, [RRb[0]], [RRb[0]], scale=-1.0)
                        act(RR[1][:], RR[1][:], AF.Exp, [RRb[1]], [RRb[1]], scale=-1.0)
                        dve_tt(T0[:], O0[:, :], RR[0][:], ALU.mult, [O0b, RRb[0]], [b["T0"]])
                        dve_tt(T1[:], O1[:, :], RR[1][:], ALU.mult, [O1b, RRb[1]], [b["T1"]])
                        S.op("dve", lambda e: e.scalar_tensor_tensor(out=T0[:], in0=T1[:], scalar=NEGLAM, in1=T0[:],
                                                                     op0=ALU.mult, op1=ALU.add),
                             [b["T0"], b["T1"], b["LAMS"]], [b["T0"]])

                        def d1():
                            act(SQ[:], T0[:], AF.Square, [b["T0"]], [b["SQ"]])
                            pb, pbb = next_pb(0, 3)
                            mm(pb[:, :], ONESb[:], SQ[:], True, [b["ONES"], b["SQ"]], [pbb])
                            dve_copy(RS[:], pb[:, :], [pbb], [b["RS"]])

                        def d2(hh=hh):
                            act(RS[:], RS[:], AF.Ln, [b["RS"], b["EPSC"]], [b["RS"]], scale=1.0 / 128, bias=EPSC[:, 0:1])
                            act(RS[:], RS[:], AF.Exp, [b["RS"]], [b["RS"]], scale=-0.5)
                            dve_tt(T0[:], T0[:], RS[:], ALU.mult, [b["T0"], b["RS"]], [b["T0"]])
                            dve_tt(X2T[:, hh, :], T0[:], SGT[:, hh, :], ALU.mult, [b["T0"], b["SGT"]], [b["X2T"]])
                        pend_d1[0] = d1
                        pend_d2[0] = d2

                qkB(0)
                for idx in range(len(itemsB)):
                    pts = expB(idx)
                    if idx + 1 < len(itemsB):
                        qkB(idx + 1)
                    pvB(idx, pts)
                for pd_ in (pend_d1, pend_d2):
                    if pd_[0] is not None:
                        pd_[0]()
                        pd_[0] = None

                stage(6)
                outproj_norm_residual("w5", GPB, b["GPB"], s, g, 1, X2T, b["X2T"])

        try:
            run_all()
        except _Stop:
            pass
        outs = [d_out[s][g][t] for s in range(NSEQ) for g in range(NG) for t in range(4)]
        names = {"EB": (EB, [128, 16 * 256], BF16), "CH": (CH, [128, 16], F32), "LAMS": (LAMS, [128, 8], F32),
                 "XT": (XT, [128, 8 * GT], BF16), "X2T": (X2T, [128, 8 * GT], BF16), "QT": (QT, [128, 8 * GT], BF16),
                 "SGT": (SGT, [128, 8 * GT], BF16), "KT": (KT, [128, 8 * 1024], BF16), "VA": (VA, [128, 8 * 1536], BF16),
                 "COS": (COS, [128, GT], F32), "SIN": (SIN, [128, GT], F32), "GN": (GN, [128, 24], F32),
                 "PERM": (PERMb, [128, 128], BF16), "ST": (ST, [128, 16], F32), "YN0": (YN[0], [128, D], F32),
                 "YN1": (YN[1], [128, D], F32)}
        b["YN0"] = YNb[0]; b["YN1"] = YNb[1]
        for nm in dbg_dump:
            tl, shp, dt = names[nm]
            dd = nc.dram_tensor("dbg_" + nm, shp, dt, kind="ExternalOutput")
            db_ = Buf("dbg_" + nm)
            src = tl[:] if len(tl.shape) == 2 else tl[:].rearrange("p a b -> p (a b)")
            dma(dd.ap(), src, db_, b.get(nm, b.get("PERM")), queue="pool")
            outs.append(db_)
        S.final_wait("pool", outs)
        print("sbuf bytes remaining/partition:", nc.sbuf_bytes_remaining, flush=True)
        S.emit(st)
        print("instr counts:", {e: len(S.ops[e]) for e in ENGS}, "dma sems:", S.ndsem, flush=True)
    return nc


_CACHE = {}


def _consts():
    ident = np.eye(128, dtype=np.float32)
    ones = np.ones((128, 128), np.float32)
    perm = np.zeros((128, 128), np.float32)
    for base in (0, 64):
        for d in range(8):
            perm[base + d + 8, base + d] = -1.0
            perm[base + d, base + d + 8] = 1.0
    anti = np.ascontiguousarray(np.eye(128, dtype=np.float32)[::-1])
    cmat = np.stack([ident, ones, perm, anti]).astype(np.float32)
    rtab = np.zeros((128, 4), np.float32)
    inv = np.power(np.float32(ROPE_THETA), -np.arange(8, dtype=np.float32) * np.float32(2.0) / np.float32(16))
    for p in range(128):
        d = p % 64
        if d < 16:
            rtab[p, 0] = inv[d % 8]
            rtab[p, 1] = 1.0
        else:
            rtab[p, 2] = 1.0
        rtab[p, 3] = math.pi
    return cmat, rtab.astype(np.float32)


def kernel(x, positions, a_norm_pre, a_w_in, a_rel_bias, a_w_out, a_norm_post, kv_norm, kv_w,
           b_norm_pre, b_w_in, b_lambda_q1, b_lambda_k1, b_lambda_q2, b_lambda_k2, b_subln,
           b_w_out, b_norm_post):
    f = lambda a: np.ascontiguousarray(np.asarray(a, dtype=np.float32))
    x = f(x)
    positions = np.ascontiguousarray(np.asarray(positions, dtype=np.int32))
    cmat, rtab = _consts()
    shared = {
        "a_norm_pre": f(a_norm_pre).reshape(D), "a_w_in": f(a_w_in).reshape(D, 4096),
        "a_rel_bias": f(a_rel_bias).reshape(16, 257), "a_w_out": f(a_w_out).reshape(D, D),
        "a_norm_post": f(a_norm_post).reshape(D), "kv_norm": f(kv_norm).reshape(D),
        "kv_w": f(kv_w).reshape(D, 2048), "b_norm_pre": f(b_norm_pre).reshape(D),
        "b_w_in": f(b_w_in).reshape(D, 2048),
        "b_lam": np.stack([f(b_lambda_q1).reshape(64), f(b_lambda_k1).reshape(64),
                           f(b_lambda_q2).reshape(64), f(b_lambda_k2).reshape(64)]),
        "b_subln": f(b_subln).reshape(128), "b_w_out": f(b_w_out).reshape(D, D),
        "b_norm_post": f(b_norm_post).reshape(D), "cmat": cmat, "rtab": rtab,
    }
    if "nc" not in _CACHE:
        _CACHE["nc"] = build_program()
    nc = _CACHE["nc"]
    in_maps = []
    for c in range(NCORES):
        m = dict(shared)
        m["x"] = x[c * NSEQ:(c + 1) * NSEQ]
        m["pos"] = positions[c * NSEQ:(c + 1) * NSEQ]
        in_maps.append(m)
    res = run_bass_kernel_spmd(nc, in_maps, core_ids=list(range(NCORES)))
    return np.concatenate([np.asarray(r["out"]).reshape(NSEQ, SEQ, D) for r in res.results], axis=0).astype(np.float32)
```

```python
import math
from contextlib import ExitStack
import numpy as np
import concourse.bass as bass
import concourse.mybir as mybir
from concourse.bass_utils import run_bass_kernel_spmd

F32 = mybir.dt.float32
BF16 = mybir.dt.bfloat16
I32 = mybir.dt.int32
AF = mybir.ActivationFunctionType
ALU = mybir.AluOpType
AX = mybir.AxisListType

NCORES = 8
SEQ = 2048
D = 1024
NSEQ = 2
GT = 512
NG = SEQ // GT
EPS = 1e-6
LAM_INIT = 0.8 - 0.6 * math.exp(-0.3 * 1)
ROPE_THETA = 500000.0

ENGS = ("pe", "act", "dve", "pool", "sp")
DBG_TILES = 4


class Buf:
    __slots__ = ("name", "w", "r", "dsem", "excl")

    def __init__(self, name, excl=False):
        self.name = name
        self.w = None
        self.r = []
        self.dsem = None
        self.excl = excl


class Sched:
    def __init__(self, nc):
        self.nc = nc
        self.ops = {e: [] for e in ENGS}
        self.cnt = {e: 0 for e in ENGS}
        self.seen = {e: {} for e in ENGS}
        self.ndsem = 0
        self.dcnt = {}

    def _waits(self, eng, reads, writes):
        waits = {}

        def need(t):
            if t is None:
                return
            k, n = t
            if eng == "pe" and k == "pe":
                return
            if n > self.seen[eng].get(k, 0) and n > waits.get(k, 0):
                waits[k] = n

        for b in reads:
            need(b.w)
            if b.excl:
                for t in b.r:
                    if t[0] != eng:
                        need(t)
        for b in writes:
            need(b.w)
            for t in b.r:
                need(t)
        for k, n in waits.items():
            self.seen[eng][k] = n
        return list(waits.items())

    def op(self, eng, fn, reads=(), writes=()):
        waits = self._waits(eng, reads, writes)
        self.cnt[eng] += 1
        tick = (eng, self.cnt[eng])
        for b in reads:
            if len(b.r) > 64:
                _prune(b)
            b.r.append(tick)
        for b in writes:
            b.w = tick
            b.r = []
        self.ops[eng].append((fn, waits, tick))
        return tick

    def dma(self, fn, dst, src, queue="sp"):
        waits = self._waits(queue, [src], [dst])
        if dst.dsem is None:
            dst.dsem = "q%d" % self.ndsem
            self.ndsem += 1
            self.dcnt[dst.dsem] = 0
        self.dcnt[dst.dsem] += 16
        tick = (dst.dsem, self.dcnt[dst.dsem])
        if len(src.r) > 64:
            _prune(src)
        src.r.append(tick)
        dst.w = tick
        dst.r = []
        self.ops[queue].append((fn, waits, tick))
        return tick

    def final_wait(self, eng, bufs):
        waits = dict(self._waits(eng, [], bufs))
        for k, n in self.dcnt.items():
            if n > self.seen[eng].get(k, 0):
                waits[k] = n
                self.seen[eng][k] = n
        self.ops[eng].append((None, list(waits.items()), None))

    def emit(self, stack):
        nc = self.nc
        sems = {}
        for e in ENGS:
            sems[e] = stack.enter_context(nc.semaphore("s_" + e))
        for k in self.dcnt:
            sems[k] = stack.enter_context(nc.semaphore("s_" + k))
        block = stack.enter_context(nc.Block())

        def run(ename):
            def body(eng):
                for fn, waits, tick in self.ops[ename]:
                    for k, n in waits:
                        eng.wait_ge(sems[k], n)
                    if fn is None:
                        continue
                    ins = fn(eng)
                    k, n = tick
                    ins.then_inc(sems[k], 16 if k.startswith("q") else 1)
            return body

        block.tensor(run("pe"))
        block.scalar(run("act"))
        block.vector(run("dve"))
        block.gpsimd(run("pool"))
        block.sync(run("sp"))


def _prune(b):
    best = {}
    for k, n in b.r:
        if n > best.get(k, 0):
            best[k] = n
    b.r = list(best.items())


class _Stop(Exception):
    pass


def build_program(dbg_stage=None, dbg_groups=None, dbg_dump=()):
    def stage(k):
        if dbg_stage is not None and dbg_stage == k:
            raise _Stop()

    nc = bass.Bass("TRN2", target_bir_lowering=False)

    def din(name, shape, dt=F32):
        return nc.dram_tensor(name, list(shape), dt, kind="ExternalInput")

    x_d = din("x", [NSEQ, SEQ, D])
    pos_d = din("pos", [NSEQ, SEQ], I32)
    a_npre_d = din("a_norm_pre", [D])
    a_win_d = din("a_w_in", [D, 4096])
    a_rb_d = din("a_rel_bias", [16, 257])
    a_wout_d = din("a_w_out", [D, D])
    a_npost_d = din("a_norm_post", [D])
    kvn_d = din("kv_norm", [D])
    kvw_d = din("kv_w", [D, 2048])
    b_npre_d = din("b_norm_pre", [D])
    b_win_d = din("b_w_in", [D, 2048])
    lam_d = din("b_lam", [4, 64])
    subln_d = din("b_subln", [128])
    b_wout_d = din("b_w_out", [D, D])
    b_npost_d = din("b_norm_post", [D])
    cmat_d = din("cmat", [4, 128, 128])
    rtab_d = din("rtab", [128, 4])
    out_d = nc.dram_tensor("out", [NSEQ, SEQ, D], F32, kind="ExternalOutput")

    w1s = nc.dram_tensor("w1s", [D, 4096], BF16)
    w2s = nc.dram_tensor("w2s", [D, D], BF16)
    w3s = nc.dram_tensor("w3s", [D, 2048], BF16)
    w4s = nc.dram_tensor("w4s", [D, 2048], BF16)
    w5s = nc.dram_tensor("w5s", [D, D], BF16)
    bx_s = nc.dram_tensor("bx_s", [16, 512], F32)
    chs = nc.dram_tensor("chs", [16], F32)
    ksh_s = nc.dram_tensor("ksh_s", [NSEQ, 8, 128, SEQ], BF16)
    vsh_s = nc.dram_tensor("vsh_s", [NSEQ, 8, SEQ, 128], BF16)

    x_ap = x_d.ap(); out_ap = out_d.ap(); cmat_ap = cmat_d.ap(); rtab_ap = rtab_d.ap()

    def dap(t, offset, ap):
        return bass.AP(tensor=t, offset=offset, ap=[list(a) for a in ap])

    S = Sched(nc)
    with ExitStack() as st:
        def sb(name, shape, dt):
            return st.enter_context(nc.sbuf_tensor(name, list(shape), dt))

        def ps(name, shape, dt=F32):
            return st.enter_context(nc.psum_tensor(name, list(shape), dt))

        IDb16 = sb("IDb16", [128, 128], BF16); ONESb = sb("ONESb", [128, 128], BF16)
        PERMb = sb("PERMb", [128, 128], BF16); JJ = sb("JJ", [128, 128], F32)
        CST = sb("CST", [128, 128], F32)
        RTAB = sb("RTAB", [128, 4], F32)
        EB = sb("EB", [128, 16, 256], BF16)
        CH = sb("CH", [128, 16], F32)
        RB16 = sb("RB16", [16, 257], F32)
        GPA = sb("GPA", [128, D], F32); GPB = sb("GPB", [128, D], F32)
        GN = sb("GN", [128, 3, 8], F32)
        SUBS = sb("SUBS", [128, 1], F32)
        LAMT = sb("LAMT", [128, 4, 64], F32)
        LAMS = sb("LAMS", [128, 8], F32)
        EPSC = sb("EPSC", [128, 2], F32)
        ZER = sb("ZER", [128, 256], F32)
        ST = sb("ST", [128, 16], F32)
        STN = [sb("STN%d" % i, [128, 4], F32) for i in range(3)]
        STO = [sb("STO%d" % i, [128, 4], F32) for i in range(2)]
        NCH = sb("NCH", [128, 16], F32)
        NW = 3
        WB = [sb("WB%d" % i, [128, 8, 512], BF16) for i in range(NW)]
        NXT = 2
        NCX = 3
        XTL = [sb("XTL%d" % i, [128, D], F32) for i in range(NXT)]
        CXL = [sb("CXL%d" % i, [128, D], F32) for i in range(NCX)]
        YN = [sb("YN%d" % i, [128, D], F32) for i in range(2)]
        UU = [sb("UU%d" % i, [128, D], BF16) for i in range(3)]
        XT = sb("XT", [128, 8, GT], BF16)
        X2T = sb("X2T", [128, 8, GT], BF16)
        QT = sb("QT", [128, 8, GT], BF16)
        SGT = sb("SGT", [128, 8, GT], BF16)
        KT = sb("KT", [128, 8, 1024], BF16)
        VA = sb("VA", [128, 8, 1536], BF16)
        NPT = 4
        PT = [sb("PT%d" % i, [128, GT], BF16) for i in range(NPT)]
        RR = [sb("RR%d" % i, [128, GT], F32) for i in range(2)]
        T0 = sb("T0", [128, GT], F32); T1 = sb("T1", [128, GT], F32)
        TA = [sb("TA%d" % i, [128, GT], F32) for i in range(2)]
        TBB = [sb("TBB%d" % i, [128, GT], F32) for i in range(2)]
        SQ = sb("SQ", [128, GT], BF16); RS = sb("RS", [128, GT], F32)
        KRAW = [sb("KRAW%d" % i, [128, GT], BF16) for i in range(2)]
        COS = sb("COS", [128, GT], F32); SIN = sb("SIN", [128, GT], F32)
        POSI = sb("POSI", [128, GT], I32)
        KB = [sb("KB%d" % i, [128, SEQ], BF16) for i in range(2)]
        VB = [sb("VB%d" % i, [128, 16, 128], BF16) for i in range(2)]
        KST = [sb("KST%d" % i, [128, GT], BF16) for i in range(2)]
        VST = [sb("VST%d" % i, [128, D], BF16) for i in range(2)]

        PB = [ps("PB%d" % i, [128, 512], F32) for i in range(7)]
        TRB = ps("TRB", [128, 1024], BF16)

        b = {}
        def B(name, excl=False):
            b[name] = Buf(name, excl)
            return b[name]
        for nm in ["ID", "ONES", "PERM", "JJ", "CST", "RTAB", "EB", "CH", "GPA", "GPB", "GN", "SUBS", "LAMT",
                   "LAMS", "EPSC", "ZER", "ST", "XT", "X2T", "RB16", "QT", "SGT", "KT", "VA", "T0", "T1", "SQ", "RS",
                   "COS", "SIN", "POSI"]:
            B(nm)
        WBb = [Buf("WB%d" % i) for i in range(NW)]
        STNb = [Buf("STN%d" % i) for i in range(3)]
        STOb = [Buf("STO%d" % i) for i in range(2)]
        TAb = [Buf("TA%d" % i) for i in range(2)]
        TBBb = [Buf("TBB%d" % i) for i in range(2)]
        KRAWb = [Buf("KRAW%d" % i) for i in range(2)]
        B("NCH")
        XTLb = [Buf("XTL%d" % i) for i in range(NXT)]
        CXLb = [Buf("CXL%d" % i) for i in range(NCX)]
        YNb = [Buf("YN%d" % i) for i in range(2)]
        UUb = [Buf("UU%d" % i) for i in range(3)]
        PTb = [Buf("PT%d" % i) for i in range(NPT)]
        RRb = [Buf("RR%d" % i) for i in range(2)]
        KBb = [Buf("KB%d" % i) for i in range(2)]
        VBb = [Buf("VB%d" % i) for i in range(2)]
        KSTb = [Buf("KST%d" % i) for i in range(2)]
        VSTb = [Buf("VST%d" % i) for i in range(2)]
        PBb = [Buf("PB%d" % i, True) for i in range(7)]
        TRBb = Buf("TRB", True)
        d_in = Buf("d_in")
        d_w = {k: Buf("d_" + k) for k in ["w1", "w2", "w3", "w4", "w5", "bx", "chs"]}
        d_ksh = [[Buf("d_ksh%d_%d" % (s, g)) for g in range(NG)] for s in range(NSEQ)]
        d_vsh = [[Buf("d_vsh%d_%d" % (s, g)) for g in range(NG)] for s in range(NSEQ)]
        d_out = [[[Buf("d_out%d_%d_%d" % (s, g, t)) for t in range(4)] for g in range(NG)] for s in range(NSEQ)]

        def act(out, in_, func, reads, writes, **kw):
            S.op("act", lambda e: e.activation(out=out, in_=in_, func=func, **kw), reads, writes)

        def mm(out, lhsT, rhs, start, reads, writes):
            S.op("pe", lambda e: e.matmul(out, lhsT=lhsT, rhs=rhs, start=start, stop=False,
                                          skip_group_check=True), reads, writes)

        def dve_copy(out, in_, reads, writes):
            S.op("dve", lambda e: e.tensor_copy(out=out, in_=in_), reads, writes)

        def dve_tt(out, in0, in1, op, reads, writes, eng="dve"):
            S.op(eng, lambda e: e.tensor_tensor(out=out, in0=in0, in1=in1, op=op), reads, writes)

        def dve_ts(out, in0, s1, s2, op0, op1, reads, writes, eng="dve"):
            if op1 is None:
                S.op(eng, lambda e: e.tensor_scalar(out=out, in0=in0, scalar1=s1, scalar2=None, op0=op0), reads, writes)
            else:
                S.op(eng, lambda e: e.tensor_scalar(out=out, in0=in0, scalar1=s1, scalar2=s2, op0=op0, op1=op1),
                     reads, writes)

        def dma(out, in_, dst, src, queue="sp", slow=False):
            if slow:
                S.dma(lambda e: e.dma_start(out=out, in_=in_, allow_slow_non_contiguous=True), dst, src, queue)
            else:
                S.dma(lambda e: e.dma_start(out=out, in_=in_), dst, src, queue)

        evac_flip = [0]

        def evac(out, in_, reads, writes):
            evac_flip[0] ^= 1
            if evac_flip[0]:
                act(out, in_, AF.Copy, reads, writes)
            else:
                dve_copy(out, in_, reads, writes)

        for i, (t, nm) in enumerate([(IDb16, "ID"), (ONESb, "ONES"), (PERMb, "PERM")]):
            dma(CST[:], cmat_ap[i], b["CST"], d_in)
            dve_copy(t[:], CST[:], [b["CST"]], [b[nm]])
        dma(JJ[:], cmat_ap[3], b["JJ"], d_in)
        dma(RTAB[:], rtab_ap, b["RTAB"], d_in)
        dma(GPA[:], dap(a_npost_d, 0, [[0, 128], [1, D]]), b["GPA"], d_in)
        dma(GPB[:], dap(b_npost_d, 0, [[0, 128], [1, D]]), b["GPB"], d_in)
        for i, t in enumerate([a_npre_d, kvn_d, b_npre_d]):
            for kc in range(8):
                dma(GN[:, i, kc:kc + 1], dap(t, kc * 128, [[1, 128], [1, 1]]), b["GN"], d_in)
        dma(SUBS[:], dap(subln_d, 0, [[1, 128], [1, 1]]), b["SUBS"], d_in)
        dma(LAMT[:], dap(lam_d, 0, [[0, 128], [64, 4], [1, 64]]), b["LAMT"], d_in)
        dma(RB16[:], a_rb_d.ap(), b["RB16"], d_in)
        dma(dap(chs, 0, [[1, 16], [1, 1]]), RB16[:, 256:257], d_w["chs"], b["RB16"])
        dma(CH[:], dap(chs, 0, [[0, 128], [1, 16]]), b["CH"], d_w["chs"])
        dve_ts(NCH[:], CH[:], -1.0, None, ALU.mult, None, [b["CH"]], [b["NCH"]])
        S.op("dve", lambda e: e.memset(EPSC[:, 0:1], EPS), [], [b["EPSC"]])
        S.op("dve", lambda e: e.memset(EPSC[:, 1:2], 0.0), [], [b["EPSC"]])
        S.op("dve", lambda e: e.memset(ZER[:], 0.0), [], [b["ZER"]])
        S.op("pool", lambda e: e.memset(VA[:], 1.0), [], [b["VA"]])
        dve_tt(LAMT[:, 0, :], LAMT[:, 0, :], LAMT[:, 1, :], ALU.mult, [b["LAMT"]], [b["LAMT"]])
        dve_tt(LAMT[:, 2, :], LAMT[:, 2, :], LAMT[:, 3, :], ALU.mult, [b["LAMT"]], [b["LAMT"]])
        S.op("dve", lambda e: e.tensor_reduce(out=LAMS[:, 0:1], in_=LAMT[:, 0, :], axis=AX.X, op=ALU.add),
             [b["LAMT"]], [b["LAMS"]])
        S.op("dve", lambda e: e.tensor_reduce(out=LAMS[:, 1:2], in_=LAMT[:, 2, :], axis=AX.X, op=ALU.add),
             [b["LAMT"]], [b["LAMS"]])
        act(LAMS[:, 4:6], LAMS[:, 0:2], AF.Exp, [b["LAMS"]], [b["LAMS"]])
        dve_tt(LAMS[:, 2:3], LAMS[:, 4:5], LAMS[:, 5:6], ALU.subtract, [b["LAMS"]], [b["LAMS"]])
        dve_ts(LAMS[:, 3:4], LAMS[:, 2:3], LAM_INIT, -1.0, ALU.add, ALU.mult, [b["LAMS"]], [b["LAMS"]])
        NEGLAM = LAMS[:, 3:4]
        dve_ts(SUBS[:], SUBS[:], 1.0 - LAM_INIT, None, ALU.mult, None, [b["SUBS"]], [b["SUBS"]])

        dbg_t = {}
        wlist = [(a_win_d, w1s, 4096, 0, "w1"), (a_wout_d, w2s, D, None, "w2"), (kvw_d, w3s, 2048, 1, "w3"),
                 (b_win_d, w4s, 2048, 2, "w4"), (b_wout_d, w5s, D, None, "w5")]
        conv_jobs = []
        conv_state = {"loaded": 0, "done": 0}
        CONV_LA = NCX - 1

        def conv_load(i):
            src, dst, ncol, gi, key, kc, c0 = conv_jobs[i]
            xi = i % NCX
            dma(CXL[xi][:], dap(src, kc * 128 * ncol + c0, [[ncol, 128], [1, D]]), CXLb[xi], d_in,
                queue="sp" if i % 2 == 0 else "pool")

        def conv_compute(i):
            src, dst, ncol, gi, key, kc, c0 = conv_jobs[i]
            xi = i % NCX
            ui = i % 3
            if gi is None:
                evac(UU[ui][:], CXL[xi][:], [CXLb[xi]], [UUb[ui]])
            elif i % 2 == 0:
                act(UU[ui][:], CXL[xi][:], AF.Copy, [CXLb[xi], b["GN"]], [UUb[ui]], scale=GN[:, gi, kc:kc + 1])
            else:
                dve_ts(UU[ui][:], CXL[xi][:], GN[:, gi, kc:kc + 1], None, ALU.mult, None,
                       [CXLb[xi], b["GN"]], [UUb[ui]])
            dma(dap(dst, kc * 128 * ncol + c0, [[ncol, 128], [1, D]]), UU[ui][:], d_w[key], UUb[ui],
                queue="pool" if i % 2 == 0 else "sp")
            _prune(d_w[key])

        def conv_step():
            i = conv_state["done"]
            while conv_state["loaded"] < min(len(conv_jobs), i + 1 + CONV_LA):
                conv_load(conv_state["loaded"])
                conv_state["loaded"] += 1
            conv_compute(i)
            conv_state["done"] += 1

        bg_work = []
        conv_done = {}

        def phase0b():
            for (src, dst, ncol, gi, key) in wlist:
                for c0 in range(0, ncol, D):
                    for kc in range(8):
                        conv_jobs.append((src, dst, ncol, gi, key, kc, c0))
                        bg_work.append((key, c0 // D))

        def bg_pop(n=1):
            for _ in range(n):
                if bg_work:
                    kk = bg_work.pop(0)
                    conv_step()
                    conv_done[kk] = conv_done.get(kk, 0) + 1

        def ensure_converted(key, blk):
            kk = (key, blk // 2)
            while conv_done.get(kk, 0) < 8:
                assert bg_work, kk
                bg_pop()

        if dbg_stage is None or dbg_stage >= 1:
            phase0b()
            bg_pop(16)
        dve_copy(T1[0:16, 0:257], RB16[:, :], [b["RB16"]], [b["T1"]])
        dve_ts(T1[0:16, 257:512], ZER[0:16, 0:255], RB16[:, 256:257], None, ALU.add, None,
               [b["ZER"], b["RB16"]], [b["T1"]])
        dma(bx_s.ap(), T1[0:16, :], d_w["bx"], b["T1"])
        for h in range(16):
            hk, hkb = RR[h % 2], RRb[h % 2]
            pb, pbb = PB[h % 2], PBb[h % 2]
            dma(hk[:, 0:256], dap(bx_s, h * 512 + 1, [[1, 128], [1, 256]]), hkb, d_w["bx"])
            S.op("pe", lambda e, pb=pb, hk=hk: e.matmul(pb[:, 0:256], lhsT=JJ[:], rhs=hk[:, 0:256], start=True, stop=True),
                 [b["JJ"], hkb], [pbb])
            act(EB[:, h, 0:256], pb[:, 0:256], AF.Exp, [pbb, b["NCH"]], [b["EB"]], bias=NCH[:, h:h + 1])
        S.op("dve", lambda e: e.memset(EB[64:128, :, 0:64], 0.0), [], [b["EB"]])

        wsrc = {"w1": (w1s, 4096), "w2": (w2s, D), "w3": (w3s, 2048), "w4": (w4s, 2048), "w5": (w5s, D)}
        group_blocks = ([("w1", c) for c in range(8)] + [("w2", c) for c in range(2)] + [("w3", c) for c in (2, 3, 0, 1)]
                        + [("w4", c) for c in range(4)] + [("w5", c) for c in range(2)])
        all_blocks = group_blocks * (NSEQ * NG)
        wstate = {"issued": 0, "used": 0}

        def wissue(upto):
            while wstate["issued"] <= upto and wstate["issued"] < len(all_blocks):
                n = wstate["issued"]
                key, c = all_blocks[n]
                ensure_converted(key, c)
                t, ncol = wsrc[key]
                slot = n % NW
                dma(WB[slot][:], dap(t, c * 512, [[ncol, 128], [128 * ncol, 8], [1, 512]]), WBb[slot], d_w[key])
                _prune(d_w[key])
                wstate["issued"] += 1

        def wnext(expect, live_prev=0):
            n = wstate["used"]
            assert all_blocks[n] == expect, (all_blocks[n], expect)
            wissue(n + NW - 1 - live_prev)
            wstate["used"] += 1
            return WB[n % NW], WBb[n % NW]

        pbr = [0]

        def next_pb(lo, hi):
            i = lo + pbr[0] % (hi - lo)
            pbr[0] += 1
            return PB[i], PBb[i]

        xtr = [0]

        def next_xtl():
            i = xtr[0] % NXT
            xtr[0] += 1
            return XTL[i], XTLb[i]

        ptr = [0]

        def next_pt():
            i = ptr[0] % NPT
            ptr[0] += 1
            return PT[i], PTb[i]

        def rstd_from_ss(ss_ap, n, out_ap, stb):
            act(out_ap, ss_ap, AF.Ln, [stb, b["EPSC"]], [stb], scale=1.0 / n, bias=EPSC[:, 0:1])
            act(out_ap, out_ap, AF.Exp, [stb], [stb], scale=-0.5)

        def norm_part(src_tile, src_buf, gi):
            ui = gi % 3
            yj = gi % 2
            st_, stb = STN[ui], STNb[ui]
            act(YN[yj][:], src_tile[:], AF.Square, [src_buf], [YNb[yj], stb], accum_out=st_[:, 0:1])
            rstd_from_ss(st_[:, 0:1], D, st_[:, 1:2], stb)
            act(UU[ui][:], src_tile[:], AF.Copy, [src_buf, stb], [UUb[ui]], scale=st_[:, 1:2])

        def tr_part(tcol, gi, XD, XDb):
            ui = gi % 3
            for kc in range(8):
                S.op("pe", lambda e, kc=kc: e.transpose(TRB[:, kc * 128:(kc + 1) * 128],
                                                        UU[ui][:, kc * 128:(kc + 1) * 128], IDb16[:]),
                     [UUb[ui], b["ID"]], [TRBb])
            evac(XD[:, :, tcol:tcol + 128], TRB[:].rearrange("p (k t) -> p k t", t=128), [TRBb], [XDb])

        def norm_transpose(src_tile, src_buf, tcol, gi, XD, XDb):
            norm_part(src_tile, src_buf, gi)
            tr_part(tcol, gi, XD, XDb)

        def proj_fm(wb, wbb, c, post, XS=None, XSb=None):
            if XS is None:
                XS, XSb = XT, b["XT"]
            pb, pbb = next_pb(0, 3)
            for kc in range(8):
                mm(pb[:, :], wb[:, kc, c * 128:(c + 1) * 128], XS[:, kc, :], kc == 0, [wbb, XSb], [pbb])
            if rope_pend[0] is not None:
                rope_pend[0]()
                rope_pend[0] = None
            post(pb, pbb)
            bg_pop()

        def outproj_norm_residual(w_key, gp_tile, gp_buf, s, g, layer, XS, XSb):
            w0, w0b = wnext((w_key, 0))
            w1_, w1b = wnext((w_key, 1), live_prev=1)
            xts = {}
            bankss = {}
            ob7 = [0]

            def mm_tile(t):
                row0 = g * GT + t * 128
                xt, xtb = next_xtl()
                if layer == 0:
                    dma(xt[:], x_ap[s, row0:row0 + 128, :], xtb, d_in)
                else:
                    dma(xt[:], out_ap[s, row0:row0 + 128, :], xtb, d_out[s][g][t])
                xts[t] = (xt, xtb)
                banks = []
                for hb, (w, wbuf) in enumerate([(w0, w0b), (w1_, w1b)]):
                    bi_ = ob7[0] % 6
                    ob7[0] += 1
                    pb, pbb = PB[bi_], PBb[bi_]
                    for fc in range(8):
                        mm(pb[:, :], XS[:, fc, t * 128:(t + 1) * 128], w[:, fc, :], fc == 0, [XSb, wbuf], [pbb])
                    banks.append((pb, pbb))
                bankss[t] = banks

            def post_tile(t):
                row0 = g * GT + t * 128
                xt, xtb = xts.pop(t)
                banks = bankss.pop(t)
                yi = t % 2
                so_, sob = STO[yi], STOb[yi]
                for hb, (pb, pbb) in enumerate(banks):
                    act(YN[yi][:, hb * 512:(hb + 1) * 512], pb[:, :], AF.Square, [pbb], [YNb[yi], sob],
                        accum_out=so_[:, 2 + hb:3 + hb])
                dve_tt(so_[:, 0:1], so_[:, 2:3], so_[:, 3:4], ALU.add, [sob], [sob])
                rstd_from_ss(so_[:, 0:1], D, so_[:, 1:2], sob)
                for hb, (pb, pbb) in enumerate(banks):
                    S.op("dve", lambda e, pb=pb, hb=hb, yi=yi, so_=so_: e.scalar_tensor_tensor(
                        out=YN[yi][:, hb * 512:(hb + 1) * 512], in0=pb[:, :], scalar=so_[:, 1:2],
                        in1=gp_tile[:, hb * 512:(hb + 1) * 512], op0=ALU.mult, op1=ALU.mult),
                        [pbb, sob, gp_buf], [YNb[yi]])
                dve_tt(xt[:], xt[:], YN[yi][:], ALU.add, [xtb, YNb[yi]], [xtb])
                dma(out_ap[s, row0:row0 + 128, :], xt[:], d_out[s][g][t], xtb, queue="pool")
                if layer == 0:
                    norm_part(xt, xtb, t)

            NT = DBG_TILES
            INFL = 2
            for t in range(min(INFL, NT)):
                mm_tile(t)
            for t in range(NT):
                post_tile(t)
                if t + INFL < NT:
                    mm_tile(t + INFL)
                if layer == 0:
                    tr_part(t * 128, t, X2T, b["X2T"])

        glist = [(s, g) for s in range(NSEQ) for g in range(NG)]
        if dbg_groups is not None:
            glist = glist[:dbg_groups]

        def run_all():
            stage(0)
            stage(1)
            pre(*glist[0])
            for gi_, (s, g) in enumerate(glist):
                do_group(s, g, glist[gi_ + 1] if gi_ + 1 < len(glist) else None)

        rope_hook = [None]
        rope_pend = [None]

        def ld_norm(s, g, t):
            xt, xtb = next_xtl()
            dma(xt[:], x_ap[s, g * GT + t * 128: g * GT + (t + 1) * 128, :], xtb, d_in)
            norm_part(xt, xtb, t)

        def pre(s, g, with_x=True):
            if True:
                tok0 = g * GT
                dma(POSI[:], dap(pos_d, s * SEQ + tok0, [[0, 128], [1, GT]]), b["POSI"], d_in)
                dve_copy(T0[:], POSI[:], [b["POSI"]], [b["T0"]])
                dve_ts(T0[:], T0[:], RTAB[:, 0:1], None, ALU.mult, None, [b["T0"], b["RTAB"]], [b["T0"]])
                TWO_PI = 2.0 * math.pi
                for (ang, angb, shift) in ((TA[0], TAb[0], 0.0), (TA[1], TAb[1], math.pi / 2)):
                    dve_ts(T1[:], T0[:], shift, 1.0 / TWO_PI, ALU.add, ALU.mult, [b["T0"]], [b["T1"]])
                    dve_copy(POSI[:], T1[:], [b["T1"]], [b["POSI"]])
                    dve_copy(T1[:], POSI[:], [b["POSI"]], [b["T1"]])
                    dve_ts(T1[:], T1[:], -TWO_PI, shift, ALU.mult, ALU.add, [b["T1"]], [b["T1"]])
                    dve_tt(T1[:], T1[:], T0[:], ALU.add, [b["T1"], b["T0"]], [b["T1"]])
                    dve_ts(RS[:], T1[:], math.pi, TWO_PI, ALU.is_gt, ALU.mult, [b["T1"]], [b["RS"]])
                    dve_tt(T1[:], T1[:], RS[:], ALU.subtract, [b["T1"], b["RS"]], [b["T1"]])
                    dve_ts(RS[:], T1[:], -math.pi, TWO_PI, ALU.is_lt, ALU.mult, [b["T1"]], [b["RS"]])
                    dve_tt(ang[:], T1[:], RS[:], ALU.add, [b["T1"], b["RS"]], [angb])

                def rope_fin():
                    act(SIN[:], TA[0][:], AF.Sin, [TAb[0]], [b["SIN"]])
                    act(COS[:], TA[1][:], AF.Sin, [TAb[1]], [b["COS"]])
                    dve_ts(SIN[:], SIN[:], RTAB[:, 1:2], None, ALU.mult, None, [b["SIN"], b["RTAB"]], [b["SIN"]])
                    dve_ts(COS[:], COS[:], RTAB[:, 1:2], RTAB[:, 2:3], ALU.mult, ALU.add, [b["COS"], b["RTAB"]], [b["COS"]])
                if with_x:
                    rope_fin()
                else:
                    rope_hook[0] = rope_fin

                if with_x:
                    for t in range(4):
                        ld_norm(s, g, t)
                        tr_part(t * 128, t, XT, b["XT"])

        def do_group(s, g, nxt):
            if True:
                tok0 = g * GT
                kcol0 = (g % 2) * 512
                for blk in range(2):
                    wb, wbb = wnext(("w1", blk))
                    for c in range(4):
                        fc = blk * 4 + c
                        proj_fm(wb, wbb, c, lambda pb, pbb, fc=fc: evac(QT[:, fc, :], pb[:, :], [pbb], [b["QT"]]))
                for blk in range(2):
                    wb, wbb = wnext(("w1", 2 + blk))
                    for c in range(4):
                        fc = blk * 4 + c
                        proj_fm(wb, wbb, c, lambda pb, pbb, fc=fc: evac(KT[:, fc, kcol0:kcol0 + 512], pb[:, :],
                                                                        [pbb], [b["KT"]]))
                for blk in range(2):
                    wb, wbb = wnext(("w1", 4 + blk))
                    for t in range(4):
                        slot = (4 * g + t) % 8
                        pb, pbb = next_pb(0, 3)
                        for kc in range(8):
                            mm(pb[:, :], XT[:, kc, t * 128:(t + 1) * 128], wb[:, kc, :], kc == 0,
                               [b["XT"], wbb], [pbb])
                        src = pb[:, :].rearrange("p (i c) -> p i c", c=128)
                        dst = VA[:, slot, blk * 768:(blk + 1) * 768].rearrange("p (i c) -> p i c", c=192)
                        act(dst[:, :, 0:64], src[:, :, 0:64], AF.Copy, [pbb], [b["VA"]])
                        dve_copy(dst[:, :, 128:192], src[:, :, 64:128], [pbb], [b["VA"]])
                for blk in range(2):
                    wb, wbb = wnext(("w1", 6 + blk))
                    for c in range(4):
                        fc = blk * 4 + c
                        proj_fm(wb, wbb, c, lambda pb, pbb, fc=fc: act(SGT[:, fc, :], pb[:, :], AF.Silu,
                                                                       [pbb], [b["SGT"]]))

                stage(2)
                items = []
                tl_ = []
                for Tk in range(max(0, 4 * g - 4), 4 * g + 4):
                    qlo = max(Tk, 4 * g)
                    qhi = min(Tk + 4, 4 * g + 3)
                    tl_.append((Tk, (qlo - 4 * g) * 128, (qhi + 1 - 4 * g) * 128, qlo - Tk))
                for i_ in range(8):
                    for n, (Tk, c0, c1, jlo) in enumerate(tl_):
                        for h in (2 * i_, 2 * i_ + 1):
                            items.append((h, n, Tk, c0, c1, jlo, n == len(tl_) - 1))
                DEPTH_A = 4
                sbank = {}
                sA = [0]
                postq = []

                def qkA(idx):
                    h, n, Tk, c0, c1, jlo, last = items[idx]
                    i, r0 = h // 2, 64 * (h % 2)
                    bi_ = sA[0] % 4
                    sA[0] += 1
                    pb, pbb = PB[bi_], PBb[bi_]
                    kc0 = (Tk % 8) * 128
                    mm(pb[:, c0:c1], KT[r0:r0 + 64, i, kc0:kc0 + 128], QT[r0:r0 + 64, i, c0:c1], True,
                       [b["KT"], b["QT"]], [pbb])
                    sbank[idx] = (pb, pbb)

                def finA(idx):
                    h, n, Tk, c0, c1, jlo, last = items[idx]
                    i, half = h // 2, h % 2
                    r0 = 64 * half
                    ob, obb = PB[4 + h % 3], PBb[4 + h % 3]
                    pb, pbb = sbank.pop(idx)
                    pt, ptb = next_pt()
                    act(pt[:, c0:c1], pb[:, c0:c1], AF.Exp, [pbb], [ptb], scale=0.125)
                    jhi = jlo + (c1 - c0) // 128 - 1
                    if jlo <= 1:
                        nb_ = (min(jhi, 1) - jlo + 1) * 128
                        dve_tt(pt[:, c0:c0 + nb_], pt[:, c0:c0 + nb_], EB[:, h, jlo * 128: jlo * 128 + nb_],
                               ALU.mult, [ptb, b["EB"]], [ptb])
                    if jhi == 4:
                        S.op("dve", lambda e, pt=pt, c1=c1: e.memset(pt[0:64, c1 - 64:c1], 0.0), [], [ptb])
                    vcol = i * 192 + 64 * half
                    mm(ob[:, c0:c1], VA[:, Tk % 8, vcol:vcol + 128], pt[:, c0:c1], n == 0, [b["VA"], ptb], [obb])
                    if last:
                        def postA(h=h, i=i, r0=r0, ob=ob, obb=obb):
                            so = 64 - r0
                            rr, rrb = RR[h % 2], RRb[h % 2]
                            act(rr[so:so + 64, :], ob[so:so + 64, :], AF.Ln, [obb], [rrb])
                            act(rr[so:so + 64, :], rr[so:so + 64, :], AF.Exp, [rrb], [rrb], scale=-1.0)
                            ta, tab_ = TA[h % 2], TAb[h % 2]
                            dve_tt(ta[r0:r0 + 64, :], ob[r0:r0 + 64, :], rr[so:so + 64, :], ALU.mult, [obb, rrb], [tab_])
                            dve_tt(XT[r0:r0 + 64, i, :], ta[r0:r0 + 64, :], SGT[r0:r0 + 64, i, :], ALU.mult,
                                   [tab_, b["SGT"]], [b["XT"]])
                            bg_pop(3)
                        postq.append([2, postA])

                def tickA():
                    for it in postq:
                        it[0] -= 1
                    while postq and postq[0][0] < 0:
                        postq.pop(0)[1]()

                for idx in range(0, len(items) + DEPTH_A, 2):
                    for k_ in (idx - DEPTH_A, idx + 1 - DEPTH_A):
                        if 0 <= k_ < len(items):
                            tickA()
                            finA(k_)
                    for k_ in (idx, idx + 1):
                        if k_ < len(items):
                            qkA(k_)
                while postq:
                    postq.pop(0)[1]()

                stage(3)
                outproj_norm_residual("w2", GPA, b["GPA"], s, g, 0, XT, b["XT"])

                stage(4)
                rp = [0]

                def rope_post(pb, pbb, dst_ap, dst_buf, after=None):
                    k = rp[0] % 2
                    rp[0] += 1
                    kr, krb, ta, tab_, tb, tbb = KRAW[k], KRAWb[k], TA[k], TAb[k], TBB[k], TBBb[k]
                    act(kr[:], pb[:, :], AF.Copy, [pbb], [krb])
                    dve_tt(ta[:], pb[:, :], COS[:], ALU.mult, [pbb, b["COS"]], [tab_])

                    def part2():
                        p2, p2b = PB[3 + k], PBb[3 + k]
                        mm(p2[:, :], PERMb[:], kr[:], True, [b["PERM"], krb], [p2b])
                        dve_tt(tb[:], p2[:, :], SIN[:], ALU.mult, [p2b, b["SIN"]], [tbb])
                        dve_tt(dst_ap, ta[:], tb[:], ALU.add, [tab_, tbb], [dst_buf])
                        if after is not None:
                            after()
                    rope_pend[0] = part2

                if nxt is not None:
                    for t_ in range(3):
                        ld_norm(nxt[0], nxt[1], t_)
                vblocks = [wnext(("w3", 2)), wnext(("w3", 3), live_prev=1)]
                for t in range(4):
                    vs, vsb = VST[t % 2], VSTb[t % 2]
                    for blk in range(2):
                        wb, wbb = vblocks[blk]
                        pb, pbb = next_pb(0, 3)
                        for kc in range(8):
                            mm(pb[:, :], X2T[:, kc, t * 128:(t + 1) * 128], wb[:, kc, :], kc == 0,
                               [b["X2T"], wbb], [pbb])
                        evac(vs[:, blk * 512:(blk + 1) * 512], pb[:, :], [pbb], [vsb])
                    dma(dap(vsh_s, (s * 8 * SEQ + tok0 + t * 128) * 128, [[128, 128], [SEQ * 128, 8], [1, 128]]),
                        vs[:].rearrange("p (h v) -> p h v", v=128), d_vsh[s][g], vsb, queue="pool")
                    _prune(d_vsh[s][g])

                for blk in range(2):
                    wb, wbb = wnext(("w3", blk))
                    for c in range(4):
                        hh = blk * 4 + c
                        ks, ksb = KST[hh % 2], KSTb[hh % 2]
                        def kstore(hh=hh, ks=ks, ksb=ksb):
                            dma(dap(ksh_s, ((s * 8 + hh) * 128) * SEQ + tok0, [[SEQ, 128], [1, GT]]), ks[:],
                                d_ksh[s][g], ksb, queue="pool")
                            _prune(d_ksh[s][g])
                        proj_fm(wb, wbb, c, lambda pb, pbb, ks=ks, ksb=ksb, kstore=kstore:
                                rope_post(pb, pbb, ks[:], ksb, kstore), X2T, b["X2T"])
                if nxt is not None:
                    tr_part(0, 0, XT, b["XT"])
                    ld_norm(nxt[0], nxt[1], 3)
                    tr_part(128, 1, XT, b["XT"])
                    tr_part(256, 2, XT, b["XT"])
                for blk in range(2):
                    wb, wbb = wnext(("w4", blk))
                    for c in range(4):
                        hh = blk * 4 + c
                        proj_fm(wb, wbb, c, lambda pb, pbb, hh=hh: rope_post(pb, pbb, QT[:, hh, :], b["QT"]),
                                X2T, b["X2T"])
                for blk in range(2):
                    wb, wbb = wnext(("w4", 2 + blk))
                    for c in range(4):
                        hh = blk * 4 + c

                        def gpost(pb, pbb, hh=hh):
                            act(T0[:], pb[:, :], AF.Silu, [pbb], [b["T0"]])
                            dve_ts(SGT[:, hh, :], T0[:], SUBS[:, 0:1], None, ALU.mult, None,
                                   [b["T0"], b["SUBS"]], [b["SGT"]])
                        proj_fm(wb, wbb, c, gpost, X2T, b["X2T"])

                stage(5)
                if rope_pend[0] is not None:
                    rope_pend[0]()
                    rope_pend[0] = None
                if nxt is not None:
                    pre(nxt[0], nxt[1], with_x=False)
                    tr_part(384, 3, XT, b["XT"])
                ntile = 4 * g + 4
                O0, O0b, O1, O1b = PB[3], PBb[3], PB[4], PBb[4]
                Z0, Z0b, Z1, Z1b = PB[5], PBb[5], PB[6], PBb[6]
                itemsB = [(hh, n) for hh in range(8) for n in range(ntile)]
                kvslot = {}
                sbankB = {}
                pend_d1 = [None]
                pend_d2 = [None]
                ssb = [None]

                def loadB(hh):
                    kb, kbb = KB[hh % 2], KBb[hh % 2]
                    vb, vbb = VB[hh % 2], VBb[hh % 2]
                    for gg in range(g + 1):
                        dma(kb[:, gg * GT:(gg + 1) * GT],
                            dap(ksh_s, ((s * 8 + hh) * 128) * SEQ + gg * GT, [[SEQ, 128], [1, GT]]),
                            kbb, d_ksh[s][gg])
                        dma(vb[:, gg * 4:(gg + 1) * 4, :],
                            dap(vsh_s, ((s * 8 + hh) * SEQ + gg * GT) * 128, [[128, 128], [128 * 128, 4], [1, 128]]),
                            vbb, d_vsh[s][gg])
                    kvslot[hh] = (kb, kbb, vb, vbb)

                def qkB(idx):
                    hh, n = itemsB[idx]
                    if n == 0:
                        loadB(hh)
                    kb, kbb, vb, vbb = kvslot[hh]
                    c0 = max(0, n - 4 * g) * 128
                    res = []
                    for m in range(2):
                        pb, pbb = next_pb(0, 3)
                        mm(pb[:, c0:GT], kb[64 * m:64 * m + 64, n * 128:(n + 1) * 128],
                           QT[64 * m:64 * m + 64, hh, c0:GT], True, [kbb, b["QT"]], [pbb])
                        res.append((pb, pbb))
                    sbankB[idx] = res

                def expB(idx):
                    hh, n = itemsB[idx]
                    c0 = max(0, n - 4 * g) * 128
                    res = sbankB.pop(idx)
                    pts = []
                    for m in range(2):
                        pb, pbb = res[m]
                        pt, ptb = next_pt()
                        act(pt[:, c0:GT], pb[:, c0:GT], AF.Exp, [pbb], [ptb], scale=0.125)
                        if n >= 4 * g:
                            S.op("dve", lambda e, pt=pt, c0=c0: e.memset(pt[64:128, c0:c0 + 64], 0.0), [], [ptb])
                        pts.append((pt, ptb))
                    return pts

                def pvB(idx, pts):
                    hh, n = itemsB[idx]
                    kb, kbb, vb, vbb = kvslot[hh]
                    c0 = max(0, n - 4 * g) * 128
                    for m, (O, Ob, Z, Zb) in enumerate([(O0, O0b, Z0, Z0b), (O1, O1b, Z1, Z1b)]):
                        pt, ptb = pts[m]
                        mm(O[:, c0:GT], vb[:, n, :], pt[:, c0:GT], n == 0, [vbb, ptb], [Ob])
                        mm(Z[:, c0:GT], ONESb[:], pt[:, c0:GT], n == 0, [b["ONES"], ptb], [Zb])
                    if n == 2 and rope_hook[0] is not None:
                        rope_hook[0]()
                        rope_hook[0] = None
                    if n == 1 and pend_d1[0] is not None:
                        pend_d1[0]()
                        pend_d1[0] = None
                    if n == 3 and pend_d2[0] is not None:
                        pend_d2[0]()
                        pend_d2[0] = None
                    if n == ntile - 1:
                        act(RR[0][:], Z0[:, :], AF.Ln, [Z0b], [RRb[0]])
                        act(RR[1][:], Z1[:, :], AF.Ln, [Z1b], [RRb[1]])
                        act(RR[0][:], RR[0][:], AF.Exp, [RRb[0]], [RRb[0]], scale=-1.0)
                        act(RR[1][:], RR[1][:], AF.Exp, [RRb[1]], [RRb[1]], scale=-1.0)
                        dve_tt(T0[:], O0[:, :], RR[0][:], ALU.mult, [O0b, RRb[0]], [b["T0"]])
                        dve_tt(T1[:], O1[:, :], RR[1][:], ALU.mult, [O1b, RRb[1]], [b["T1"]])
                        S.op("dve", lambda e: e.scalar_tensor_tensor(out=T0[:], in0=T1[:], scalar=NEGLAM, in1=T0[:],
                                                                     op0=ALU.mult, op1=ALU.add),
                             [b["T0"], b["T1"], b["LAMS"]], [b["T0"]])

                        def d1():
                            act(SQ[:], T0[:], AF.Square, [b["T0"]], [b["SQ"]])
                            pb, pbb = next_pb(0, 3)
                            mm(pb[:, :], ONESb[:], SQ[:], True, [b["ONES"], b["SQ"]], [pbb])
                            dve_copy(RS[:], pb[:, :], [pbb], [b["RS"]])

                        def d2(hh=hh):
                            act(RS[:], RS[:], AF.Ln, [b["RS"], b["EPSC"]], [b["RS"]], scale=1.0 / 128, bias=EPSC[:, 0:1])
                            act(RS[:], RS[:], AF.Exp, [b["RS"]], [b["RS"]], scale=-0.5)
                            dve_tt(T0[:], T0[:], RS[:], ALU.mult, [b["T0"], b["RS"]], [b["T0"]])
                            dve_tt(X2T[:, hh, :], T0[:], SGT[:, hh, :], ALU.mult, [b["T0"], b["SGT"]], [b["X2T"]])
                        pend_d1[0] = d1
                        pend_d2[0] = d2

                qkB(0)
                for idx in range(len(itemsB)):
                    pts = expB(idx)
                    if idx + 1 < len(itemsB):
                        qkB(idx + 1)
                    pvB(idx, pts)
                for pd_ in (pend_d1, pend_d2):
                    if pd_[0] is not None:
                        pd_[0]()
                        pd_[0] = None

                stage(6)
                outproj_norm_residual("w5", GPB, b["GPB"], s, g, 1, X2T, b["X2T"])

        try:
            run_all()
        except _Stop:
            pass
        outs = [d_out[s][g][t] for s in range(NSEQ) for g in range(NG) for t in range(4)]
        names = {"EB": (EB, [128, 16 * 256], BF16), "CH": (CH, [128, 16], F32), "LAMS": (LAMS, [128, 8], F32),
                 "XT": (XT, [128, 8 * GT], BF16), "X2T": (X2T, [128, 8 * GT], BF16), "QT": (QT, [128, 8 * GT], BF16),
                 "SGT": (SGT, [128, 8 * GT], BF16), "KT": (KT, [128, 8 * 1024], BF16), "VA": (VA, [128, 8 * 1536], BF16),
                 "COS": (COS, [128, GT], F32), "SIN": (SIN, [128, GT], F32), "GN": (GN, [128, 24], F32),
                 "PERM": (PERMb, [128, 128], BF16), "ST": (ST, [128, 16], F32), "YN0": (YN[0], [128, D], F32),
                 "YN1": (YN[1], [128, D], F32)}
        b["YN0"] = YNb[0]; b["YN1"] = YNb[1]
        for nm in dbg_dump:
            tl, shp, dt = names[nm]
            dd = nc.dram_tensor("dbg_" + nm, shp, dt, kind="ExternalOutput")
            db_ = Buf("dbg_" + nm)
            src = tl[:] if len(tl.shape) == 2 else tl[:].rearrange("p a b -> p (a b)")
            dma(dd.ap(), src, db_, b.get(nm, b.get("PERM")), queue="pool")
            outs.append(db_)
        S.final_wait("pool", outs)
        print("sbuf bytes remaining/partition:", nc.sbuf_bytes_remaining, flush=True)
        S.emit(st)
        print("instr counts:", {e: len(S.ops[e]) for e in ENGS}, "dma sems:", S.ndsem, flush=True)
    return nc


_CACHE = {}


def _consts():
    ident = np.eye(128, dtype=np.float32)
    ones = np.ones((128, 128), np.float32)
    perm = np.zeros((128, 128), np.float32)
    for base in (0, 64):
        for d in range(8):
            perm[base + d + 8, base + d] = -1.0
            perm[base + d, base + d + 8] = 1.0
    anti = np.ascontiguousarray(np.eye(128, dtype=np.float32)[::-1])
    cmat = np.stack([ident, ones, perm, anti]).astype(np.float32)
    rtab = np.zeros((128, 4), np.float32)
    inv = np.power(np.float32(ROPE_THETA), -np.arange(8, dtype=np.float32) * np.float32(2.0) / np.float32(16))
    for p in range(128):
        d = p % 64
        if d < 16:
            rtab[p, 0] = inv[d % 8]
            rtab[p, 1] = 1.0
        else:
            rtab[p, 2] = 1.0
        rtab[p, 3] = math.pi
    return cmat, rtab.astype(np.float32)


def kernel(x, positions, a_norm_pre, a_w_in, a_rel_bias, a_w_out, a_norm_post, kv_norm, kv_w,
           b_norm_pre, b_w_in, b_lambda_q1, b_lambda_k1, b_lambda_q2, b_lambda_k2, b_subln,
           b_w_out, b_norm_post):
    f = lambda a: np.ascontiguousarray(np.asarray(a, dtype=np.float32))
    x = f(x)
    positions = np.ascontiguousarray(np.asarray(positions, dtype=np.int32))
    cmat, rtab = _consts()
    shared = {
        "a_norm_pre": f(a_norm_pre).reshape(D), "a_w_in": f(a_w_in).reshape(D, 4096),
        "a_rel_bias": f(a_rel_bias).reshape(16, 257), "a_w_out": f(a_w_out).reshape(D, D),
        "a_norm_post": f(a_norm_post).reshape(D), "kv_norm": f(kv_norm).reshape(D),
        "kv_w": f(kv_w).reshape(D, 2048), "b_norm_pre": f(b_norm_pre).reshape(D),
        "b_w_in": f(b_w_in).reshape(D, 2048),
        "b_lam": np.stack([f(b_lambda_q1).reshape(64), f(b_lambda_k1).reshape(64),
                           f(b_lambda_q2).reshape(64), f(b_lambda_k2).reshape(64)]),
        "b_subln": f(b_subln).reshape(128), "b_w_out": f(b_w_out).reshape(D, D),
        "b_norm_post": f(b_norm_post).reshape(D), "cmat": cmat, "rtab": rtab,
    }
    if "nc" not in _CACHE:
        _CACHE["nc"] = build_program()
    nc = _CACHE["nc"]
    in_maps = []
    for c in range(NCORES):
        m = dict(shared)
        m["x"] = x[c * NSEQ:(c + 1) * NSEQ]
        m["pos"] = positions[c * NSEQ:(c + 1) * NSEQ]
        in_maps.append(m)
    res = run_bass_kernel_spmd(nc, in_maps, core_ids=list(range(NCORES)))
    return np.concatenate([np.asarray(r["out"]).reshape(NSEQ, SEQ, D) for r in res.results], axis=0).astype(np.float32)
```

```python
import math
from contextlib import ExitStack
import numpy as np
import concourse.bass as bass
import concourse.mybir as mybir
from concourse.bass_utils import run_bass_kernel_spmd

F32 = mybir.dt.float32
BF16 = mybir.dt.bfloat16
I32 = mybir.dt.int32
AF = mybir.ActivationFunctionType
ALU = mybir.AluOpType
AX = mybir.AxisListType

NCORES = 8
SEQ = 2048
D = 1024
NSEQ = 2
GT = 512
NG = SEQ // GT
EPS = 1e-6
LAM_INIT = 0.8 - 0.6 * math.exp(-0.3 * 1)
ROPE_THETA = 500000.0

ENGS = ("pe", "act", "dve", "pool", "sp")
DBG_TILES = 4


class Buf:
    __slots__ = ("name", "w", "r", "dsem", "excl")

    def __init__(self, name, excl=False):
        self.name = name
        self.w = None
        self.r = []
        self.dsem = None
        self.excl = excl


class Sched:
    def __init__(self, nc):
        self.nc = nc
        self.ops = {e: [] for e in ENGS}
        self.cnt = {e: 0 for e in ENGS}
        self.seen = {e: {} for e in ENGS}
        self.ndsem = 0
        self.dcnt = {}

    def _waits(self, eng, reads, writes):
        waits = {}

        def need(t):
            if t is None:
                return
            k, n = t
            if eng == "pe" and k == "pe":
                return
            if n > self.seen[eng].get(k, 0) and n > waits.get(k, 0):
                waits[k] = n

        for b in reads:
            need(b.w)
            if b.excl:
                for t in b.r:
                    if t[0] != eng:
                        need(t)
        for b in writes:
            need(b.w)
            for t in b.r:
                need(t)
        for k, n in waits.items():
            self.seen[eng][k] = n
        return list(waits.items())

    def op(self, eng, fn, reads=(), writes=()):
        waits = self._waits(eng, reads, writes)
        self.cnt[eng] += 1
        tick = (eng, self.cnt[eng])
        for b in reads:
            if len(b.r) > 64:
                _prune(b)
            b.r.append(tick)
        for b in writes:
            b.w = tick
            b.r = []
        self.ops[eng].append((fn, waits, tick))
        return tick

    def dma(self, fn, dst, src, queue="sp"):
        waits = self._waits(queue, [src], [dst])
        if dst.dsem is None:
            dst.dsem = "q%d" % self.ndsem
            self.ndsem += 1
            self.dcnt[dst.dsem] = 0
        self.dcnt[dst.dsem] += 16
        tick = (dst.dsem, self.dcnt[dst.dsem])
        if len(src.r) > 64:
            _prune(src)
        src.r.append(tick)
        dst.w = tick
        dst.r = []
        self.ops[queue].append((fn, waits, tick))
        return tick

    def final_wait(self, eng, bufs):
        waits = dict(self._waits(eng, [], bufs))
        for k, n in self.dcnt.items():
            if n > self.seen[eng].get(k, 0):
                waits[k] = n
                self.seen[eng][k] = n
        self.ops[eng].append((None, list(waits.items()), None))

    def emit(self, stack):
        nc = self.nc
        sems = {}
        for e in ENGS:
            sems[e] = stack.enter_context(nc.semaphore("s_" + e))
        for k in self.dcnt:
            sems[k] = stack.enter_context(nc.semaphore("s_" + k))
        block = stack.enter_context(nc.Block())

        def run(ename):
            def body(eng):
                for fn, waits, tick in self.ops[ename]:
                    for k, n in waits:
                        eng.wait_ge(sems[k], n)
                    if fn is None:
                        continue
                    ins = fn(eng)
                    k, n = tick
                    ins.then_inc(sems[k], 16 if k.startswith("q") else 1)
            return body

        block.tensor(run("pe"))
        block.scalar(run("act"))
        block.vector(run("dve"))
        block.gpsimd(run("pool"))
        block.sync(run("sp"))


def _prune(b):
    best = {}
    for k, n in b.r:
        if n > best.get(k, 0):
            best[k] = n
    b.r = list(best.items())


class _Stop(Exception):
    pass


def build_program(dbg_stage=None, dbg_groups=None, dbg_dump=()):
    def stage(k):
        if dbg_stage is not None and dbg_stage == k:
            raise _Stop()

    nc = bass.Bass("TRN2", target_bir_lowering=False)

    def din(name, shape, dt=F32):
        return nc.dram_tensor(name, list(shape), dt, kind="ExternalInput")

    x_d = din("x", [NSEQ, SEQ, D])
    pos_d = din("pos", [NSEQ, SEQ], I32)
    a_npre_d = din("a_norm_pre", [D])
    a_win_d = din("a_w_in", [D, 4096])
    a_rb_d = din("a_rel_bias", [16, 257])
    a_wout_d = din("a_w_out", [D, D])
    a_npost_d = din("a_norm_post", [D])
    kvn_d = din("kv_norm", [D])
    kvw_d = din("kv_w", [D, 2048])
    b_npre_d = din("b_norm_pre", [D])
    b_win_d = din("b_w_in", [D, 2048])
    lam_d = din("b_lam", [4, 64])
    subln_d = din("b_subln", [128])
    b_wout_d = din("b_w_out", [D, D])
    b_npost_d = din("b_norm_post", [D])
    cmat_d = din("cmat", [4, 128, 128])
    rtab_d = din("rtab", [128, 4])
    out_d = nc.dram_tensor("out", [NSEQ, SEQ, D], F32, kind="ExternalOutput")

    w1s = nc.dram_tensor("w1s", [D, 4096], BF16)
    w2s = nc.dram_tensor("w2s", [D, D], BF16)
    w3s = nc.dram_tensor("w3s", [D, 2048], BF16)
    w4s = nc.dram_tensor("w4s", [D, 2048], BF16)
    w5s = nc.dram_tensor("w5s", [D, D], BF16)
    bx_s = nc.dram_tensor("bx_s", [16, 512], F32)
    chs = nc.dram_tensor("chs", [16], F32)
    ksh_s = nc.dram_tensor("ksh_s", [NSEQ, 8, 128, SEQ], BF16)
    vsh_s = nc.dram_tensor("vsh_s", [NSEQ, 8, SEQ, 128], BF16)

    x_ap = x_d.ap(); out_ap = out_d.ap(); cmat_ap = cmat_d.ap(); rtab_ap = rtab_d.ap()

    def dap(t, offset, ap):
        return bass.AP(tensor=t, offset=offset, ap=[list(a) for a in ap])

    S = Sched(nc)
    with ExitStack() as st:
        def sb(name, shape, dt):
            return st.enter_context(nc.sbuf_tensor(name, list(shape), dt))

        def ps(name, shape, dt=F32):
            return st.enter_context(nc.psum_tensor(name, list(shape), dt))

        IDb16 = sb("IDb16", [128, 128], BF16); ONESb = sb("ONESb", [128, 128], BF16)
        PERMb = sb("PERMb", [128, 128], BF16); JJ = sb("JJ", [128, 128], F32)
        CST = sb("CST", [128, 128], F32)
        RTAB = sb("RTAB", [128, 4], F32)
        EB = sb("EB", [128, 16, 256], BF16)
        CH = sb("CH", [128, 16], F32)
        RB16 = sb("RB16", [16, 257], F32)
        GPA = sb("GPA", [128, D], F32); GPB = sb("GPB", [128, D], F32)
        GN = sb("GN", [128, 3, 8], F32)
        SUBS = sb("SUBS", [128, 1], F32)
        LAMT = sb("LAMT", [128, 4, 64], F32)
        LAMS = sb("LAMS", [128, 8], F32)
        EPSC = sb("EPSC", [128, 2], F32)
        ZER = sb("ZER", [128, 256], F32)
        ST = sb("ST", [128, 16], F32)
        STN = [sb("STN%d" % i, [128, 4], F32) for i in range(3)]
        STO = [sb("STO%d" % i, [128, 4], F32) for i in range(2)]
        NCH = sb("NCH", [128, 16], F32)
        NW = 3
        WB = [sb("WB%d" % i, [128, 8, 512], BF16) for i in range(NW)]
        NXT = 2
        NCX = 3
        XTL = [sb("XTL%d" % i, [128, D], F32) for i in range(NXT)]
        CXL = [sb("CXL%d" % i, [128, D], F32) for i in range(NCX)]
        YN = [sb("YN%d" % i, [128, D], F32) for i in range(2)]
        UU = [sb("UU%d" % i, [128, D], BF16) for i in range(3)]
        XT = sb("XT", [128, 8, GT], BF16)
        X2T = sb("X2T", [128, 8, GT], BF16)
        QT = sb("QT", [128, 8, GT], BF16)
        SGT = sb("SGT", [128, 8, GT], BF16)
        KT = sb("KT", [128, 8, 1024], BF16)
        VA = sb("VA", [128, 8, 1536], BF16)
        NPT = 4
        PT = [sb("PT%d" % i, [128, GT], BF16) for i in range(NPT)]
        RR = [sb("RR%d" % i, [128, GT], F32) for i in range(2)]
        T0 = sb("T0", [128, GT], F32); T1 = sb("T1", [128, GT], F32)
        TA = [sb("TA%d" % i, [128, GT], F32) for i in range(2)]
        TBB = [sb("TBB%d" % i, [128, GT], F32) for i in range(2)]
        SQ = sb("SQ", [128, GT], BF16); RS = sb("RS", [128, GT], F32)
        KRAW = [sb("KRAW%d" % i, [128, GT], BF16) for i in range(2)]
        COS = sb("COS", [128, GT], F32); SIN = sb("SIN", [128, GT], F32)
        POSI = sb("POSI", [128, GT], I32)
        KB = [sb("KB%d" % i, [128, SEQ], BF16) for i in range(2)]
        VB = [sb("VB%d" % i, [128, 16, 128], BF16) for i in range(2)]
        KST = [sb("KST%d" % i, [128, GT], BF16) for i in range(2)]
        VST = [sb("VST%d" % i, [128, D], BF16) for i in range(2)]

        PB = [ps("PB%d" % i, [128, 512], F32) for i in range(7)]
        TRB = ps("TRB", [128, 1024], BF16)

        b = {}
        def B(name, excl=False):
            b[name] = Buf(name, excl)
            return b[name]
        for nm in ["ID", "ONES", "PERM", "JJ", "CST", "RTAB", "EB", "CH", "GPA", "GPB", "GN", "SUBS", "LAMT",
                   "LAMS", "EPSC", "ZER", "ST", "XT", "X2T", "RB16", "QT", "SGT", "KT", "VA", "T0", "T1", "SQ", "RS",
                   "COS", "SIN", "POSI"]:
            B(nm)
        WBb = [Buf("WB%d" % i) for i in range(NW)]
        STNb = [Buf("STN%d" % i) for i in range(3)]
        STOb = [Buf("STO%d" % i) for i in range(2)]
        TAb = [Buf("TA%d" % i) for i in range(2)]
        TBBb = [Buf("TBB%d" % i) for i in range(2)]
        KRAWb = [Buf("KRAW%d" % i) for i in range(2)]
        B("NCH")
        XTLb = [Buf("XTL%d" % i) for i in range(NXT)]
        CXLb = [Buf("CXL%d" % i) for i in range(NCX)]
        YNb = [Buf("YN%d" % i) for i in range(2)]
        UUb = [Buf("UU%d" % i) for i in range(3)]
        PTb = [Buf("PT%d" % i) for i in range(NPT)]
        RRb = [Buf("RR%d" % i) for i in range(2)]
        KBb = [Buf("KB%d" % i) for i in range(2)]
        VBb = [Buf("VB%d" % i) for i in range(2)]
        KSTb = [Buf("KST%d" % i) for i in range(2)]
        VSTb = [Buf("VST%d" % i) for i in range(2)]
        PBb = [Buf("PB%d" % i, True) for i in range(7)]
        TRBb = Buf("TRB", True)
        d_in = Buf("d_in")
        d_w = {k: Buf("d_" + k) for k in ["w1", "w2", "w3", "w4", "w5", "bx", "chs"]}
        d_ksh = [[Buf("d_ksh%d_%d" % (s, g)) for g in range(NG)] for s in range(NSEQ)]
        d_vsh = [[Buf("d_vsh%d_%d" % (s, g)) for g in range(NG)] for s in range(NSEQ)]
        d_out = [[[Buf("d_out%d_%d_%d" % (s, g, t)) for t in range(4)] for g in range(NG)] for s in range(NSEQ)]

        def act(out, in_, func, reads, writes, **kw):
            S.op("act", lambda e: e.activation(out=out, in_=in_, func=func, **kw), reads, writes)

        def mm(out, lhsT, rhs, start, reads, writes):
            S.op("pe", lambda e: e.matmul(out, lhsT=lhsT, rhs=rhs, start=start, stop=False,
                                          skip_group_check=True), reads, writes)

        def dve_copy(out, in_, reads, writes):
            S.op("dve", lambda e: e.tensor_copy(out=out, in_=in_), reads, writes)

        def dve_tt(out, in0, in1, op, reads, writes, eng="dve"):
            S.op(eng, lambda e: e.tensor_tensor(out=out, in0=in0, in1=in1, op=op), reads, writes)

        def dve_ts(out, in0, s1, s2, op0, op1, reads, writes, eng="dve"):
            if op1 is None:
                S.op(eng, lambda e: e.tensor_scalar(out=out, in0=in0, scalar1=s1, scalar2=None, op0=op0), reads, writes)
            else:
                S.op(eng, lambda e: e.tensor_scalar(out=out, in0=in0, scalar1=s1, scalar2=s2, op0=op0, op1=op1),
                     reads, writes)

        def dma(out, in_, dst, src, queue="sp", slow=False):
            if slow:
                S.dma(lambda e: e.dma_start(out=out, in_=in_, allow_slow_non_contiguous=True), dst, src, queue)
            else:
                S.dma(lambda e: e.dma_start(out=out, in_=in_), dst, src, queue)

        evac_flip = [0]

        def evac(out, in_, reads, writes):
            evac_flip[0] ^= 1
            if evac_flip[0]:
                act(out, in_, AF.Copy, reads, writes)
            else:
                dve_copy(out, in_, reads, writes)

        for i, (t, nm) in enumerate([(IDb16, "ID"), (ONESb, "ONES"), (PERMb, "PERM")]):
            dma(CST[:], cmat_ap[i], b["CST"], d_in)
            dve_copy(t[:], CST[:], [b["CST"]], [b[nm]])
        dma(JJ[:], cmat_ap[3], b["JJ"], d_in)
        dma(RTAB[:], rtab_ap, b["RTAB"], d_in)
        dma(GPA[:], dap(a_npost_d, 0, [[0, 128], [1, D]]), b["GPA"], d_in)
        dma(GPB[:], dap(b_npost_d, 0, [[0, 128], [1, D]]), b["GPB"], d_in)
        for i, t in enumerate([a_npre_d, kvn_d, b_npre_d]):
            for kc in range(8):
                dma(GN[:, i, kc:kc + 1], dap(t, kc * 128, [[1, 128], [1, 1]]), b["GN"], d_in)
        dma(SUBS[:], dap(subln_d, 0, [[1, 128], [1, 1]]), b["SUBS"], d_in)
        dma(LAMT[:], dap(lam_d, 0, [[0, 128], [64, 4], [1, 64]]), b["LAMT"], d_in)
        dma(RB16[:], a_rb_d.ap(), b["RB16"], d_in)
        dma(dap(chs, 0, [[1, 16], [1, 1]]), RB16[:, 256:257], d_w["chs"], b["RB16"])
        dma(CH[:], dap(chs, 0, [[0, 128], [1, 16]]), b["CH"], d_w["chs"])
        dve_ts(NCH[:], CH[:], -1.0, None, ALU.mult, None, [b["CH"]], [b["NCH"]])
        S.op("dve", lambda e: e.memset(EPSC[:, 0:1], EPS), [], [b["EPSC"]])
        S.op("dve", lambda e: e.memset(EPSC[:, 1:2], 0.0), [], [b["EPSC"]])
        S.op("dve", lambda e: e.memset(ZER[:], 0.0), [], [b["ZER"]])
        S.op("pool", lambda e: e.memset(VA[:], 1.0), [], [b["VA"]])
        dve_tt(LAMT[:, 0, :], LAMT[:, 0, :], LAMT[:, 1, :], ALU.mult, [b["LAMT"]], [b["LAMT"]])
        dve_tt(LAMT[:, 2, :], LAMT[:, 2, :], LAMT[:, 3, :], ALU.mult, [b["LAMT"]], [b["LAMT"]])
        S.op("dve", lambda e: e.tensor_reduce(out=LAMS[:, 0:1], in_=LAMT[:, 0, :], axis=AX.X, op=ALU.add),
             [b["LAMT"]], [b["LAMS"]])
        S.op("dve", lambda e: e.tensor_reduce(out=LAMS[:, 1:2], in_=LAMT[:, 2, :], axis=AX.X, op=ALU.add),
             [b["LAMT"]], [b["LAMS"]])
        act(LAMS[:, 4:6], LAMS[:, 0:2], AF.Exp, [b["LAMS"]], [b["LAMS"]])
        dve_tt(LAMS[:, 2:3], LAMS[:, 4:5], LAMS[:, 5:6], ALU.subtract, [b["LAMS"]], [b["LAMS"]])
        dve_ts(LAMS[:, 3:4], LAMS[:, 2:3], LAM_INIT, -1.0, ALU.add, ALU.mult, [b["LAMS"]], [b["LAMS"]])
        NEGLAM = LAMS[:, 3:4]
        dve_ts(SUBS[:], SUBS[:], 1.0 - LAM_INIT, None, ALU.mult, None, [b["SUBS"]], [b["SUBS"]])

        dbg_t = {}
        wlist = [(a_win_d, w1s, 4096, 0, "w1"), (a_wout_d, w2s, D, None, "w2"), (kvw_d, w3s, 2048, 1, "w3"),
                 (b_win_d, w4s, 2048, 2, "w4"), (b_wout_d, w5s, D, None, "w5")]
        conv_jobs = []
        conv_state = {"loaded": 0, "done": 0}
        CONV_LA = NCX - 1

        def conv_load(i):
            src, dst, ncol, gi, key, kc, c0 = conv_jobs[i]
            xi = i % NCX
            dma(CXL[xi][:], dap(src, kc * 128 * ncol + c0, [[ncol, 128], [1, D]]), CXLb[xi], d_in,
                queue="sp")

        def conv_compute(i):
            src, dst, ncol, gi, key, kc, c0 = conv_jobs[i]
            xi = i % NCX
            ui = i % 3
            if gi is None:
                evac(UU[ui][:], CXL[xi][:], [CXLb[xi]], [UUb[ui]])
            elif i % 2 == 0:
                act(UU[ui][:], CXL[xi][:], AF.Copy, [CXLb[xi], b["GN"]], [UUb[ui]], scale=GN[:, gi, kc:kc + 1])
            else:
                dve_ts(UU[ui][:], CXL[xi][:], GN[:, gi, kc:kc + 1], None, ALU.mult, None,
                       [CXLb[xi], b["GN"]], [UUb[ui]])
            dma(dap(dst, kc * 128 * ncol + c0, [[ncol, 128], [1, D]]), UU[ui][:], d_w[key], UUb[ui],
                queue="sp")
            _prune(d_w[key])

        def conv_step():
            i = conv_state["done"]
            while conv_state["loaded"] < min(len(conv_jobs), i + 1 + CONV_LA):
                conv_load(conv_state["loaded"])
                conv_state["loaded"] += 1
            conv_compute(i)
            conv_state["done"] += 1

        bg_work = []
        conv_done = {}

        def phase0b():
            for (src, dst, ncol, gi, key) in wlist:
                for c0 in range(0, ncol, D):
                    for kc in range(8):
                        conv_jobs.append((src, dst, ncol, gi, key, kc, c0))
                        bg_work.append((key, c0 // D))

        def bg_pop(n=1):
            for _ in range(n):
                if bg_work:
                    kk = bg_work.pop(0)
                    conv_step()
                    conv_done[kk] = conv_done.get(kk, 0) + 1

        def ensure_converted(key, blk):
            kk = (key, blk // 2)
            while conv_done.get(kk, 0) < 8:
                assert bg_work, kk
                bg_pop()

        if dbg_stage is None or dbg_stage >= 1:
            phase0b()
            bg_pop(16)
        dve_copy(T1[0:16, 0:257], RB16[:, :], [b["RB16"]], [b["T1"]])
        dve_ts(T1[0:16, 257:512], ZER[0:16, 0:255], RB16[:, 256:257], None, ALU.add, None,
               [b["ZER"], b["RB16"]], [b["T1"]])
        dma(bx_s.ap(), T1[0:16, :], d_w["bx"], b["T1"])
        for h in range(16):
            hk, hkb = RR[h % 2], RRb[h % 2]
            pb, pbb = PB[h % 2], PBb[h % 2]
            dma(hk[:, 0:256], dap(bx_s, h * 512 + 1, [[1, 128], [1, 256]]), hkb, d_w["bx"])
            S.op("pe", lambda e, pb=pb, hk=hk: e.matmul(pb[:, 0:256], lhsT=JJ[:], rhs=hk[:, 0:256], start=True, stop=True),
                 [b["JJ"], hkb], [pbb])
            act(EB[:, h, 0:256], pb[:, 0:256], AF.Exp, [pbb, b["NCH"]], [b["EB"]], bias=NCH[:, h:h + 1])
        S.op("dve", lambda e: e.memset(EB[64:128, :, 0:64], 0.0), [], [b["EB"]])

        wsrc = {"w1": (w1s, 4096), "w2": (w2s, D), "w3": (w3s, 2048), "w4": (w4s, 2048), "w5": (w5s, D)}
        group_blocks = ([("w1", c) for c in range(8)] + [("w2", c) for c in range(2)] + [("w3", c) for c in (2, 3, 0, 1)]
                        + [("w4", c) for c in range(4)] + [("w5", c) for c in range(2)])
        all_blocks = group_blocks * (NSEQ * NG)
        wstate = {"issued": 0, "used": 0}

        def wissue(upto):
            while wstate["issued"] <= upto and wstate["issued"] < len(all_blocks):
                n = wstate["issued"]
                key, c = all_blocks[n]
                ensure_converted(key, c)
                t, ncol = wsrc[key]
                slot = n % NW
                dma(WB[slot][:], dap(t, c * 512, [[ncol, 128], [128 * ncol, 8], [1, 512]]), WBb[slot], d_w[key])
                _prune(d_w[key])
                wstate["issued"] += 1

        def wnext(expect, live_prev=0):
            n = wstate["used"]
            assert all_blocks[n] == expect, (all_blocks[n], expect)
            wissue(n + NW - 1 - live_prev)
            wstate["used"] += 1
            return WB[n % NW], WBb[n % NW]

        pbr = [0]

        def next_pb(lo, hi):
            i = lo + pbr[0] % (hi - lo)
            pbr[0] += 1
            return PB[i], PBb[i]

        xtr = [0]

        def next_xtl():
            i = xtr[0] % NXT
            xtr[0] += 1
            return XTL[i], XTLb[i]

        ptr = [0]

        def next_pt():
            i = ptr[0] % NPT
            ptr[0] += 1
            return PT[i], PTb[i]

        def rstd_from_ss(ss_ap, n, out_ap, stb):
            act(out_ap, ss_ap, AF.Ln, [stb, b["EPSC"]], [stb], scale=1.0 / n, bias=EPSC[:, 0:1])
            act(out_ap, out_ap, AF.Exp, [stb], [stb], scale=-0.5)

        def norm_part(src_tile, src_buf, gi):
            ui = gi % 3
            yj = gi % 2
            st_, stb = STN[ui], STNb[ui]
            act(YN[yj][:], src_tile[:], AF.Square, [src_buf], [YNb[yj], stb], accum_out=st_[:, 0:1])
            rstd_from_ss(st_[:, 0:1], D, st_[:, 1:2], stb)
            act(UU[ui][:], src_tile[:], AF.Copy, [src_buf, stb], [UUb[ui]], scale=st_[:, 1:2])

        def tr_part(tcol, gi, XD, XDb):
            ui = gi % 3
            for kc in range(8):
                S.op("pe", lambda e, kc=kc: e.transpose(TRB[:, kc * 128:(kc + 1) * 128],
                                                        UU[ui][:, kc * 128:(kc + 1) * 128], IDb16[:]),
                     [UUb[ui], b["ID"]], [TRBb])
            evac(XD[:, :, tcol:tcol + 128], TRB[:].rearrange("p (k t) -> p k t", t=128), [TRBb], [XDb])

        def norm_transpose(src_tile, src_buf, tcol, gi, XD, XDb):
            norm_part(src_tile, src_buf, gi)
            tr_part(tcol, gi, XD, XDb)

        def proj_fm(wb, wbb, c, post, XS=None, XSb=None):
            if XS is None:
                XS, XSb = XT, b["XT"]
            pb, pbb = next_pb(0, 3)
            for kc in range(8):
                mm(pb[:, :], wb[:, kc, c * 128:(c + 1) * 128], XS[:, kc, :], kc == 0, [wbb, XSb], [pbb])
            if rope_pend[0] is not None:
                rope_pend[0]()
                rope_pend[0] = None
            post(pb, pbb)
            bg_pop()

        def outproj_norm_residual(w_key, gp_tile, gp_buf, s, g, layer, XS, XSb):
            w0, w0b = wnext((w_key, 0))
            w1_, w1b = wnext((w_key, 1), live_prev=1)
            xts = {}
            bankss = {}
            ob7 = [0]

            def mm_tile(t):
                row0 = g * GT + t * 128
                xt, xtb = next_xtl()
                if layer == 0:
                    dma(xt[:], x_ap[s, row0:row0 + 128, :], xtb, d_in)
                else:
                    dma(xt[:], out_ap[s, row0:row0 + 128, :], xtb, d_out[s][g][t])
                xts[t] = (xt, xtb)
                banks = []
                for hb, (w, wbuf) in enumerate([(w0, w0b), (w1_, w1b)]):
                    bi_ = ob7[0] % 6
                    ob7[0] += 1
                    pb, pbb = PB[bi_], PBb[bi_]
                    for fc in range(8):
                        mm(pb[:, :], XS[:, fc, t * 128:(t + 1) * 128], w[:, fc, :], fc == 0, [XSb, wbuf], [pbb])
                    banks.append((pb, pbb))
                bankss[t] = banks

            def post_tile(t):
                row0 = g * GT + t * 128
                xt, xtb = xts.pop(t)
                banks = bankss.pop(t)
                yi = t % 2
                so_, sob = STO[yi], STOb[yi]
                for hb, (pb, pbb) in enumerate(banks):
                    act(YN[yi][:, hb * 512:(hb + 1) * 512], pb[:, :], AF.Square, [pbb], [YNb[yi], sob],
                        accum_out=so_[:, 2 + hb:3 + hb])
                dve_tt(so_[:, 0:1], so_[:, 2:3], so_[:, 3:4], ALU.add, [sob], [sob])
                rstd_from_ss(so_[:, 0:1], D, so_[:, 1:2], sob)
                for hb, (pb, pbb) in enumerate(banks):
                    S.op("dve", lambda e, pb=pb, hb=hb, yi=yi, so_=so_: e.scalar_tensor_tensor(
                        out=YN[yi][:, hb * 512:(hb + 1) * 512], in0=pb[:, :], scalar=so_[:, 1:2],
                        in1=gp_tile[:, hb * 512:(hb + 1) * 512], op0=ALU.mult, op1=ALU.mult),
                        [pbb, sob, gp_buf], [YNb[yi]])
                dve_tt(xt[:], xt[:], YN[yi][:], ALU.add, [xtb, YNb[yi]], [xtb])
                dma(out_ap[s, row0:row0 + 128, :], xt[:], d_out[s][g][t], xtb, queue="sp")
                if layer == 0:
                    norm_part(xt, xtb, t)

            NT = DBG_TILES
            INFL = 2
            for t in range(min(INFL, NT)):
                mm_tile(t)
            for t in range(NT):
                post_tile(t)
                if t + INFL < NT:
                    mm_tile(t + INFL)
                if layer == 0:
                    tr_part(t * 128, t, X2T, b["X2T"])

        glist = [(s, g) for s in range(NSEQ) for g in range(NG)]
        if dbg_groups is not None:
            glist = glist[:dbg_groups]

        def run_all():
            stage(0)
            stage(1)
            pre(*glist[0])
            for gi_, (s, g) in enumerate(glist):
                do_group(s, g, glist[gi_ + 1] if gi_ + 1 < len(glist) else None)

        rope_hook = [None]
        rope_pend = [None]

        def ld_norm(s, g, t):
            xt, xtb = next_xtl()
            dma(xt[:], x_ap[s, g * GT + t * 128: g * GT + (t + 1) * 128, :], xtb, d_in)
            norm_part(xt, xtb, t)

        def pre(s, g, with_x=True):
            if True:
                tok0 = g * GT
                dma(POSI[:], dap(pos_d, s * SEQ + tok0, [[0, 128], [1, GT]]), b["POSI"], d_in)
                dve_copy(T0[:], POSI[:], [b["POSI"]], [b["T0"]])
                dve_ts(T0[:], T0[:], RTAB[:, 0:1], None, ALU.mult, None, [b["T0"], b["RTAB"]], [b["T0"]])
                TWO_PI = 2.0 * math.pi
                for (ang, angb, shift) in ((TA[0], TAb[0], 0.0), (TA[1], TAb[1], math.pi / 2)):
                    dve_ts(T1[:], T0[:], shift, 1.0 / TWO_PI, ALU.add, ALU.mult, [b["T0"]], [b["T1"]])
                    dve_copy(POSI[:], T1[:], [b["T1"]], [b["POSI"]])
                    dve_copy(T1[:], POSI[:], [b["POSI"]], [b["T1"]])
                    dve_ts(T1[:], T1[:], -TWO_PI, shift, ALU.mult, ALU.add, [b["T1"]], [b["T1"]])
                    dve_tt(T1[:], T1[:], T0[:], ALU.add, [b["T1"], b["T0"]], [b["T1"]])
                    dve_ts(RS[:], T1[:], math.pi, TWO_PI, ALU.is_gt, ALU.mult, [b["T1"]], [b["RS"]])
                    dve_tt(T1[:], T1[:], RS[:], ALU.subtract, [b["T1"], b["RS"]], [b["T1"]])
                    dve_ts(RS[:], T1[:], -math.pi, TWO_PI, ALU.is_lt, ALU.mult, [b["T1"]], [b["RS"]])
                    dve_tt(ang[:], T1[:], RS[:], ALU.add, [b["T1"], b["RS"]], [angb])

                def rope_fin():
                    act(SIN[:], TA[0][:], AF.Sin, [TAb[0]], [b["SIN"]])
                    act(COS[:], TA[1][:], AF.Sin, [TAb[1]], [b["COS"]])
                    dve_ts(SIN[:], SIN[:], RTAB[:, 1:2], None, ALU.mult, None, [b["SIN"], b["RTAB"]], [b["SIN"]])
                    dve_ts(COS[:], COS[:], RTAB[:, 1:2], RTAB[:, 2:3], ALU.mult, ALU.add, [b["COS"], b["RTAB"]], [b["COS"]])
                if with_x:
                    rope_fin()
                else:
                    rope_hook[0] = rope_fin

                if with_x:
                    for t in range(4):
                        ld_norm(s, g, t)
                        tr_part(t * 128, t, XT, b["XT"])

        def do_group(s, g, nxt):
            if True:
                tok0 = g * GT
                kcol0 = (g % 2) * 512
                for blk in range(2):
                    wb, wbb = wnext(("w1", blk))
                    for c in range(4):
                        fc = blk * 4 + c
                        proj_fm(wb, wbb, c, lambda pb, pbb, fc=fc: evac(QT[:, fc, :], pb[:, :], [pbb], [b["QT"]]))
                for blk in range(2):
                    wb, wbb = wnext(("w1", 2 + blk))
                    for c in range(4):
                        fc = blk * 4 + c
                        proj_fm(wb, wbb, c, lambda pb, pbb, fc=fc: evac(KT[:, fc, kcol0:kcol0 + 512], pb[:, :],
                                                                        [pbb], [b["KT"]]))
                for blk in range(2):
                    wb, wbb = wnext(("w1", 4 + blk))
                    for t in range(4):
                        slot = (4 * g + t) % 8
                        pb, pbb = next_pb(0, 3)
                        for kc in range(8):
                            mm(pb[:, :], XT[:, kc, t * 128:(t + 1) * 128], wb[:, kc, :], kc == 0,
                               [b["XT"], wbb], [pbb])
                        src = pb[:, :].rearrange("p (i c) -> p i c", c=128)
                        dst = VA[:, slot, blk * 768:(blk + 1) * 768].rearrange("p (i c) -> p i c", c=192)
                        act(dst[:, :, 0:64], src[:, :, 0:64], AF.Copy, [pbb], [b["VA"]])
                        dve_copy(dst[:, :, 128:192], src[:, :, 64:128], [pbb], [b["VA"]])
                for blk in range(2):
                    wb, wbb = wnext(("w1", 6 + blk))
                    for c in range(4):
                        fc = blk * 4 + c
                        proj_fm(wb, wbb, c, lambda pb, pbb, fc=fc: act(SGT[:, fc, :], pb[:, :], AF.Silu,
                                                                       [pbb], [b["SGT"]]))

                stage(2)
                items = []
                tl_ = []
                for Tk in range(max(0, 4 * g - 4), 4 * g + 4):
                    qlo = max(Tk, 4 * g)
                    qhi = min(Tk + 4, 4 * g + 3)
                    tl_.append((Tk, (qlo - 4 * g) * 128, (qhi + 1 - 4 * g) * 128, qlo - Tk))
                for i_ in range(8):
                    for n, (Tk, c0, c1, jlo) in enumerate(tl_):
                        for h in (2 * i_, 2 * i_ + 1):
                            items.append((h, n, Tk, c0, c1, jlo, n == len(tl_) - 1))
                DEPTH_A = 4
                sbank = {}
                sA = [0]
                postq = []

                def qkA(idx):
                    h, n, Tk, c0, c1, jlo, last = items[idx]
                    i, r0 = h // 2, 64 * (h % 2)
                    bi_ = sA[0] % 4
                    sA[0] += 1
                    pb, pbb = PB[bi_], PBb[bi_]
                    kc0 = (Tk % 8) * 128
                    mm(pb[:, c0:c1], KT[r0:r0 + 64, i, kc0:kc0 + 128], QT[r0:r0 + 64, i, c0:c1], True,
                       [b["KT"], b["QT"]], [pbb])
                    sbank[idx] = (pb, pbb)

                def finA(idx):
                    h, n, Tk, c0, c1, jlo, last = items[idx]
                    i, half = h // 2, h % 2
                    r0 = 64 * half
                    ob, obb = PB[4 + h % 3], PBb[4 + h % 3]
                    pb, pbb = sbank.pop(idx)
                    pt, ptb = next_pt()
                    act(pt[:, c0:c1], pb[:, c0:c1], AF.Exp, [pbb], [ptb], scale=0.125)
                    jhi = jlo + (c1 - c0) // 128 - 1
                    if jlo <= 1:
                        nb_ = (min(jhi, 1) - jlo + 1) * 128
                        dve_tt(pt[:, c0:c0 + nb_], pt[:, c0:c0 + nb_], EB[:, h, jlo * 128: jlo * 128 + nb_],
                               ALU.mult, [ptb, b["EB"]], [ptb])
                    if jhi == 4:
                        S.op("dve", lambda e, pt=pt, c1=c1: e.memset(pt[0:64, c1 - 64:c1], 0.0), [], [ptb])
                    vcol = i * 192 + 64 * half
                    mm(ob[:, c0:c1], VA[:, Tk % 8, vcol:vcol + 128], pt[:, c0:c1], n == 0, [b["VA"], ptb], [obb])
                    if last:
                        def postA(h=h, i=i, r0=r0, ob=ob, obb=obb):
                            so = 64 - r0
                            rr, rrb = RR[h % 2], RRb[h % 2]
                            act(rr[so:so + 64, :], ob[so:so + 64, :], AF.Ln, [obb], [rrb])
                            act(rr[so:so + 64, :], rr[so:so + 64, :], AF.Exp, [rrb], [rrb], scale=-1.0)
                            ta, tab_ = TA[h % 2], TAb[h % 2]
                            dve_tt(ta[r0:r0 + 64, :], ob[r0:r0 + 64, :], rr[so:so + 64, :], ALU.mult, [obb, rrb], [tab_])
                            dve_tt(XT[r0:r0 + 64, i, :], ta[r0:r0 + 64, :], SGT[r0:r0 + 64, i, :], ALU.mult,
                                   [tab_, b["SGT"]], [b["XT"]])
                            bg_pop(3)
                        postq.append([2, postA])

                def tickA():
                    for it in postq:
                        it[0] -= 1
                    while postq and postq[0][0] < 0:
                        postq.pop(0)[1]()

                for idx in range(0, len(items) + DEPTH_A, 2):
                    for k_ in (idx - DEPTH_A, idx + 1 - DEPTH_A):
                        if 0 <= k_ < len(items):
                            tickA()
                            finA(k_)
                    for k_ in (idx, idx + 1):
                        if k_ < len(items):
                            qkA(k_)
                while postq:
                    postq.pop(0)[1]()

                stage(3)
                outproj_norm_residual("w2", GPA, b["GPA"], s, g, 0, XT, b["XT"])

                stage(4)
                rp = [0]

                def rope_post(pb, pbb, dst_ap, dst_buf, after=None):
                    k = rp[0] % 2
                    rp[0] += 1
                    kr, krb, ta, tab_, tb, tbb = KRAW[k], KRAWb[k], TA[k], TAb[k], TBB[k], TBBb[k]
                    act(kr[:], pb[:, :], AF.Copy, [pbb], [krb])
                    dve_tt(ta[:], pb[:, :], COS[:], ALU.mult, [pbb, b["COS"]], [tab_])

                    def part2():
                        p2, p2b = PB[3 + k], PBb[3 + k]
                        mm(p2[:, :], PERMb[:], kr[:], True, [b["PERM"], krb], [p2b])
                        dve_tt(tb[:], p2[:, :], SIN[:], ALU.mult, [p2b, b["SIN"]], [tbb])
                        dve_tt(dst_ap, ta[:], tb[:], ALU.add, [tab_, tbb], [dst_buf])
                        if after is not None:
                            after()
                    rope_pend[0] = part2

                if nxt is not None:
                    for t_ in range(3):
                        ld_norm(nxt[0], nxt[1], t_)
                vblocks = [wnext(("w3", 2)), wnext(("w3", 3), live_prev=1)]
                for t in range(4):
                    vs, vsb = VST[t % 2], VSTb[t % 2]
                    for blk in range(2):
                        wb, wbb = vblocks[blk]
                        pb, pbb = next_pb(0, 3)
                        for kc in range(8):
                            mm(pb[:, :], X2T[:, kc, t * 128:(t + 1) * 128], wb[:, kc, :], kc == 0,
                               [b["X2T"], wbb], [pbb])
                        evac(vs[:, blk * 512:(blk + 1) * 512], pb[:, :], [pbb], [vsb])
                    dma(dap(vsh_s, (s * 8 * SEQ + tok0 + t * 128) * 128, [[128, 128], [SEQ * 128, 8], [1, 128]]),
                        vs[:].rearrange("p (h v) -> p h v", v=128), d_vsh[s][g], vsb, queue="sp")
                    _prune(d_vsh[s][g])

                for blk in range(2):
                    wb, wbb = wnext(("w3", blk))
                    for c in range(4):
                        hh = blk * 4 + c
                        ks, ksb = KST[hh % 2], KSTb[hh % 2]
                        def kstore(hh=hh, ks=ks, ksb=ksb):
                            dma(dap(ksh_s, ((s * 8 + hh) * 128) * SEQ + tok0, [[SEQ, 128], [1, GT]]), ks[:],
                                d_ksh[s][g], ksb, queue="sp")
                            _prune(d_ksh[s][g])
                        proj_fm(wb, wbb, c, lambda pb, pbb, ks=ks, ksb=ksb, kstore=kstore:
                                rope_post(pb, pbb, ks[:], ksb, kstore), X2T, b["X2T"])
                if nxt is not None:
                    tr_part(0, 0, XT, b["XT"])
                    ld_norm(nxt[0], nxt[1], 3)
                    tr_part(128, 1, XT, b["XT"])
                    tr_part(256, 2, XT, b["XT"])
                for blk in range(2):
                    wb, wbb = wnext(("w4", blk))
                    for c in range(4):
                        hh = blk * 4 + c
                        proj_fm(wb, wbb, c, lambda pb, pbb, hh=hh: rope_post(pb, pbb, QT[:, hh, :], b["QT"]),
                                X2T, b["X2T"])
                for blk in range(2):
                    wb, wbb = wnext(("w4", 2 + blk))
                    for c in range(4):
                        hh = blk * 4 + c

                        def gpost(pb, pbb, hh=hh):
                            act(T0[:], pb[:, :], AF.Silu, [pbb], [b["T0"]])
                            dve_ts(SGT[:, hh, :], T0[:], SUBS[:, 0:1], None, ALU.mult, None,
                                   [b["T0"], b["SUBS"]], [b["SGT"]])
                        proj_fm(wb, wbb, c, gpost, X2T, b["X2T"])

                stage(5)
                if rope_pend[0] is not None:
                    rope_pend[0]()
                    rope_pend[0] = None
                if nxt is not None:
                    pre(nxt[0], nxt[1], with_x=False)
                    tr_part(384, 3, XT, b["XT"])
                ntile = 4 * g + 4
                O0, O0b, O1, O1b = PB[3], PBb[3], PB[4], PBb[4]
                Z0, Z0b, Z1, Z1b = PB[5], PBb[5], PB[6], PBb[6]
                itemsB = [(hh, n) for hh in range(8) for n in range(ntile)]
                kvslot = {}
                sbankB = {}
                pend_d1 = [None]
                pend_d2 = [None]
                ssb = [None]

                def loadB(hh):
                    kb, kbb = KB[hh % 2], KBb[hh % 2]
                    vb, vbb = VB[hh % 2], VBb[hh % 2]
                    for gg in range(g + 1):
                        dma(kb[:, gg * GT:(gg + 1) * GT],
                            dap(ksh_s, ((s * 8 + hh) * 128) * SEQ + gg * GT, [[SEQ, 128], [1, GT]]),
                            kbb, d_ksh[s][gg])
                        dma(vb[:, gg * 4:(gg + 1) * 4, :],
                            dap(vsh_s, ((s * 8 + hh) * SEQ + gg * GT) * 128, [[128, 128], [128 * 128, 4], [1, 128]]),
                            vbb, d_vsh[s][gg])
                    kvslot[hh] = (kb, kbb, vb, vbb)

                def qkB(idx):
                    hh, n = itemsB[idx]
                    if n == 0:
                        loadB(hh)
                    kb, kbb, vb, vbb = kvslot[hh]
                    c0 = max(0, n - 4 * g) * 128
                    res = []
                    for m in range(2):
                        pb, pbb = next_pb(0, 3)
                        mm(pb[:, c0:GT], kb[64 * m:64 * m + 64, n * 128:(n + 1) * 128],
                           QT[64 * m:64 * m + 64, hh, c0:GT], True, [kbb, b["QT"]], [pbb])
                        res.append((pb, pbb))
                    sbankB[idx] = res

                def expB(idx):
                    hh, n = itemsB[idx]
                    c0 = max(0, n - 4 * g) * 128
                    res = sbankB.pop(idx)
                    pts = []
                    for m in range(2):
                        pb, pbb = res[m]
                        pt, ptb = next_pt()
                        act(pt[:, c0:GT], pb[:, c0:GT], AF.Exp, [pbb], [ptb], scale=0.125)
                        if n >= 4 * g:
                            S.op("dve", lambda e, pt=pt, c0=c0: e.memset(pt[64:128, c0:c0 + 64], 0.0), [], [ptb])
                        pts.append((pt, ptb))
                    return pts

                def pvB(idx, pts):
                    hh, n = itemsB[idx]
                    kb, kbb, vb, vbb = kvslot[hh]
                    c0 = max(0, n - 4 * g) * 128
                    for m, (O, Ob, Z, Zb) in enumerate([(O0, O0b, Z0, Z0b), (O1, O1b, Z1, Z1b)]):
                        pt, ptb = pts[m]
                        mm(O[:, c0:GT], vb[:, n, :], pt[:, c0:GT], n == 0, [vbb, ptb], [Ob])
                        mm(Z[:, c0:GT], ONESb[:], pt[:, c0:GT], n == 0, [b["ONES"], ptb], [Zb])
                    if n == 2 and rope_hook[0] is not None:
                        rope_hook[0]()
                        rope_hook[0] = None
                    if n == 1 and pend_d1[0] is not None:
                        pend_d1[0]()
                        pend_d1[0] = None
                    if n == 3 and pend_d2[0] is not None:
                        pend_d2[0]()
                        pend_d2[0] = None
                    if n == ntile - 1:
                        act(RR[0][:], Z0[:, :], AF.Ln, [Z0b], [RRb[0]])
                        act(RR[1][:], Z1[:, :], AF.Ln, [Z1b], [RRb[1]])
                        act(RR[0][:], RR[0][:], AF.Exp, [RRb[0]], [RRb[0]], scale=-1.0)
                        act(RR[1][:], RR[1][:], AF.Exp, [RRb[1]], [RRb[1]], scale=-1.0)
                        dve_tt(T0[:], O0[:, :], RR[0][:], ALU.mult, [O0b, RRb[0]], [b["T0"]])
                        dve_tt(T1[:], O1[:, :], RR[1][:], ALU.mult, [O1b, RRb[1]], [b["T1"]])
                        S.op("dve", lambda e: e.scalar_tensor_tensor(out=T0[:], in0=T1[:], scalar=NEGLAM, in1=T0[:],
                                                                     op0=ALU.mult, op1=ALU.add),
                             [b["T0"], b["T1"], b["LAMS"]], [b["T0"]])

                        def d1():
                            act(SQ[:], T0[:], AF.Square, [b["T0"]], [b["SQ"]])
                            pb, pbb = next_pb(0, 3)
                            mm(pb[:, :], ONESb[:], SQ[:], True, [b["ONES"], b["SQ"]], [pbb])
                            dve_copy(RS[:], pb[:, :], [pbb], [b["RS"]])

                        def d2(hh=hh):
                            act(RS[:], RS[:], AF.Ln, [b["RS"], b["EPSC"]], [b["RS"]], scale=1.0 / 128, bias=EPSC[:, 0:1])
                            act(RS[:], RS[:], AF.Exp, [b["RS"]], [b["RS"]], scale=-0.5)
                            dve_tt(T0[:], T0[:], RS[:], ALU.mult, [b["T0"], b["RS"]], [b["T0"]])
                            dve_tt(X2T[:, hh, :], T0[:], SGT[:, hh, :], ALU.mult, [b["T0"], b["SGT"]], [b["X2T"]])
                        pend_d1[0] = d1
                        pend_d2[0] = d2

                qkB(0)
                for idx in range(len(itemsB)):
                    pts = expB(idx)
                    if idx + 1 < len(itemsB):
                        qkB(idx + 1)
                    pvB(idx, pts)
                for pd_ in (pend_d1, pend_d2):
                    if pd_[0] is not None:
                        pd_[0]()
                        pd_[0] = None

                stage(6)
                outproj_norm_residual("w5", GPB, b["GPB"], s, g, 1, X2T, b["X2T"])

        try:
            run_all()
        except _Stop:
            pass
        outs = [d_out[s][g][t] for s in range(NSEQ) for g in range(NG) for t in range(4)]
        names = {"EB": (EB, [128, 16 * 256], BF16), "CH": (CH, [128, 16], F32), "LAMS": (LAMS, [128, 8], F32),
                 "XT": (XT, [128, 8 * GT], BF16), "X2T": (X2T, [128, 8 * GT], BF16), "QT": (QT, [128, 8 * GT], BF16),
                 "SGT": (SGT, [128, 8 * GT], BF16), "KT": (KT, [128, 8 * 1024], BF16), "VA": (VA, [128, 8 * 1536], BF16),
                 "COS": (COS, [128, GT], F32), "SIN": (SIN, [128, GT], F32), "GN": (GN, [128, 24], F32),
                 "PERM": (PERMb, [128, 128], BF16), "ST": (ST, [128, 16], F32), "YN0": (YN[0], [128, D], F32),
                 "YN1": (YN[1], [128, D], F32)}
        b["YN0"] = YNb[0]; b["YN1"] = YNb[1]
        for nm in dbg_dump:
            tl, shp, dt = names[nm]
            dd = nc.dram_tensor("dbg_" + nm, shp, dt, kind="ExternalOutput")
            db_ = Buf("dbg_" + nm)
            src = tl[:] if len(tl.shape) == 2 else tl[:].rearrange("p a b -> p (a b)")
            dma(dd.ap(), src, db_, b.get(nm, b.get("PERM")), queue="sp")
            outs.append(db_)
        S.final_wait("sp", outs)
        print("sbuf bytes remaining/partition:", nc.sbuf_bytes_remaining, flush=True)
        S.emit(st)
        print("instr counts:", {e: len(S.ops[e]) for e in ENGS}, "dma sems:", S.ndsem, flush=True)
    return nc


_CACHE = {}


def _consts():
    ident = np.eye(128, dtype=np.float32)
    ones = np.ones((128, 128), np.float32)
    perm = np.zeros((128, 128), np.float32)
    for base in (0, 64):
        for d in range(8):
            perm[base + d + 8, base + d] = -1.0
            perm[base + d, base + d + 8] = 1.0
    anti = np.ascontiguousarray(np.eye(128, dtype=np.float32)[::-1])
    cmat = np.stack([ident, ones, perm, anti]).astype(np.float32)
    rtab = np.zeros((128, 4), np.float32)
    inv = np.power(np.float32(ROPE_THETA), -np.arange(8, dtype=np.float32) * np.float32(2.0) / np.float32(16))
    for p in range(128):
        d = p % 64
        if d < 16:
            rtab[p, 0] = inv[d % 8]
            rtab[p, 1] = 1.0
        else:
            rtab[p, 2] = 1.0
        rtab[p, 3] = math.pi
    return cmat, rtab.astype(np.float32)


def kernel(x, positions, a_norm_pre, a_w_in, a_rel_bias, a_w_out, a_norm_post, kv_norm, kv_w,
           b_norm_pre, b_w_in, b_lambda_q1, b_lambda_k1, b_lambda_q2, b_lambda_k2, b_subln,
           b_w_out, b_norm_post):
    f = lambda a: np.ascontiguousarray(np.asarray(a, dtype=np.float32))
    x = f(x)
    positions = np.ascontiguousarray(np.asarray(positions, dtype=np.int32))
    cmat, rtab = _consts()
    shared = {
        "a_norm_pre": f(a_norm_pre).reshape(D), "a_w_in": f(a_w_in).reshape(D, 4096),
        "a_rel_bias": f(a_rel_bias).reshape(16, 257), "a_w_out": f(a_w_out).reshape(D, D),
        "a_norm_post": f(a_norm_post).reshape(D), "kv_norm": f(kv_norm).reshape(D),
        "kv_w": f(kv_w).reshape(D, 2048), "b_norm_pre": f(b_norm_pre).reshape(D),
        "b_w_in": f(b_w_in).reshape(D, 2048),
        "b_lam": np.stack([f(b_lambda_q1).reshape(64), f(b_lambda_k1).reshape(64),
                           f(b_lambda_q2).reshape(64), f(b_lambda_k2).reshape(64)]),
        "b_subln": f(b_subln).reshape(128), "b_w_out": f(b_w_out).reshape(D, D),
        "b_norm_post": f(b_norm_post).reshape(D), "cmat": cmat, "rtab": rtab,
    }
    if "nc" not in _CACHE:
        _CACHE["nc"] = build_program()
    nc = _CACHE["nc"]
    in_maps = []
    for c in range(NCORES):
        m = dict(shared)
        m["x"] = x[c * NSEQ:(c + 1) * NSEQ]
        m["pos"] = positions[c * NSEQ:(c + 1) * NSEQ]
        in_maps.append(m)
    res = run_bass_kernel_spmd(nc, in_maps, core_ids=list(range(NCORES)))
    return np.concatenate([np.asarray(r["out"]).reshape(NSEQ, SEQ, D) for r in res.results], axis=0).astype(np.float32)
```
